# Optimizing a Trainium2 kernel written in Bass

```python
import jax, jax.numpy as jnp
from jax import lax
import numpy as np

D_MODEL = 1024
BATCH = 8
SEQ = 4096
DEPTH = 2

CTX_LEN = 256
GRID_W = 64

GLA_HEADS = 4
GLA_DK = D_MODEL // 2 // GLA_HEADS
GLA_DV = D_MODEL // GLA_HEADS
GLA_K = GLA_HEADS * GLA_DK
GLA_V = GLA_HEADS * GLA_DV
GLA_LR = 16
GLA_TAU = 16.0
GLA_CHUNK = 64

CONV_W = D_MODEL // 2
CONV_K = 31

POOL_GROUPS = 4
POOL_W = D_MODEL // 2
POOL_GC = POOL_W // POOL_GROUPS
POOL_WINDOWS = (2, 4, 8, 16)

N_BRANCH = 3
D_FF = 4 * D_MODEL
EPS = 1e-6

SPLIT_SIZES = (GLA_K, GLA_K, GLA_V, GLA_V, GLA_LR, GLA_LR, CONV_W, CONV_W, POOL_W, N_BRANCH * D_MODEL)
IN_COLS = 2 * GLA_K + 2 * GLA_V + 2 * GLA_LR + 2 * CONV_W + POOL_W + N_BRANCH * D_MODEL

kernel_name = "hybrid_gla_conformer_pool_dit_block"


def rms_norm(x, g):
    x32 = x.astype(jnp.float32)
    y = x32 * lax.rsqrt(jnp.mean(x32 * x32, axis=-1, keepdims=True) + EPS)
    return (y * g.astype(jnp.float32)).astype(x.dtype)


def layer_norm(x, g, b):
    x32 = x.astype(jnp.float32)
    mu = jnp.mean(x32, axis=-1, keepdims=True)
    xc = x32 - mu
    y = xc * lax.rsqrt(jnp.mean(xc * xc, axis=-1, keepdims=True) + EPS)
    return (y * g.astype(jnp.float32) + b.astype(jnp.float32)).astype(x.dtype)


def modulate(x, shift, scale):
    return x * (1.0 + scale) + shift


def in_projection(h, w_in):
    parts, start = [], 0
    for size in SPLIT_SIZES:
        parts.append(h @ w_in[:, start:start + size])
        start += size
    return parts


def gla_chunk(q, k, v, la, s0):
    B, L, H, DK = q.shape
    DV = v.shape[-1]
    C = GLA_CHUNK
    nC = L // C
    q, k, la = (t.reshape(B, nC, C, H, DK) for t in (q, k, la))
    v = v.reshape(B, nC, C, H, DV)
    b = jnp.cumsum(la, axis=2)
    q_i = q * jnp.exp(b)
    k_i = k * jnp.exp(-b)
    mask = jnp.tril(jnp.ones((C, C), dtype=bool))
    att = jnp.einsum('bnthd,bnshd->bnhts', q_i, k_i)
    att = jnp.where(mask, att, 0.0)
    o = jnp.einsum('bnhts,bnshv->bnthv', att, v)
    b_end = b[:, :, -1]
    k_end = k * jnp.exp(b_end[:, :, None] - b)
    d_state = jnp.einsum('bnshd,bnshv->bnhdv', k_end, v)
    gamma = jnp.exp(b_end)

    def step(s, inp):
        g, ds = inp
        return g[..., None] * s + ds, s

    s_fin, s_start = lax.scan(step, s0, (jnp.moveaxis(gamma, 1, 0), jnp.moveaxis(d_state, 1, 0)))
    s_start = jnp.moveaxis(s_start, 0, 1)
    o = o + jnp.einsum('bnthd,bnhdv->bnthv', q_i, s_start)
    return o.reshape(B, L, H, DV), s_fin


def gla_bidir(q, k, v, la_f, la_b, s_f0, s_b0):
    o_f, s_f = gla_chunk(q, k, v, la_f, s_f0)
    rev = lambda t: jnp.flip(t, axis=1)
    o_b, s_b = gla_chunk(rev(q), rev(k), rev(v), rev(la_b), s_b0)
    return o_f + rev(o_b), s_f, s_b


def gla_inputs(parts, p):
    pq, pk, pv, _, plf, plb = parts[:6]
    B, L, _ = pq.shape
    q = pq.astype(jnp.float32).reshape(B, L, GLA_HEADS, GLA_DK) * (GLA_DK ** -0.5)
    k = pk.astype(jnp.float32).reshape(B, L, GLA_HEADS, GLA_DK)
    v = pv.astype(jnp.float32).reshape(B, L, GLA_HEADS, GLA_DV)

    def log_decay(plr, i):
        z = (plr @ p['w_decay'][i] + p['b_decay'][i]).astype(jnp.float32)
        return (jax.nn.log_sigmoid(z) / GLA_TAU).reshape(B, L, GLA_HEADS, GLA_DK)

    return q, k, v, log_decay(plf, 0), log_decay(plb, 1)


def gla_out(o, pg, p):
    B, L = o.shape[:2]
    o = o * lax.rsqrt(jnp.mean(o * o, axis=-1, keepdims=True) + EPS)
    o = o * p['g_gla'].astype(jnp.float32).reshape(GLA_HEADS, GLA_DV)
    o = o.reshape(B, L, GLA_V).astype(pg.dtype) * jax.nn.silu(pg)
    return o @ p['w_gla_o']


def depthwise_conv(u, w, b):
    C = u.shape[-1]
    y = lax.conv_general_dilated(u, w[:, None, :].astype(u.dtype), window_strides=(1,),
                                 padding=[(CONV_K // 2, CONV_K // 2)],
                                 dimension_numbers=('NWC', 'WIO', 'NWC'),
                                 feature_group_count=C)
    return y + b


def conv_branch(pa, pb, p, rows):
    u = pa * jax.nn.sigmoid(pb)
    B, L, C = u.shape
    seqs = u if rows is None else u.reshape(B * rows, GRID_W, C)
    y = depthwise_conv(seqs, p['w_dw'], p['b_dw']).reshape(B, L, C)
    y = jax.nn.silu(layer_norm(y, p['g_conv_ln'], p['b_conv_ln']))
    return y @ p['w_conv_o']


def window_mean(u, w):
    L = u.shape[1]
    left = w // 2
    right = w - 1 - left
    cs = jnp.cumsum(u.astype(jnp.float32), axis=1)
    cs = jnp.concatenate([jnp.zeros_like(cs[:, :1]), cs], axis=1)
    t = jnp.arange(L)
    lo = jnp.clip(t - left, 0, L)
    hi = jnp.clip(t + right + 1, 0, L)
    total = jnp.take(cs, hi, axis=1) - jnp.take(cs, lo, axis=1)
    cnt = (hi - lo).astype(jnp.float32).reshape((1, L) + (1,) * (u.ndim - 2))
    return (total / cnt).astype(u.dtype)


def pool_branch(u, p, rows):
    B, L, C = u.shape
    if rows is None:
        grid = u.reshape(B, L, POOL_GROUPS, POOL_GC)
    else:
        grid = u.reshape(B, rows, GRID_W, POOL_GROUPS, POOL_GC)
    pooled = jnp.stack([window_mean(grid[..., i, :], w) for i, w in enumerate(POOL_WINDOWS)], axis=-2)
    y = jnp.einsum('...gc,gcd->...gd', pooled - grid, p['w_pool_g'])
    y = y.reshape(B, L, C) * p['s_pool']
    return y @ p['w_pool_o']


def merge(ya, yb, yc, pgate, p):
    B, L, _ = pgate.shape
    gates = jax.nn.sigmoid(pgate.reshape(B, L, N_BRANCH, D_MODEL) + p['b_gate'])
    mixed = gates[:, :, 0] * ya + gates[:, :, 1] * yb + gates[:, :, 2] * yc
    return mixed @ p['w_out']


def token_mixer(h, hc, p, need_ctx_out):
    rows = h.shape[1] // GRID_W
    parts = in_projection(h, p['w_in'])
    cparts = in_projection(hc, p['w_in'])
    zero = jnp.zeros((hc.shape[0], GLA_HEADS, GLA_DK, GLA_DV), jnp.float32)
    cq, ck, cv, cla_f, cla_b = gla_inputs(cparts, p)
    co, cs_f, cs_b = gla_bidir(cq, ck, cv, cla_f, cla_b, zero, zero)
    q, k, v, la_f, la_b = gla_inputs(parts, p)
    o, _, _ = gla_bidir(q, k, v, la_f, la_b, cs_f, cs_b)
    ya = gla_out(o, parts[3], p)
    yb = conv_branch(parts[6], parts[7], p, rows)
    yc = pool_branch(parts[8], p, rows)
    y = merge(ya, yb, yc, parts[9], p)
    if not need_ctx_out:
        return y, None
    ca = gla_out(co, cparts[3], p)
    cb = conv_branch(cparts[6], cparts[7], p, None)
    cc = pool_branch(cparts[8], p, None)
    y_ctx = merge(ca, cb, cc, cparts[9], p)
    return y, y_ctx


def sq_relu_mlp(h, w1, w2):
    return jnp.square(jax.nn.relu(h @ w1)) @ w2


def setup_inputs(seed: int = 0) -> dict:
    key = jax.random.key(seed)
    ks = iter(jax.random.split(key, 32))

    def nrm(shape, scale):
        return jax.random.normal(next(ks), shape, jnp.float32) * scale

    L = DEPTH
    return {
        'x': nrm((BATCH, SEQ, D_MODEL), 1.0),
        'c': nrm((BATCH, D_MODEL), 1.0),
        'ctx': nrm((BATCH, CTX_LEN, D_MODEL), 1.0),
        'c_ctx': nrm((D_MODEL,), 1.0),
        'w_ada': nrm((L, D_MODEL, 6 * D_MODEL), D_MODEL ** -0.5),
        'b_ada': nrm((L, 6 * D_MODEL), 0.02),
        'g_pre_mix': 1.0 + nrm((L, D_MODEL), 0.05),
        'g_post_mix': 1.0 + nrm((L, D_MODEL), 0.05),
        'g_pre_mlp': 1.0 + nrm((L, D_MODEL), 0.05),
        'g_post_mlp': 1.0 + nrm((L, D_MODEL), 0.05),
        'w_in': nrm((L, D_MODEL, IN_COLS), D_MODEL ** -0.5),
        'w_decay': nrm((L, 2, GLA_LR, GLA_K), GLA_LR ** -0.5),
        'b_decay': nrm((L, 2, GLA_K), 0.1),
        'g_gla': 1.0 + nrm((L, GLA_V), 0.05),
        'w_gla_o': nrm((L, GLA_V, D_MODEL), GLA_V ** -0.5),
        'w_dw': nrm((L, CONV_K, CONV_W), CONV_K ** -0.5),
        'b_dw': nrm((L, CONV_W), 0.02),
        'g_conv_ln': 1.0 + nrm((L, CONV_W), 0.05),
        'b_conv_ln': nrm((L, CONV_W), 0.02),
        'w_conv_o': nrm((L, CONV_W, D_MODEL), CONV_W ** -0.5),
        'w_pool_g': nrm((L, POOL_GROUPS, POOL_GC, POOL_GC), POOL_GC ** -0.5),
        's_pool': 1.0 + nrm((L, POOL_W), 0.1),
        'w_pool_o': nrm((L, POOL_W, D_MODEL), POOL_W ** -0.5),
        'b_gate': nrm((L, N_BRANCH, D_MODEL), 0.1),
        'w_out': nrm((L, D_MODEL, D_MODEL), D_MODEL ** -0.5),
        'w_mlp1': nrm((L, D_MODEL, D_FF), D_MODEL ** -0.5),
        'w_mlp2': nrm((L, D_FF, D_MODEL), D_FF ** -0.5),
    }


def reference(x, c, ctx, c_ctx, w_ada, b_ada, g_pre_mix, g_post_mix, g_pre_mlp, g_post_mlp,
              w_in, w_decay, b_decay, g_gla, w_gla_o, w_dw, b_dw, g_conv_ln, b_conv_ln, w_conv_o,
              w_pool_g, s_pool, w_pool_o, b_gate, w_out, w_mlp1, w_mlp2):
    silu_c = jax.nn.silu(c)
    silu_cc = jax.nn.silu(c_ctx)
    for l in range(DEPTH):
        last = l == DEPTH - 1
        p = {
            'w_in': w_in[l], 'w_decay': w_decay[l], 'b_decay': b_decay[l], 'g_gla': g_gla[l],
            'w_gla_o': w_gla_o[l], 'w_dw': w_dw[l], 'b_dw': b_dw[l], 'g_conv_ln': g_conv_ln[l],
            'b_conv_ln': b_conv_ln[l], 'w_conv_o': w_conv_o[l], 'w_pool_g': w_pool_g[l],
            's_pool': s_pool[l], 'w_pool_o': w_pool_o[l], 'b_gate': b_gate[l], 'w_out': w_out[l],
        }
        mod = jnp.split((silu_c @ w_ada[l] + b_ada[l])[:, None, :], 6, axis=-1)
        mod_c = jnp.split(silu_cc @ w_ada[l] + b_ada[l], 6, axis=-1)

        h = modulate(rms_norm(x, g_pre_mix[l]), mod[0], mod[1])
        hc = modulate(rms_norm(ctx, g_pre_mix[l]), mod_c[0], mod_c[1])
        y, y_ctx = token_mixer(h, hc, p, not last)
        x = x + mod[2] * rms_norm(y, g_post_mix[l])
        h = modulate(rms_norm(x, g_pre_mlp[l]), mod[3], mod[4])
        x = x + mod[5] * rms_norm(sq_relu_mlp(h, w_mlp1[l], w_mlp2[l]), g_post_mlp[l])

        if not last:
            ctx = ctx + mod_c[2] * rms_norm(y_ctx, g_post_mix[l])
            hc = modulate(rms_norm(ctx, g_pre_mlp[l]), mod_c[3], mod_c[4])
            ctx = ctx + mod_c[5] * rms_norm(sq_relu_mlp(hc, w_mlp1[l], w_mlp2[l]), g_post_mlp[l])
    return x
```

```python
import numpy as np
import concourse.bass as bass
import concourse.mybir as mybir
from concourse.bass_utils import run_bass_kernel_spmd

F32 = mybir.dt.float32
BF16 = mybir.dt.bfloat16
AF = mybir.ActivationFunctionType
ALU = mybir.AluOpType
AX = mybir.AxisListType
DT_BYTES = {F32: 4, BF16: 2}
ENGS = ("pe", "act", "dve", "pool", "sp")

D = 1024
NCTX = 256
NLAT = 4096
NT = NCTX + NLAT
L = 2
EPS = 1e-6
NR = 12288
NPP = 164
NCST = 2304
C_Q, C_K, C_V, C_G, C_LF, C_LB, C_PA, C_PB, C_PL, C_GT = 0, 512, 1024, 2048, 3072, 3088, 3104, 3616, 4128, 4640
TILES512 = [(0, 256, "ctx")] + [(256 + 512 * i, 512, "lat") for i in range(8)]
TILES256 = [(256 * i, 256, "ctx" if i == 0 else "lat") for i in range(17)]


class Buf:
    __slots__ = ("name", "ap", "lw", "rd", "sem")

    def __init__(self, name, ap=None):
        self.name = name
        self.ap = ap
        self.lw = None
        self.rd = {}
        self.sem = {}


class Op:
    __slots__ = ("eng", "fn", "waits", "signal", "dma_sem", "seq")

    def __init__(self, eng, fn):
        self.eng = eng
        self.fn = fn
        self.waits = []
        self.signal = False
        self.dma_sem = None


class K:
    def __init__(self, nc, arena_bytes=186 * 1024):
        self.nc = nc
        self.ops = {e: [] for e in ENGS}
        self.waited = {e: {} for e in ENGS}
        self.arena_bytes = arena_bytes
        self.arena = nc.alloc_sbuf_tensor("arena", [128, arena_bytes // 2], BF16)
        self.top = 0
        self.psum_t = nc.alloc_psum_tensor("psum_all", [128, 8, 512], F32)
        self.ps = [Buf(f"ps{i}", self.psum_t[:, i, :]) for i in range(8)]
        self.esem = {e: nc.alloc_semaphore(f"sem_{e}") for e in ENGS if e != "sp"}
        self.dsems_free = {"hw": [], "sw": []}
        self.nsem = 0
        self.phase_bufs = []
        self.phase_marks = []
        self.all_dma_sems = []
        self._bank = 0
        self.bank_set = list(range(8))

    def bank(self):
        self._bank = (self._bank + 1) % len(self.bank_set)
        return self.ps[self.bank_set[self._bank]]

    def alloc(self, name, shape, dtype):
        free = int(np.prod(shape[1:]))
        nbytes = (free * DT_BYTES[dtype] + 63) // 64 * 64
        off = self.top
        self.top += nbytes
        assert self.top <= self.arena_bytes, f"SBUF arena overflow at {name}: {self.top}"
        ap = self.arena[0:shape[0], off // 2: off // 2 + free * DT_BYTES[dtype] // 2]
        if dtype != BF16:
            ap = ap.bitcast(dtype)
        if len(shape) > 2:
            names = " ".join(f"d{i}" for i in range(len(shape) - 1))
            kw = {f"d{i}": shape[i + 1] for i in range(len(shape) - 1)}
            ap = ap.rearrange(f"p ({names}) -> p {names}", **kw)
        b = Buf(name, ap)
        self.phase_bufs.append(b)
        return b

    def phase_begin(self):
        self.phase_marks.append((self.top, len(self.phase_bufs)))

    def phase_end(self):
        self.barrier()
        top, nb = self.phase_marks.pop()
        for b in self.phase_bufs[nb:]:
            for cls, sm in b.sem.items():
                self.dsems_free[cls].append(sm)
            b.sem = {}
        del self.phase_bufs[nb:]
        self.top = top

    def _getsem(self, buf, cls):
        if cls not in buf.sem:
            if self.dsems_free[cls]:
                buf.sem[cls] = self.dsems_free[cls].pop()
            else:
                h = self.nc.alloc_semaphore(f"dsem{cls}{self.nsem}")
                self.nsem += 1
                buf.sem[cls] = [h, 0]
                self.all_dma_sems.append(buf.sem[cls])
        return buf.sem[cls]

    def _add_wait(self, op, tok):
        e = op.eng
        if tok[0] == "eng":
            _, f, seq = tok
            if f == e and e == "pe":
                return
            key = ("eng", f)
            if self.waited[e].get(key, -1) >= seq:
                return
            self.waited[e][key] = seq
            self.ops[f][seq].signal = True
            op.waits.append(tok)
        else:
            _, sem, cnt = tok
            key = ("dma", id(sem))
            if self.waited[e].get(key, -1) >= cnt:
                return
            self.waited[e][key] = cnt
            op.waits.append(tok)

    def _record(self, op, reads, writes, tok):
        deps = []
        for b in reads:
            if b.lw is not None:
                deps.append(b.lw)
        for b in writes:
            if b.lw is not None and not (op.dma_sem is not None and b.lw[0] == "dma" and b.lw[1] is op.dma_sem):
                deps.append(b.lw)
            deps.extend(b.rd.values())
        for t in deps:
            self._add_wait(op, t)
        for b in writes:
            b.lw = tok
            b.rd = {}
        for b in reads:
            if b in writes:
                continue
            kk = (tok[0], tok[1]) if tok[0] == "eng" else ("dma", id(tok[1]))
            b.rd[kk] = tok

    def op(self, eng, fn, reads=(), writes=()):
        o = Op(eng, fn)
        o.seq = len(self.ops[eng])
        self.ops[eng].append(o)
        self._record(o, reads, writes, ("eng", eng, o.seq))
        return o

    def dma(self, queue, out_ap, in_ap, reads, writes, sbuf_side):
        sem = self._getsem(sbuf_side, "sw" if queue == "pool" else "hw")
        sem[1] += 16
        o = Op(queue, lambda e, o_=out_ap, i_=in_ap: e.dma_start(out=o_, in_=i_))
        o.dma_sem = sem
        o.seq = len(self.ops[queue])
        self.ops[queue].append(o)
        self._record(o, reads, writes, ("dma", sem, sem[1]))
        return o

    def load(self, buf, src, queue="sp", dst=None):
        return self.dma(queue, buf.ap if dst is None else dst, src, [], [buf], buf)

    def store(self, dst, buf, src=None, queue="sp"):
        return self.dma(queue, dst, buf.ap if src is None else src, [buf], [], buf)

    def barrier(self):
        lasts = {}
        for e in ENGS:
            if e == "sp":
                continue
            s = len(self.ops[e]) - 1
            while s >= 0 and (self.ops[e][s].fn is None or self.ops[e][s].dma_sem is not None):
                s -= 1
            lasts[e] = s
        for e in ENGS:
            o = Op(e, None)
            o.seq = len(self.ops[e])
            for f, s in lasts.items():
                if s >= 0:
                    self._add_wait(o, ("eng", f, s))
            for sem in self.all_dma_sems:
                if sem[1] > 0:
                    self._add_wait(o, ("dma", sem, sem[1]))
            if o.waits:
                self.ops[e].append(o)

    def emit(self):
        nc = self.nc
        cum = {}
        for e in ENGS:
            if e == "sp":
                continue
            c = 0
            arr = []
            for o in self.ops[e]:
                if o.signal:
                    c += 1
                arr.append(c)
            cum[e] = arr
        self.n_instr = {e: len(self.ops[e]) for e in ENGS}

        def run(e, eng):
            for o in self.ops[e]:
                for t in o.waits:
                    if t[0] == "eng":
                        eng.wait_ge(self.esem[t[1]], cum[t[1]][t[2]])
                    else:
                        eng.wait_ge(t[1][0], t[2])
                if o.fn is None:
                    continue
                ins = o.fn(eng)
                if o.dma_sem is not None:
                    ins.then_inc(o.dma_sem[0], 16)
                elif o.signal:
                    ins.then_inc(self.esem[e], 1)

        with nc.Block() as block:
            @block.tensor
            def _(eng):
                run("pe", eng)

            @block.scalar
            def _(eng):
                run("act", eng)

            @block.vector
            def _(eng):
                run("dve", eng)

            @block.gpsimd
            def _(eng):
                run("pool", eng)

            @block.sync
            def _(eng):
                run("sp", eng)

    def mm(self, ps, out_ap, lhsT, rhs, start, stop, reads):
        self.op("pe", lambda e: e.matmul(out_ap, lhsT, rhs, start=start, stop=stop), reads, [ps])

    def act(self, out_ap, in_ap, func, reads, writes, **kw):
        self.op("act", lambda e: e.activation(out=out_ap, in_=in_ap, func=func, **kw), reads, writes)

    def tt(self, eng, out, in0, in1, op, reads, writes):
        self.op(eng, lambda e: e.tensor_tensor(out=out, in0=in0, in1=in1, op=op), reads, writes)

    def stt(self, eng, out, in0, scalar, in1, op0, op1, reads, writes):
        self.op(eng, lambda e: e.scalar_tensor_tensor(out=out, in0=in0, scalar=scalar, in1=in1, op0=op0, op1=op1),
                reads, writes)

    def ts(self, eng, out, in0, s1, s2, op0, op1, reads, writes):
        self.op(eng, lambda e: e.tensor_scalar(out=out, in0=in0, scalar1=s1, scalar2=s2, op0=op0, op1=op1),
                reads, writes)

    def copy(self, eng, out, in_, reads, writes):
        if eng == "act":
            self.op(eng, lambda e: e.activation(out=out, in_=in_, func=AF.Copy), reads, writes)
        else:
            self.op(eng, lambda e: e.tensor_copy(out=out, in_=in_), reads, writes)

    def recip(self, eng, out, in_, reads, writes):
        self.op(eng, lambda e: e.reciprocal(out=out, in_=in_), reads, writes)

    def tsmax(self, eng, out, in0, val, reads, writes):
        self.op(eng, lambda e: e.tensor_scalar_max(out=out, in0=in0, scalar1=val), reads, writes)

    def transpose(self, ps, out, in_, ident, reads):
        self.op("pe", lambda e: e.transpose(out, in_, ident), reads, [ps])

    def reduce_add(self, eng, out, in_, reads, writes):
        self.op(eng, lambda e: e.tensor_reduce(out=out, in_=in_, axis=AX.X, op=ALU.add), reads, writes)

    def memset(self, eng, buf, val, ap=None):
        a = buf.ap if ap is None else ap
        self.op(eng, lambda e: e.memset(a, val), [], [buf])


def tokmaj(X, t0, n):
    return X[t0:t0 + n, :].rearrange("(s p) c -> p s c", p=128)


def wview(W, c0, c1):
    return W[:, c0:c1].rearrange("(k p) c -> p k c", p=128)


DEBUG_IMM = False
DEBUG_TILES = None
DEBUG_PAD = 0


def build_program(dbg=False, stop_after=None):
    nc = bass.Bass("TRN2", target_bir_lowering=False)

    def din(name, shape, dt=F32):
        return nc.dram_tensor(name, shape, dt, kind="ExternalInput").ap()

    def dscr(name, shape, dt):
        return nc.dram_tensor(name, shape, dt, kind="ExternalOutput" if dbg else "Internal").ap()

    xin = din("xin", [NT, D])
    cvec = din("cvec", [128, 16])
    rows = din("rows", [L, 1, NR])
    pp_d = din("pp", [L, 128, NPP])
    cst_d = din("cst", [128, NCST])
    w_ada = din("w_ada", [L, D, 6 * D])
    w_in = din("w_in", [L, D, 7712])
    w_decay = din("w_decay", [L, 2, 16, 512])
    w_gla_o = din("w_gla_o", [L, D, D])
    w_conv_o = din("w_conv_o", [L, 512, D])
    w_pool_g = din("w_pool_g", [L, 4, 128, 128])
    w_pool_o = din("w_pool_o", [L, 512, D])
    w_out = din("w_out", [L, D, D])
    w_mlp1 = din("w_mlp1", [L, D, 4 * D])
    w_mlp2 = din("w_mlp2", [L, 4 * D, D])
    out = nc.dram_tensor("out", [NLAT, D], F32, kind="ExternalOutput").ap()

    modrow = dscr("modrow", [L, 2, 6, D], F32)
    xres = dscr("xres", [NT, D], F32)
    hT = dscr("hT", [128, 8, NT], BF16)
    h2T = dscr("h2T", [128, 8, NT], BF16)
    qT = dscr("qT", [128, 4, NT], BF16)
    kT = dscr("kT", [128, 4, NT], BF16)
    kk = dscr("kk", [NT, 512], BF16)
    vv = dscr("vv", [NT, D], BF16)
    sg = dscr("sg", [NT, D], BF16)
    lrT = dscr("lrT", [2, 16, NT], BF16)
    cvT = dscr("cvT", [128, 4, NT], BF16)
    plT = dscr("plT", [128, 4, NT], BF16)
    of_d = dscr("of", [NT, D], F32)
    ogT = dscr("ogT", [128, 8, NT], BF16)

    k = K(nc)
    ps = k.ps

    cstb = k.alloc("cstb", [128, 1024], BF16)
    onesr = k.alloc("onesr", [1, 128], BF16)
    if DEBUG_PAD:
        k.alloc("pad", [128, DEBUG_PAD // 4], F32)
    k.load(cstb, cst_d[:, 0:1024], queue="pool")
    k.memset("dve", onesr, 1.0)
    ident = cstb.ap[:, 0:128]
    tri = {"f": cstb.ap[:, 128:256], "b": cstb.ap[:, 256:384]}
    UU = {"f": cstb.ap[:, 384:512], "b": cstb.ap[:, 512:640]}
    msk = {"f": cstb.ap[:, 640:768], "b": cstb.ap[:, 768:896]}
    onesm = cstb.ap[:, 896:1024]

    def phase_adaln():
        k.phase_begin()
        cv = k.alloc("cv", [128, 16], F32)
        sc = k.alloc("sc", [128, 16], F32)
        k.load(cv, cvec)
        k.act(sc.ap, cv.ap, AF.Silu, [cv], [sc])
        wa = [k.alloc(f"wa{i}", [128, 8, 512], F32) for i in range(3)]
        for l in range(L):
            k.phase_begin()
            rw = k.alloc(f"rw{l}", [2, 10240], F32)
            k.load(rw, rows[l, 0, 0:10240].partition_broadcast(2))
            modr = k.alloc(f"modr{l}", [2, 6 * D], F32)
            for blk in range(12):
                wb = wa[blk % 3]
                k.load(wb, wview(w_ada[l], blk * 512, (blk + 1) * 512))
                pb = k.bank()
                for kc in range(8):
                    k.mm(pb, pb.ap[0:2, :], sc.ap[:, kc:16:8], wb.ap[:, kc, :], kc == 0, kc == 7, [sc, wb])
                k.tt("dve", modr.ap[:, blk * 512:(blk + 1) * 512], pb.ap[0:2, :],
                     rw.ap[:, blk * 512:(blk + 1) * 512], ALU.add, [pb, rw], [modr])
            m = modr.ap
            o6 = k.alloc(f"o6{l}", [2, 6 * D], F32)
            g = lambda i: rw.ap[:, 6144 + i * D: 6144 + (i + 1) * D]
            sl = lambda a, i: a[:, i * D:(i + 1) * D]
            k.stt("dve", sl(o6.ap, 0), sl(m, 1), 1.0, g(0), ALU.add, ALU.mult, [modr, rw], [o6])
            k.copy("dve", sl(o6.ap, 1), sl(m, 0), [modr], [o6])
            k.tt("dve", sl(o6.ap, 2), sl(m, 2), g(1), ALU.mult, [modr, rw], [o6])
            k.stt("dve", sl(o6.ap, 3), sl(m, 4), 1.0, g(2), ALU.add, ALU.mult, [modr, rw], [o6])
            k.copy("dve", sl(o6.ap, 4), sl(m, 3), [modr], [o6])
            k.tt("dve", sl(o6.ap, 5), sl(m, 5), g(3), ALU.mult, [modr, rw], [o6])
            k.store(modrow[l].rearrange("w a d -> w (a d)"), o6)
            k.phase_end()
        k.phase_end()

    def load_bc(name, l, which, idx):
        b = k.alloc(name, [128, D], F32)
        k.load(b, modrow[l, which, idx, :].partition_broadcast(128))
        return b

    def norm_mod_T(xs, nsub, A, B, hstage, tmp):
        ss, rt, rstd, junk, t1, hb = tmp
        k.memset("pool", ss, 0.0)
        for s in range(nsub):
            k.act(junk.ap, xs.ap[:, s, :], AF.Square, [xs], [junk, ss], accum_out=ss.ap[:, s:s + 1])
        k.act(rt.ap[:, 0:nsub], ss.ap[:, 0:nsub], AF.Sqrt, [ss], [rt], scale=1.0 / D, bias=EPS)
        k.recip("dve", rstd.ap[:, 0:nsub], rt.ap[:, 0:nsub], [rt], [rstd])
        for s in range(nsub):
            k.stt("dve", t1.ap, xs.ap[:, s, :], rstd.ap[:, s:s + 1], A.ap, ALU.mult, ALU.mult, [xs, rstd, A], [t1])
            k.tt("pool", hb.ap, t1.ap, B.ap, ALU.add, [t1, B], [hb])
            pb = k.bank()
            pbb = pb.ap.bitcast(BF16)
            for j in range(8):
                k.transpose(pb, pbb[:, j * 128:(j + 1) * 128], hb.ap[:, j * 128:(j + 1) * 128], ident, [hb, cstb])
            k.copy("act", hstage.ap[:, :, s * 128:(s + 1) * 128],
                   pbb[:, 0:1024].rearrange("p (j t) -> p j t", j=8), [pb], [hstage])

    def norm_tmp(nsub):
        return (k.alloc("ss", [128, 8], F32), k.alloc("rt", [128, 8], F32), k.alloc("rstd", [128, 8], F32),
                k.alloc("junk", [128, D], BF16), k.alloc("t1", [128, D], F32), k.alloc("hb", [128, D], BF16))

    def epi_gen(xs, nsub, G, A, B, hstage, tmp, ss1, rt1, rs1, store_x, store_h, do_norm):
        ss, rt, rstd, junk, t1, hb = tmp
        k.memset("pool", ss1, 0.0)
        for s in range(nsub):
            for hh in range(2):
                pb = ps[4 + 2 * s + hh]
                k.act(junk.ap[:, hh * 512:(hh + 1) * 512], pb.ap, AF.Square, [pb], [junk, ss1],
                      accum_out=ss1.ap[:, 2 * s + hh: 2 * s + hh + 1])
            k.tt("dve", rt1.ap[:, s:s + 1], ss1.ap[:, 2 * s:2 * s + 1], ss1.ap[:, 2 * s + 1:2 * s + 2], ALU.add, [ss1], [rt1])
        yield
        for s in range(nsub):
            k.act(rt1.ap[:, s:s + 1], rt1.ap[:, s:s + 1], AF.Sqrt, [rt1], [rt1], scale=1.0 / D, bias=EPS)
            k.recip("dve", rs1.ap[:, s:s + 1], rt1.ap[:, s:s + 1], [rt1], [rs1])
        yield
        for s in range(nsub):
            for hh in range(2):
                pb = ps[4 + 2 * s + hh]
                hs_ = slice(hh * 512, (hh + 1) * 512)
                k.stt("dve", t1.ap[:, hs_], pb.ap, rs1.ap[:, s:s + 1], G.ap[:, hs_], ALU.mult, ALU.mult, [pb, rs1, G], [t1])
            k.tt("pool", xs.ap[:, s, :], xs.ap[:, s, :], t1.ap, ALU.add, [xs, t1], [xs])
            yield
        store_x()
        if not do_norm:
            return
        k.memset("pool", ss, 0.0)
        for s in range(nsub):
            k.act(junk.ap, xs.ap[:, s, :], AF.Square, [xs], [junk, ss], accum_out=ss.ap[:, s:s + 1])
        yield
        k.act(rt.ap[:, 0:nsub], ss.ap[:, 0:nsub], AF.Sqrt, [ss], [rt], scale=1.0 / D, bias=EPS)
        k.recip("dve", rstd.ap[:, 0:nsub], rt.ap[:, 0:nsub], [rt], [rstd])
        yield
        for s in range(nsub):
            k.stt("dve", t1.ap, xs.ap[:, s, :], rstd.ap[:, s:s + 1], A.ap, ALU.mult, ALU.mult, [xs, rstd, A], [t1])
            k.tt("pool", hb.ap, t1.ap, B.ap, ALU.add, [t1, B], [hb])
            yield
            pb = k.bank()
            pbb = pb.ap.bitcast(BF16)
            for j in range(8):
                k.transpose(pb, pbb[:, j * 128:(j + 1) * 128], hb.ap[:, j * 128:(j + 1) * 128], ident, [hb, cstb])
            k.copy("act", hstage.ap[:, :, s * 128:(s + 1) * 128],
                   pbb[:, 0:1024].rearrange("p (j t) -> p j t", j=8), [pb], [hstage])
            yield
        store_h()

    def phase_prenorm0():
        k.phase_begin()
        AB = {}
        for w, nm in ((0, "lat"), (1, "ctx")):
            AB[nm] = (load_bc(f"A1{nm}", 0, w, 0), load_bc(f"B1{nm}", 0, w, 1))
        tmp = norm_tmp(4)
        xs = [k.alloc(f"xs{i}", [128, 4, D], F32) for i in range(2)]
        hs = [k.alloc(f"hs{i}", [128, 8, 512], BF16) for i in range(2)]
        for ti, (t0, n, kind) in enumerate(TILES512):
            nsub = n // 128
            x_ = xs[ti % 2]
            h_ = hs[ti % 2]
            k.load(x_, tokmaj(xin, t0, n), dst=x_.ap[:, 0:nsub, :])
            norm_mod_T(x_, nsub, AB[kind][0], AB[kind][1], h_, tmp)
            k.store(hT[:, :, t0:t0 + n], h_, src=h_.ap[:, :, 0:n])
        k.phase_end()

    def phase_inproj_gla(l):
        k.phase_begin()
        W = w_in[l]
        wq = k.alloc("wq", [128, 8, 512], BF16)
        wk = k.alloc("wk", [128, 8, 512], BF16)
        wv = k.alloc("wv", [128, 8, 1024], BF16)
        wg = k.alloc("wg", [128, 8, 1024], BF16)
        wl = k.alloc("wl", [128, 8, 32], BF16)
        k.load(wq, wview(W, C_Q, C_Q + 512), queue="pool")
        k.load(wk, wview(W, C_K, C_K + 512), queue="pool")
        k.load(wl, wview(W, C_LF, C_LF + 32), queue="pool")
        k.load(wv, wview(W, C_V, C_V + 1024), queue="pool")
        k.load(wg, wview(W, C_G, C_G + 1024), queue="pool")
        ggla = k.alloc("ggla", [128, D], F32)
        k.load(ggla, rows[l, 0, 10240:11264].partition_broadcast(128))
        hts = [k.alloc(f"ht{i}", [128, 8, 512], BF16) for i in range(2)]
        qs = [k.alloc(f"qs{i}", [128, 4, 512], BF16) for i in range(2)]
        ks_ = [k.alloc(f"ks{i}", [128, 4, 512], BF16) for i in range(2)]
        ls = [k.alloc(f"ls{i}", [32, 512], BF16) for i in range(2)]
        kks = [k.alloc(f"kks{i}", [128, 4, 512], BF16) for i in range(2)]
        vvs = [k.alloc(f"vvs{i}", [128, 4, D], BF16) for i in range(2)]
        sgs = [k.alloc(f"sgs{i}", [128, 4, D], BF16) for i in range(2)]
        stmp = [k.alloc(f"stmp{i}", [128, 512], F32) for i in range(2)]
        for ti, (t0, n, kind) in enumerate(TILES512):
            nsub = n // 128
            ht = hts[ti % 2]
            q_, k_, l_, kk_, vv_, sg_ = qs[ti % 2], ks_[ti % 2], ls[ti % 2], kks[ti % 2], vvs[ti % 2], sgs[ti % 2]
            k.load(ht, hT[:, :, t0:t0 + n], dst=ht.ap[:, :, 0:n])
            for j in range(4):
                pb = k.bank()
                for kc in range(8):
                    k.mm(pb, pb.ap[:, 0:n], wq.ap[:, kc, j * 128:(j + 1) * 128], ht.ap[:, kc, 0:n], kc == 0, kc == 7, [wq, ht])
                k.act(q_.ap[:, j, 0:n], pb.ap[:, 0:n], AF.Copy, [pb], [q_], scale=128.0 ** -0.5)
            for j in range(4):
                pb = k.bank()
                for kc in range(8):
                    k.mm(pb, pb.ap[:, 0:n], wk.ap[:, kc, j * 128:(j + 1) * 128], ht.ap[:, kc, 0:n], kc == 0, kc == 7, [wk, ht])
                k.copy("dve", k_.ap[:, j, 0:n], pb.ap[:, 0:n], [pb], [k_])
            pb = k.bank()
            for kc in range(8):
                k.mm(pb, pb.ap[0:32, 0:n], wl.ap[:, kc, :], ht.ap[:, kc, 0:n], kc == 0, kc == 7, [wl, ht])
            k.copy("dve", l_.ap[:, 0:n], pb.ap[0:32, 0:n], [pb], [l_])
            for s in range(nsub):
                hs_ = lambda kc: ht.ap[:, kc, s * 128:(s + 1) * 128]
                pb = k.bank()
                for kc in range(8):
                    k.mm(pb, pb.ap, hs_(kc), wk.ap[:, kc, :], kc == 0, kc == 7, [wk, ht])
                k.copy("act", kk_.ap[:, s, :], pb.ap, [pb], [kk_])
                for hh in range(2):
                    pb = k.bank()
                    for kc in range(8):
                        k.mm(pb, pb.ap, hs_(kc), wv.ap[:, kc, hh * 512:(hh + 1) * 512], kc == 0, kc == 7, [wv, ht])
                    k.copy("dve" if hh == 0 else "act", vv_.ap[:, s, hh * 512:(hh + 1) * 512], pb.ap, [pb], [vv_])
                for hh in range(2):
                    pb = k.bank()
                    for kc in range(8):
                        k.mm(pb, pb.ap, hs_(kc), wg.ap[:, kc, hh * 512:(hh + 1) * 512], kc == 0, kc == 7, [wg, ht])
                    st = stmp[hh]
                    k.act(st.ap, pb.ap, AF.Silu, [pb], [st])
                    k.tt("pool", sg_.ap[:, s, hh * 512:(hh + 1) * 512], st.ap, ggla.ap[:, hh * 512:(hh + 1) * 512],
                         ALU.mult, [st, ggla], [sg_])
            k.store(qT[:, :, t0:t0 + n], q_, src=q_.ap[:, :, 0:n])
            k.store(kT[:, :, t0:t0 + n], k_, src=k_.ap[:, :, 0:n])
            k.store(lrT[0, :, t0:t0 + n], l_, src=l_.ap[0:16, 0:n])
            k.store(lrT[1, :, t0:t0 + n], l_, src=l_.ap[16:32, 0:n])
            k.store(tokmaj(kk, t0, n), kk_, src=kk_.ap[:, 0:nsub, :])
            k.store(tokmaj(vv, t0, n), vv_, src=vv_.ap[:, 0:nsub, :])
            k.store(tokmaj(sg, t0, n), sg_, src=sg_.ap[:, 0:nsub, :])
        k.phase_end()

    def phase_conv_pool(l):
        k.phase_begin()
        W = w_in[l]
        wa_ = k.alloc("wa_", [128, 8, 512], BF16)
        wb_ = k.alloc("wb_", [128, 8, 512], BF16)
        wp_ = k.alloc("wp_", [128, 8, 512], BF16)
        wpg = k.alloc("wpg", [128, 4, 128], BF16)
        ppb = k.alloc("ppb", [128, NPP], F32)
        rcf = k.alloc("rcf", [128, 1280], F32)
        k.load(rcf, cst_d[:, 1024:2304])
        k.load(wa_, wview(W, C_PA, C_PA + 512), queue="pool")
        k.load(wb_, wview(W, C_PB, C_PB + 512), queue="pool")
        k.load(wp_, wview(W, C_PL, C_PL + 512), queue="pool")
        k.load(wpg, w_pool_g[l].rearrange("g c d -> c g d"), queue="pool")
        k.load(ppb, pp_d[l])
        wdw = lambda c, t: ppb.ap[:, 24 + c * 31 + t: 24 + c * 31 + t + 1]
        pcol = lambda base, c: ppb.ap[:, base + c: base + c + 1]
        B_DW, G_LN, B_LN, S_PL = 148, 152, 156, 160
        PLl = k.alloc("PLl", [128, 4, 80 * 64], BF16)
        PLc = k.alloc("PLc", [128, 4, 272], BF16)
        k.memset("pool", PLl, 0.0)
        k.memset("pool", PLc, 0.0)
        k.phase_begin()
        upls = [k.alloc(f"upl{i}", [128, 4, 8 * 94], BF16) for i in range(2)]
        upc = k.alloc("upc", [128, 4, 286], BF16)
        for u_ in upls:
            k.memset("dve", u_, 0.0)
        k.memset("dve", upc, 0.0)
        dg = k.alloc("dg", [128, 4, 31, 128], BF16)
        for c in range(4):
            k.tt("dve" if c % 2 == 0 else "pool", dg.ap[:, c, :, :], ident.unsqueeze(1).broadcast_to([128, 31, 128]),
                 ppb.ap[:, 24 + c * 31: 24 + (c + 1) * 31].unsqueeze(2).broadcast_to([128, 31, 128]), ALU.mult, [cstb, ppb], [dg])
        hts = [k.alloc(f"ht{i}", [128, 8, 512], BF16) for i in range(2)]
        sgm = [k.alloc(f"sgm{i}", [128, 512], F32) for i in range(2)]
        accs = [k.alloc(f"acc{i}", [128, 4, 512], F32) for i in range(2)]
        ybf = k.alloc("ybf", [128, 4, 512], BF16)
        ysq = k.alloc("ysq", [128, 4, 512], BF16)
        mean = k.alloc("mean", [128, 512], F32)
        m2 = k.alloc("m2", [128, 512], F32)
        var = k.alloc("var", [128, 512], F32)
        rs = k.alloc("rs", [128, 512], F32)
        tn = [k.alloc(f"tn{i}", [128, 512], F32) for i in range(2)]
        cvs = [k.alloc(f"cvs{i}", [128, 4, 512], BF16) for i in range(2)]

        def ln_gen(acc, cv_, t0, n):
            for c in range(4):
                k.copy("act", ybf.ap[:, c, 0:n], acc.ap[:, c, 0:n], [acc], [ybf])
                k.act(ysq.ap[:, c, 0:n], acc.ap[:, c, 0:n], AF.Square, [acc], [ysq])
            yield
            pm = k.bank()
            pq = k.bank()
            for c in range(4):
                k.mm(pm, pm.ap[:, 0:n], onesm, ybf.ap[:, c, 0:n], c == 0, c == 3, [cstb, ybf])
            for c in range(4):
                k.mm(pq, pq.ap[:, 0:n], onesm, ysq.ap[:, c, 0:n], c == 0, c == 3, [cstb, ysq])
            k.copy("act", mean.ap[:, 0:n], pm.ap[:, 0:n], [pm], [mean])
            k.act(m2.ap[:, 0:n], pm.ap[:, 0:n], AF.Square, [pm], [m2])
            k.tt("dve", var.ap[:, 0:n], pq.ap[:, 0:n], m2.ap[:, 0:n], ALU.subtract, [pq, m2], [var])
            k.tsmax("dve", var.ap[:, 0:n], var.ap[:, 0:n], 0.0, [var], [var])
            yield
            k.act(m2.ap[:, 0:n], var.ap[:, 0:n], AF.Sqrt, [var], [m2], bias=EPS)
            k.recip("dve", rs.ap[:, 0:n], m2.ap[:, 0:n], [m2], [rs])
            yield
            for c in range(4):
                t_ = tn[c % 2]
                k.tt("dve", t_.ap[:, 0:n], acc.ap[:, c, 0:n], mean.ap[:, 0:n], ALU.subtract, [acc, mean], [t_])
                k.tt("pool", t_.ap[:, 0:n], t_.ap[:, 0:n], rs.ap[:, 0:n], ALU.mult, [t_, rs], [t_])
                k.act(cv_.ap[:, c, 0:n], t_.ap[:, 0:n], AF.Silu, [t_, ppb], [cv_], scale=pcol(G_LN, c), bias=pcol(B_LN, c))
            k.store(cvT[:, :, t0:t0 + n], cv_, src=cv_.ap[:, :, 0:n])

        pending = None
        for ti, (t0, n, kind) in enumerate(TILES512):
            ht = hts[ti % 2]
            acc = accs[ti % 2]
            k.load(ht, hT[:, :, t0:t0 + n], dst=ht.ap[:, :, 0:n])
            lat = kind == "lat"
            up = upls[ti % 2] if lat else upc
            r0 = (t0 - NCTX) // 64
            for c in range(4):
                pa = k.bank()
                pb = k.bank()
                pl = k.bank()
                for (pbk, wsrc) in ((pa, wa_), (pb, wb_), (pl, wp_)):
                    for kc in range(8):
                        k.mm(pbk, pbk.ap[:, 0:n], wsrc.ap[:, kc, c * 128:(c + 1) * 128], ht.ap[:, kc, 0:n],
                             kc == 0, kc == 7, [wsrc, ht])
                sg_ = sgm[c % 2]
                k.act(sg_.ap[:, 0:n], pb.ap[:, 0:n], AF.Sigmoid, [pb], [sg_])
                if lat:
                    uint = up.ap[:, c, :].rearrange("p (r w) -> p r w", w=94)[:, :, 15:79]
                    k.tt("dve", uint, pa.ap.rearrange("p (r w) -> p r w", w=64), sg_.ap.rearrange("p (r w) -> p r w", w=64),
                         ALU.mult, [pa, sg_], [up])
                    k.copy("act", PLl.ap[:, c, (8 + r0) * 64:(8 + r0) * 64 + 512], pl.ap, [pl], [PLl])
                else:
                    k.tt("dve", up.ap[:, c, 15:15 + 256], pa.ap[:, 0:256], sg_.ap[:, 0:256], ALU.mult, [pa, sg_], [up])
                    k.copy("act", PLc.ap[:, c, 8:8 + 256], pl.ap[:, 0:256], [pl], [PLc])
                if pending is not None:
                    next(pending, None)
            if pending is not None:
                for _ in pending:
                    pass
                pending = None
            for c in range(4):
                pc = k.bank()
                for tap in range(31):
                    if lat:
                        src = up.ap[:, c, :].rearrange("p (r w) -> p r w", w=94)[:, :, tap:tap + 64]
                        dst = pc.ap.rearrange("p (r w) -> p r w", w=64)
                    else:
                        src = up.ap[:, c, tap:tap + 256]
                        dst = pc.ap[:, 0:256]
                    k.mm(pc, dst, dg.ap[:, c, tap, :], src, tap == 0, tap == 30, [dg, up])
                k.act(acc.ap[:, c, 0:n], pc.ap[:, 0:n], AF.Identity, [pc, ppb], [acc], bias=pcol(B_DW, c))
            pending = ln_gen(acc, cvs[ti % 2], t0, n)
        for _ in pending:
            pass

        k.phase_end()
        tA = k.alloc("tA", [128, 80 * 64], F32)
        tB = k.alloc("tB", [128, 80 * 64], F32)
        dTb = [k.alloc(f"dTb{i}", [128, 4096], BF16) for i in range(2)]
        pls = [k.alloc(f"pls{i}", [128, 4096], BF16) for i in range(2)]
        for (PL, R, Wd, rc0, tok0) in ((PLc, 256, 1, 256, 0), (PLl, 64, 64, 0, NCTX)):
            ntok = R * Wd
            for g in range(4):
                u = PL.ap[:, g, :]
                sl = lambda a, lo, hi: a[:, lo * Wd: hi * Wd]
                k.tt("dve", sl(tA.ap, 1, R + 16), sl(u, 0, R + 15), sl(u, 1, R + 16), ALU.add, [PL], [tA])
                cur = tA
                if g >= 1:
                    k.tt("dve", sl(tB.ap, 2, R + 15), sl(tA.ap, 1, R + 14), sl(tA.ap, 3, R + 16), ALU.add, [tA], [tB])
                    cur = tB
                if g >= 2:
                    k.tt("dve", sl(tA.ap, 4, R + 13), sl(tB.ap, 2, R + 11), sl(tB.ap, 6, R + 15), ALU.add, [tB], [tA])
                    cur = tA
                if g >= 3:
                    k.tt("dve", sl(tB.ap, 8, R + 9), sl(tA.ap, 4, R + 5), sl(tA.ap, 12, R + 13), ALU.add, [tA], [tB])
                    cur = tB
                oth = tB if cur is tA else tA
                S_ = sl(cur.ap, 8, 8 + R)
                rc = rcf.ap[:, rc0 + g * R: rc0 + (g + 1) * R]
                if Wd > 1:
                    S3 = S_.rearrange("p (r w) -> p r w", w=Wd)
                    O3 = sl(oth.ap, 8, 8 + R).rearrange("p (r w) -> p r w", w=Wd)
                    k.tt("dve", O3, S3, rc.unsqueeze(2).broadcast_to([128, R, Wd]), ALU.mult, [cur, rcf], [oth])
                else:
                    k.tt("dve", sl(oth.ap, 8, 8 + R), S_, rc, ALU.mult, [cur, rcf], [oth])
                db = dTb[g % 2]
                k.tt("pool", db.ap[:, 0:ntok], sl(oth.ap, 8, 8 + R), sl(u, 8, 8 + R), ALU.subtract, [oth, PL], [db])
                po = pls[g % 2]
                for c0 in range(0, ntok, 512):
                    nn = min(512, ntok - c0)
                    pb = k.bank()
                    k.mm(pb, pb.ap[:, 0:nn], wpg.ap[:, g, :], db.ap[:, c0:c0 + nn], True, True, [wpg, db])
                    k.act(po.ap[:, c0:c0 + nn], pb.ap[:, 0:nn], AF.Copy, [pb, ppb], [po], scale=pcol(S_PL, g))
                k.store(plT[:, g, tok0:tok0 + ntok], po, src=po.ap[:, 0:ntok])
        k.phase_end()

    def phase_gla(l, d):
        fwd = d == "f"
        k.phase_begin()
        di = 0 if fwd else 1
        wdec = k.alloc("wdec", [16, 512], BF16)
        bdec = k.alloc("bdec", [1, 512], BF16)
        k.load(wdec, w_decay[l, di], queue="pool")
        k.load(bdec, rows[l, :, 11264 + di * 512: 11264 + (di + 1) * 512], queue="pool")
        Sbs = [k.alloc(f"Sb{i}", [128, 4, 256], BF16) for i in range(2)]
        k.memset("pool", Sbs[0], 0.0)
        k.memset("pool", Sbs[1], 0.0)
        dgam = [k.alloc(f"dgam{i}", [128, 4, 128], BF16) for i in range(3)]
        NS = 2
        qTt = [k.alloc(f"qTt{i}", [128, 4, 512], BF16) for i in range(NS)]
        kTt = [k.alloc(f"kTt{i}", [128, 4, 512], BF16) for i in range(NS)]
        kkt = [k.alloc(f"kkt{i}", [128, 4, 512], BF16) for i in range(NS)]
        vvt = [k.alloc(f"vvt{i}", [128, 4, D], BF16) for i in range(NS)]
        lrt = [k.alloc(f"lrt{i}", [16, 512], BF16) for i in range(NS)]
        oft = [k.alloc(f"oft{i}", [128, 4, D], F32) for i in range(NS)]
        if not fwd:
            sgt = [k.alloc(f"sgt{i}", [128, 4, D], BF16) for i in range(NS)]
            ogs = [k.alloc(f"ogs{i}", [128, 8, 512], BF16) for i in range(NS)]
            otot = k.alloc("otot", [128, D], F32)
            sqj = k.alloc("sqj", [128, 256], BF16)
            ssq = k.alloc("ssq", [128, 4], F32)
            rt4 = k.alloc("rt4", [128, 4], F32)
            rs4 = k.alloc("rs4", [128, 4], F32)
            ogb = [k.alloc(f"og{i}", [128, D], BF16) for i in range(2)]
        et = [k.alloc(f"et{i}", [128, 512], F32) for i in range(2)]
        spb = [k.alloc(f"spb{i}", [128, 512], BF16) for i in range(2)]
        eq = [k.alloc(f"eq{i}", [128, 512], BF16) for i in range(2)]
        ek = [k.alloc(f"ek{i}", [128, 512], BF16) for i in range(2)]
        eke = [k.alloc(f"eke{i}", [128, 512], BF16) for i in range(2)]
        qi = [k.alloc(f"qi{i}", [128, 4, 128], BF16) for i in range(3)]
        ki = [k.alloc(f"ki{i}", [128, 4, 128], BF16) for i in range(3)]
        ke = [k.alloc(f"ke{i}", [128, 512], BF16) for i in range(3)]
        gam = [k.alloc(f"gam{i}", [128, 4], F32) for i in range(3)]
        am = k.alloc("am", [128, 4, 128], BF16)
        zt, bT, bk, at = ps[0], ps[1], ps[2], ps[3]

        tiles = list(TILES512)
        if not fwd:
            tiles = [tiles[0]] + tiles[:0:-1]
        seq = []
        for si, (t0, n, kind) in enumerate(tiles):
            cs_ = list(range(n // 128))
            if not fwd:
                cs_ = cs_[::-1]
            for ci, c in enumerate(cs_):
                seq.append((si, t0, n, c, ci == 0, ci == len(cs_) - 1))
        NCH = len(seq)

        def load_tile(si, t0, n):
            s_ = si % NS
            nsub = n // 128
            k.load(lrt[s_], lrT[di, :, t0:t0 + n], dst=lrt[s_].ap[:, 0:n])
            k.load(qTt[s_], qT[:, :, t0:t0 + n], dst=qTt[s_].ap[:, :, 0:n])
            k.load(kTt[s_], kT[:, :, t0:t0 + n], dst=kTt[s_].ap[:, :, 0:n])
            k.load(kkt[s_], tokmaj(kk, t0, n), dst=kkt[s_].ap[:, 0:nsub, :])
            k.load(vvt[s_], tokmaj(vv, t0, n), dst=vvt[s_].ap[:, 0:nsub, :])
            if not fwd:
                k.load(oft[s_], tokmaj(of_d, t0, n), dst=oft[s_].ap[:, 0:nsub, :])
                k.load(sgt[s_], tokmaj(sg, t0, n), dst=sgt[s_].ap[:, 0:nsub, :])

        def A_pe(i):
            si, t0, n, c, first, last = seq[i]
            s_ = si % NS
            if first:
                load_tile(si, t0, n)
            cs = slice(c * 128, (c + 1) * 128)
            k.mm(zt, zt.ap, lrt[s_].ap[:, cs], wdec.ap, True, False, [lrt[s_], wdec])
            k.mm(zt, zt.ap, onesr.ap, bdec.ap, False, True, [onesr, bdec])

        def A_act(i):
            p_ = i % 2
            k.act(et[p_].ap, zt.ap, AF.Exp, [zt], [et[p_]], scale=-1.0)
            k.act(spb[p_].ap, et[p_].ap, AF.Ln, [et[p_]], [spb[p_]], bias=1.0)

        def B_pe(i):
            p_ = i % 2
            for j in range(4):
                k.mm(bT, bT.ap[:, j * 128:(j + 1) * 128], spb[p_].ap[:, j * 128:(j + 1) * 128], tri[d], True, True, [spb[p_], cstb])
            k.mm(bk, bk.ap, UU[d], spb[p_].ap, True, True, [spb[p_], cstb])

        def B_act(i):
            p_ = i % 2
            k.act(eq[p_].ap, bT.ap, AF.Exp, [bT], [eq[p_]])
            k.act(ek[p_].ap, bT.ap, AF.Exp, [bT], [ek[p_]], scale=-1.0)
            col = 127 if fwd else 0
            k.act(gam[i % 3].ap, bT.ap.rearrange("p (j t) -> p j t", j=4)[:, :, col], AF.Exp, [bT], [gam[i % 3]])
            k.act(eke[p_].ap, bk.ap, AF.Exp, [bk], [eke[p_]])

        def B_vec(i):
            si, t0, n, c, first, last = seq[i]
            s_ = si % NS
            p_ = i % 2
            q_ = i % 3
            cs = slice(c * 128, (c + 1) * 128)
            k.tt("pool", qi[q_].ap, qTt[s_].ap[:, :, cs], eq[p_].ap.rearrange("p (j t) -> p j t", j=4), ALU.mult,
                 [qTt[s_], eq[p_]], [qi[q_]])
            k.tt("pool", ki[q_].ap, kTt[s_].ap[:, :, cs], ek[p_].ap.rearrange("p (j t) -> p j t", j=4), ALU.mult,
                 [kTt[s_], ek[p_]], [ki[q_]])
            k.tt("dve", ke[q_].ap, kkt[s_].ap[:, c, :], eke[p_].ap, ALU.mult, [kkt[s_], eke[p_]], [ke[q_]])
            k.tt("dve", dgam[q_].ap, ident.unsqueeze(1).broadcast_to([128, 4, 128]),
                 gam[q_].ap.unsqueeze(2).broadcast_to([128, 4, 128]), ALU.mult, [cstb, gam[q_]], [dgam[q_]])

        def C_att(i):
            q_ = i % 3
            for j in range(4):
                k.mm(at, at.ap[:, j * 128:(j + 1) * 128], ki[q_].ap[:, j, :], qi[q_].ap[:, j, :], True, True, [ki[q_], qi[q_]])

        def C_mask(i):
            k.tt("dve", am.ap, at.ap.rearrange("p (j t) -> p j t", j=4), msk[d].unsqueeze(1).broadcast_to([128, 4, 128]),
                 ALU.mult, [at, cstb], [am])

        def C_pe2(i):
            si, t0, n, c, first, last = seq[i]
            s_ = si % NS
            p_ = i % 3
            Sb = Sbs[i % 2]
            for j in range(4):
                pd = ps[6 + j // 2]
                dd = pd.ap[:, (j % 2) * 256:(j % 2 + 1) * 256]
                k.mm(pd, dd, dgam[p_].ap[:, j, :], Sb.ap[:, j, :], True, False, [dgam[p_], Sb])
                k.mm(pd, dd, ke[p_].ap[:, j * 128:(j + 1) * 128], vvt[s_].ap[:, c, j * 256:(j + 1) * 256], False, True,
                     [ke[p_], vvt[s_]])
            for j in range(4):
                po = ps[4 + j // 2]
                oo = po.ap[:, (j % 2) * 256:(j % 2 + 1) * 256]
                k.mm(po, oo, am.ap[:, j, :], vvt[s_].ap[:, c, j * 256:(j + 1) * 256], True, False, [am, vvt[s_]])
                k.mm(po, oo, qi[p_].ap[:, j, :], Sb.ap[:, j, :], False, True, [qi[p_], Sb])

        def S_copy(i):
            Sn = Sbs[(i + 1) % 2]
            k.copy("dve", Sn.ap.rearrange("p j v -> p (j v)").rearrange("p (a b) -> p a b", a=2), k.psum_t[:, 6:8, :],
                   [ps[6], ps[7]], [Sn])

        o2 = k.psum_t[:, 4:6, :]

        def D_evac(i):
            si, t0, n, c, first, last = seq[i]
            s_ = si % NS
            nsub = n // 128
            if fwd:
                k.copy("act", oft[s_].ap[:, c, :].rearrange("p (a b) -> p a b", a=2), o2, [ps[4], ps[5]], [oft[s_]])
                if last:
                    k.store(tokmaj(of_d, t0, n), oft[s_], src=oft[s_].ap[:, 0:nsub, :])
            else:
                k.tt("dve", otot.ap.rearrange("p (a b) -> p a b", a=2), o2, oft[s_].ap[:, c, :].rearrange("p (a b) -> p a b", a=2),
                     ALU.add, [ps[4], ps[5], oft[s_]], [otot])

        def D_epi_act(i):
            k.memset("pool", ssq, 0.0)
            for j in range(4):
                k.act(sqj.ap, otot.ap[:, j * 256:(j + 1) * 256], AF.Square, [otot], [sqj, ssq], accum_out=ssq.ap[:, j:j + 1])
            k.act(rt4.ap, ssq.ap, AF.Ln, [ssq], [rt4], scale=1.0 / 256, bias=EPS)
            k.act(rs4.ap, rt4.ap, AF.Exp, [rt4], [rs4], scale=-0.5)

        def D_epi_vec(i):
            si, t0, n, c, first, last = seq[i]
            s_ = si % NS
            for j in range(4):
                js = slice(j * 256, (j + 1) * 256)
                k.stt("dve", ogb[i % 2].ap[:, js], otot.ap[:, js], rs4.ap[:, j:j + 1], sgt[s_].ap[:, c, js],
                      ALU.mult, ALU.mult, [otot, rs4, sgt[s_]], [ogb[i % 2]])

        def E_all(i):
            si, t0, n, c, first, last = seq[i]
            s_ = si % NS
            pbb = at.ap.bitcast(BF16)
            og = ogb[i % 2]
            for j in range(8):
                k.transpose(at, pbb[:, j * 128:(j + 1) * 128], og.ap[:, j * 128:(j + 1) * 128], ident, [og, cstb])
            k.copy("act", ogs[s_].ap[:, :, c * 128:(c + 1) * 128], pbb[:, 0:1024].rearrange("p (j t) -> p j t", j=8),
                   [at], [ogs[s_]])
            if last:
                k.store(ogT[:, :, t0:t0 + n], ogs[s_], src=ogs[s_].ap[:, :, 0:n])

        ok = lambda i: 0 <= i < NCH
        if not fwd:
            k.memset("pool", ssq, 0.0)
        for s_ in range(NCH + 6):
            iA, iB, iC, iD, iE = s_, s_ - 1, s_ - 3, s_ - 4, s_ - 5
            if ok(iD):
                D_evac(iD)
            if ok(iC):
                C_att(iC)
                C_mask(iC)
            if ok(iB):
                B_pe(iB)
            if ok(iA):
                A_pe(iA)
            if ok(iC):
                C_pe2(iC)
                S_copy(iC)
            if ok(iB):
                B_act(iB)
            if ok(iA):
                A_act(iA)
            if ok(iB):
                B_vec(iB)
            if (not fwd) and ok(iD):
                D_epi_act(iD)
                D_epi_vec(iD)
            if (not fwd) and ok(iE):
                E_all(iE)
        k.phase_end()

    def phase_merge(l):
        k.phase_begin()
        TS = 256
        wgo = k.alloc("wgo", [128, 8, D], BF16)
        wco = k.alloc("wco", [128, 4, D], BF16)
        wpo = k.alloc("wpo", [128, 4, D], BF16)
        wgt = k.alloc("wgt", [128, 8, 3 * D], BF16)
        wo = k.alloc("wo", [128, 8, D], BF16)
        ppb = k.alloc("ppb", [128, 24], F32)
        k.load(wgt, wview(w_in[l], C_GT, C_GT + 3 * D), queue="pool")
        k.load(wgo, wview(w_gla_o[l], 0, D), queue="pool")
        k.load(wco, wview(w_conv_o[l], 0, D), queue="pool")
        k.load(wpo, wview(w_pool_o[l], 0, D), queue="pool")
        k.load(wo, wview(w_out[l], 0, D), queue="pool")
        k.load(ppb, pp_d[l, :, 0:24])
        G1 = k.alloc("G1", [128, D], F32)
        A2 = k.alloc("A2", [128, D], F32)
        B2 = k.alloc("B2", [128, D], F32)
        hts = [k.alloc(f"ht{i}", [128, 8, TS], BF16) for i in range(2)]
        ogs_ = [k.alloc(f"og_{i}", [128, 8, TS], BF16) for i in range(2)]
        cvs_ = [k.alloc(f"cv_{i}", [128, 4, TS], BF16) for i in range(2)]
        pls_ = [k.alloc(f"pl_{i}", [128, 4, TS], BF16) for i in range(2)]
        xss = [k.alloc(f"xs{i}", [128, 2, D], F32) for i in range(2)]
        h2ss = [k.alloc(f"h2s{i}", [128, 8, TS], BF16) for i in range(2)]
        gts = [k.alloc(f"gts{i}", [128, 3, TS], BF16) for i in range(2)]
        ysb = [k.alloc(f"ysb{i}", [128, 3, TS], BF16) for i in range(2)]
        mixs = [k.alloc(f"mix{i}", [128, 8, TS], BF16) for i in range(2)]
        tmp = norm_tmp(2)
        ss1 = k.alloc("ss1", [128, 8], F32)
        rt1 = k.alloc("rt1", [128, 4], F32)
        rs1 = k.alloc("rs1", [128, 4], F32)
        junk, t1 = tmp[3], tmp[4]
        xsrc = xin if l == 0 else xres
        state = {"kind": None}
        k.bank_set = [0, 1, 2, 3]

        def epilogue(t0, n, kind, xs, h2s):
            if kind != state["kind"]:
                w = 0 if kind == "lat" else 1
                k.load(G1, modrow[l, w, 2, :].partition_broadcast(128))
                k.load(A2, modrow[l, w, 3, :].partition_broadcast(128))
                k.load(B2, modrow[l, w, 4, :].partition_broadcast(128))
                state["kind"] = kind
            return epi_gen(xs, n // 128, G1, A2, B2, h2s, tmp, ss1, rt1, rs1,
                           lambda: k.store(tokmaj(xres, t0, n), xs),
                           lambda: k.store(h2T[:, :, t0:t0 + n], h2s), True)

        def drain(g):
            if g is not None:
                for _ in g:
                    pass

        pending = None
        for ti, (t0, n, kind) in enumerate(TILES256):
            nsub = n // 128
            ht, og_, cv_, pl_, xs, h2s, mix = (hts[ti % 2], ogs_[ti % 2], cvs_[ti % 2], pls_[ti % 2], xss[ti % 2],
                                               h2ss[ti % 2], mixs[ti % 2])
            k.load(ht, hT[:, :, t0:t0 + n])
            k.load(og_, ogT[:, :, t0:t0 + n])
            k.load(cv_, cvT[:, :, t0:t0 + n])
            k.load(pl_, plT[:, :, t0:t0 + n])
            k.load(xs, tokmaj(xsrc, t0, n))
            for j in range(8):
                gt = gts[j % 2]
                yb = ysb[j % 2]
                for br in range(3):
                    pb = k.bank()
                    for kc in range(8):
                        k.mm(pb, pb.ap[:, 0:n], wgt.ap[:, kc, br * D + j * 128: br * D + (j + 1) * 128], ht.ap[:, kc, :],
                             kc == 0, kc == 7, [wgt, ht])
                    k.act(gt.ap[:, br, :], pb.ap[:, 0:n], AF.Sigmoid, [pb, ppb], [gt], bias=ppb.ap[:, br * 8 + j: br * 8 + j + 1])
                for br, (wsrc, asrc, nk) in enumerate(((wgo, og_, 8), (wco, cv_, 4), (wpo, pl_, 4))):
                    pb = k.bank()
                    for kc in range(nk):
                        k.mm(pb, pb.ap[:, 0:n], wsrc.ap[:, kc, j * 128:(j + 1) * 128], asrc.ap[:, kc, :],
                             kc == 0, kc == nk - 1, [wsrc, asrc])
                    k.copy("dve", yb.ap[:, br, :], pb.ap[:, 0:n], [pb], [yb])
                k.tt("pool", yb.ap, yb.ap, gt.ap, ALU.mult, [yb, gt], [yb])
                k.tt("pool", yb.ap[:, 0, :], yb.ap[:, 0, :], yb.ap[:, 1, :], ALU.add, [yb], [yb])
                k.tt("pool", mix.ap[:, j, :], yb.ap[:, 0, :], yb.ap[:, 2, :], ALU.add, [yb], [mix])
                if pending is not None:
                    next(pending, None)
                    if j in (3, 6):
                        next(pending, None)
            drain(pending)
            for s in range(nsub):
                for hh in range(2):
                    pb = ps[4 + 2 * s + hh]
                    for kc in range(8):
                        k.mm(pb, pb.ap, mix.ap[:, kc, s * 128:(s + 1) * 128], wo.ap[:, kc, hh * 512:(hh + 1) * 512],
                             kc == 0, kc == 7, [mix, wo])
            pending = epilogue(t0, n, kind, xs, h2s)
        drain(pending)
        k.bank_set = list(range(8))
        k.phase_end()

    def phase_mlp(l):
        k.phase_begin()
        last = l == L - 1
        w1 = k.alloc("w1", [128, 8, 4 * D], BF16)
        w2 = k.alloc("w2", [128, 32, D], BF16)
        for q4 in range(4):
            k.dma("pool", w1.ap[:, :, q4 * D:(q4 + 1) * D], wview(w_mlp1[l], q4 * D, (q4 + 1) * D), [], [w1], w1)
        for q4 in range(4):
            k.dma("pool", w2.ap[:, q4 * 8:(q4 + 1) * 8, :], w_mlp2[l][q4 * D:(q4 + 1) * D, :].rearrange("(k p) c -> p k c", p=128),
                  [], [w2], w2)
        G2 = k.alloc("G2", [128, D], F32)
        if not last:
            A1 = k.alloc("A1", [128, D], F32)
            B1 = k.alloc("B1", [128, D], BF16)
        h2 = [k.alloc(f"h2_{i}", [128, 8, 256], BF16) for i in range(2)]
        hidA = k.alloc("hidA", [128, 24, 256], BF16)
        hidB = k.alloc("hidB", [128, 8, 256], BF16)
        hidf = lambda f: (hidA, hidA.ap[:, f, :]) if f < 24 else (hidB, hidB.ap[:, f - 24, :])
        xs = k.alloc("xs", [128, 2, D], F32)
        hs = None if last else hidB
        trl = [k.alloc(f"trl{i}", [128, 256], BF16) for i in range(2)]
        hb_t = k.alloc("hb", [128, D], BF16)
        tmp = (k.alloc("ss", [128, 8], F32), k.alloc("rt", [128, 8], F32), k.alloc("rstd", [128, 8], F32),
               hb_t, k.alloc("t1", [128, D], F32), hb_t)
        ss1 = k.alloc("ss1", [128, 8], F32)
        rt1 = k.alloc("rt1", [128, 4], F32)
        rs1 = k.alloc("rs1", [128, 4], F32)
        junk, t1 = tmp[3], tmp[4]
        state = {"kind": None}
        k.bank_set = [0, 1, 2, 3]

        def epilogue(t0, n, kind):
            if kind != state["kind"]:
                w = 0 if kind == "lat" else 1
                k.load(G2, modrow[l, w, 5, :].partition_broadcast(128))
                if not last:
                    k.load(A1, modrow[l + 1, w, 0, :].partition_broadcast(128))
                    k.load(B1, modrow[l + 1, w, 1, :].partition_broadcast(128), queue="pool")
                state["kind"] = kind
            if last:
                return epi_gen(xs, n // 128, G2, None, None, None, tmp, ss1, rt1, rs1,
                               lambda: k.store(tokmaj(out, t0 - NCTX, n), xs), None, False)
            return epi_gen(xs, n // 128, G2, A1, B1, hs, tmp, ss1, rt1, rs1,
                           lambda: k.store(tokmaj(xres, t0, n), xs),
                           lambda: k.store(hT[:, :, t0:t0 + n], hs), True)

        def drain(g):
            if g is not None:
                for _ in g:
                    pass

        pending = None
        for ti, (t0, n, kind) in enumerate(TILES256):
            if last and kind == "ctx":
                continue
            nsub = n // 128
            h_ = h2[ti % 2]
            k.load(h_, h2T[:, :, t0:t0 + n])
            if pending is None:
                k.load(xs, tokmaj(xres, t0, n))
            for f in range(32):
                pb = k.bank()
                for kc in range(8):
                    k.mm(pb, pb.ap[:, 0:n], w1.ap[:, kc, f * 128:(f + 1) * 128], h_.ap[:, kc, :], kc == 0, kc == 7, [w1, h_])
                tr_ = trl[f % 2]
                k.act(tr_.ap[:, 0:n], pb.ap[:, 0:n], AF.Relu, [pb], [tr_])
                hb_, ha_ = hidf(f)
                k.tt("dve" if f % 2 == 0 else "pool", ha_, tr_.ap[:, 0:n], tr_.ap[:, 0:n], ALU.mult, [tr_], [hb_])
                if pending is not None and f % 2 == 1 and f <= 21:
                    if next(pending, "done") == "done":
                        pending = None
                        k.load(xs, tokmaj(xres, t0, n))
            if pending is not None:
                drain(pending)
                pending = None
                k.load(xs, tokmaj(xres, t0, n))
            for s in range(nsub):
                for hh in range(2):
                    pb = ps[4 + 2 * s + hh]
                    for f in range(32):
                        hb_, ha_ = hidf(f)
                        k.mm(pb, pb.ap, ha_[:, s * 128:(s + 1) * 128], w2.ap[:, f, hh * 512:(hh + 1) * 512],
                             f == 0, f == 31, [hb_, w2])
            pending = epilogue(t0, n, kind)
        drain(pending)
        k.bank_set = list(range(8))
        k.phase_end()

    stages = []
    stages.append(("adaln", phase_adaln))
    stages.append(("prenorm0", phase_prenorm0))
    for l in range(L):
        stages.append((f"inproj{l}", lambda l=l: phase_inproj_gla(l)))
        stages.append((f"convpool{l}", lambda l=l: phase_conv_pool(l)))
        stages.append((f"glaf{l}", lambda l=l: phase_gla(l, "f")))
        stages.append((f"glab{l}", lambda l=l: phase_gla(l, "b")))
        stages.append((f"merge{l}", lambda l=l: phase_merge(l)))
        stages.append((f"mlp{l}", lambda l=l: phase_mlp(l)))
    for name, fn in stages:
        fn()
        if stop_after == name:
            break
    k.barrier()
    k.emit()
    nc._k_stats = (k.n_instr, k.nsem)
    return nc


def make_consts():
    cst = np.zeros((128, NCST), np.float32)
    s = np.arange(128)[:, None]
    t = np.arange(128)[None, :]
    cst[:, 0:128] = (s == t)
    cst[:, 128:256] = np.where(s <= t, -1.0 / 16, 0.0)
    cst[:, 256:384] = np.where(s >= t, -1.0 / 16, 0.0)
    cst[:, 384:512] = np.where(s > t, -1.0 / 16, 0.0)
    cst[:, 512:640] = np.where(s < t, -1.0 / 16, 0.0)
    cst[:, 640:768] = (s <= t)
    cst[:, 768:896] = (s >= t)
    cst[:, 896:1024] = 1.0 / 512
    for (Ln, off) in ((64, 1024), (256, 1280)):
        for g, w in enumerate((2, 4, 8, 16)):
            left = w // 2
            right = w - 1 - left
            tt_ = np.arange(Ln)
            lo = np.clip(tt_ - left, 0, Ln)
            hi = np.clip(tt_ + right + 1, 0, Ln)
            cst[:, off + g * Ln: off + (g + 1) * Ln] = (1.0 / (hi - lo).astype(np.float32))[None, :]
    return cst


def pack_inputs(inp):
    f = lambda a: np.ascontiguousarray(np.asarray(a, dtype=np.float32))
    rows = np.zeros((L, 1, NR), np.float32)
    pp = np.zeros((L, 128, NPP), np.float32)
    for l in range(L):
        rows[l, 0, 0:6144] = inp["b_ada"][l]
        rows[l, 0, 6144:7168] = inp["g_pre_mix"][l]
        rows[l, 0, 7168:8192] = inp["g_post_mix"][l]
        rows[l, 0, 8192:9216] = inp["g_pre_mlp"][l]
        rows[l, 0, 9216:10240] = inp["g_post_mlp"][l]
        rows[l, 0, 10240:11264] = inp["g_gla"][l]
        rows[l, 0, 11264:11776] = inp["b_decay"][l, 0]
        rows[l, 0, 11776:12288] = inp["b_decay"][l, 1]
        pp[l, :, 0:24] = np.asarray(inp["b_gate"][l]).reshape(24, 128).T
        wd = np.asarray(inp["w_dw"][l])
        pp[l, :, 24:148] = wd.T.reshape(4, 128, 31).transpose(1, 0, 2).reshape(128, 124)
        pp[l, :, 148:152] = np.asarray(inp["b_dw"][l]).reshape(4, 128).T
        pp[l, :, 152:156] = np.asarray(inp["g_conv_ln"][l]).reshape(4, 128).T
        pp[l, :, 156:160] = np.asarray(inp["b_conv_ln"][l]).reshape(4, 128).T
        pp[l, :, 160:164] = np.asarray(inp["s_pool"][l]).reshape(4, 128).T
    shared = {
        "rows": rows, "pp": pp, "cst": make_consts(),
        "w_ada": f(inp["w_ada"]), "w_in": f(inp["w_in"]), "w_decay": f(inp["w_decay"]),
        "w_gla_o": f(inp["w_gla_o"]), "w_conv_o": f(inp["w_conv_o"]), "w_pool_g": f(inp["w_pool_g"]),
        "w_pool_o": f(inp["w_pool_o"]), "w_out": f(inp["w_out"]), "w_mlp1": f(inp["w_mlp1"]), "w_mlp2": f(inp["w_mlp2"]),
    }
    maps = []
    B = inp["x"].shape[0]
    for b in range(B):
        m = dict(shared)
        m["xin"] = np.ascontiguousarray(np.concatenate([inp["ctx"][b], inp["x"][b]], axis=0).astype(np.float32))
        cv = np.zeros((128, 16), np.float32)
        cv[:, 0:8] = np.asarray(inp["c"][b]).reshape(8, 128).T
        cv[:, 8:16] = np.asarray(inp["c_ctx"]).reshape(8, 128).T
        m["cvec"] = cv
        maps.append(m)
    return maps


_NC_CACHE = {}


def kernel(**inputs):
    inp = {k_: np.asarray(v) for k_, v in inputs.items()}
    maps = pack_inputs(inp)
    if "nc" not in _NC_CACHE:
        _NC_CACHE["nc"] = build_program()
    nc = _NC_CACHE["nc"]
    res = run_bass_kernel_spmd(nc, maps, core_ids=list(range(8)))
    return np.stack([np.asarray(r["out"], dtype=np.float32) for r in res.results], axis=0)
```

```python
import numpy as np
import concourse.bass as bass
import concourse.mybir as mybir
from concourse.bass_utils import run_bass_kernel_spmd

F32 = mybir.dt.float32
BF16 = mybir.dt.bfloat16
AF = mybir.ActivationFunctionType
ALU = mybir.AluOpType
AX = mybir.AxisListType
DT_BYTES = {F32: 4, BF16: 2}
ENGS = ("pe", "act", "dve", "pool", "sp")

D = 1024
NCTX = 256
NLAT = 4096
NT = NCTX + NLAT
L = 2
EPS = 1e-6
NR = 12288
NPP = 164
NCST = 2304
C_Q, C_K, C_V, C_G, C_LF, C_LB, C_PA, C_PB, C_PL, C_GT = 0, 512, 1024, 2048, 3072, 3088, 3104, 3616, 4128, 4640
TILES512 = [(0, 256, "ctx")] + [(256 + 512 * i, 512, "lat") for i in range(8)]
TILES256 = [(256 * i, 256, "ctx" if i == 0 else "lat") for i in range(17)]


class Buf:
    __slots__ = ("name", "ap", "lw", "rd", "sem")

    def __init__(self, name, ap=None):
        self.name = name
        self.ap = ap
        self.lw = None
        self.rd = {}
        self.sem = {}


class Op:
    __slots__ = ("eng", "fn", "waits", "signal", "dma_sem", "seq")

    def __init__(self, eng, fn):
        self.eng = eng
        self.fn = fn
        self.waits = []
        self.signal = False
        self.dma_sem = None


class K:
    def __init__(self, nc, arena_bytes=186 * 1024):
        self.nc = nc
        self.ops = {e: [] for e in ENGS}
        self.waited = {e: {} for e in ENGS}
        self.arena_bytes = arena_bytes
        self.arena = nc.alloc_sbuf_tensor("arena", [128, arena_bytes // 2], BF16)
        self.top = 0
        self.psum_t = nc.alloc_psum_tensor("psum_all", [128, 8, 512], F32)
        self.ps = [Buf(f"ps{i}", self.psum_t[:, i, :]) for i in range(8)]
        self.esem = {e: nc.alloc_semaphore(f"sem_{e}") for e in ENGS if e != "sp"}
        self.dsems_free = {"hw": [], "sw": []}
        self.nsem = 0
        self.phase_bufs = []
        self.phase_marks = []
        self.all_dma_sems = []
        self._bank = 0
        self.bank_set = list(range(8))

    def bank(self):
        self._bank = (self._bank + 1) % len(self.bank_set)
        return self.ps[self.bank_set[self._bank]]

    def alloc(self, name, shape, dtype):
        free = int(np.prod(shape[1:]))
        nbytes = (free * DT_BYTES[dtype] + 63) // 64 * 64
        off = self.top
        self.top += nbytes
        assert self.top <= self.arena_bytes, f"SBUF arena overflow at {name}: {self.top}"
        ap = self.arena[0:shape[0], off // 2: off // 2 + free * DT_BYTES[dtype] // 2]
        if dtype != BF16:
            ap = ap.bitcast(dtype)
        if len(shape) > 2:
            names = " ".join(f"d{i}" for i in range(len(shape) - 1))
            kw = {f"d{i}": shape[i + 1] for i in range(len(shape) - 1)}
            ap = ap.rearrange(f"p ({names}) -> p {names}", **kw)
        b = Buf(name, ap)
        self.phase_bufs.append(b)
        return b

    def phase_begin(self):
        self.phase_marks.append((self.top, len(self.phase_bufs)))

    def phase_end(self):
        self.barrier()
        top, nb = self.phase_marks.pop()
        for b in self.phase_bufs[nb:]:
            for cls, sm in b.sem.items():
                self.dsems_free[cls].append(sm)
            b.sem = {}
        del self.phase_bufs[nb:]
        self.top = top

    def _getsem(self, buf, cls):
        if cls not in buf.sem:
            if self.dsems_free[cls]:
                buf.sem[cls] = self.dsems_free[cls].pop()
            else:
                h = self.nc.alloc_semaphore(f"dsem{cls}{self.nsem}")
                self.nsem += 1
                buf.sem[cls] = [h, 0]
                self.all_dma_sems.append(buf.sem[cls])
        return buf.sem[cls]

    def _add_wait(self, op, tok):
        e = op.eng
        if tok[0] == "eng":
            _, f, seq = tok
            if f == e and e == "pe":
                return
            key = ("eng", f)
            if self.waited[e].get(key, -1) >= seq:
                return
            self.waited[e][key] = seq
            self.ops[f][seq].signal = True
            op.waits.append(tok)
        else:
            _, sem, cnt = tok
            key = ("dma", id(sem))
            if self.waited[e].get(key, -1) >= cnt:
                return
            self.waited[e][key] = cnt
            op.waits.append(tok)

    def _record(self, op, reads, writes, tok):
        deps = []
        for b in reads:
            if b.lw is not None:
                deps.append(b.lw)
        for b in writes:
            if b.lw is not None and not (op.dma_sem is not None and b.lw[0] == "dma" and b.lw[1] is op.dma_sem):
                deps.append(b.lw)
            deps.extend(b.rd.values())
        for t in deps:
            self._add_wait(op, t)
        for b in writes:
            b.lw = tok
            b.rd = {}
        for b in reads:
            if b in writes:
                continue
            kk = (tok[0], tok[1]) if tok[0] == "eng" else ("dma", id(tok[1]))
            b.rd[kk] = tok

    def op(self, eng, fn, reads=(), writes=()):
        o = Op(eng, fn)
        o.seq = len(self.ops[eng])
        self.ops[eng].append(o)
        self._record(o, reads, writes, ("eng", eng, o.seq))
        return o

    def dma(self, queue, out_ap, in_ap, reads, writes, sbuf_side):
        sem = self._getsem(sbuf_side, "sw" if queue == "pool" else "hw")
        sem[1] += 16
        o = Op(queue, lambda e, o_=out_ap, i_=in_ap: e.dma_start(out=o_, in_=i_))
        o.dma_sem = sem
        o.seq = len(self.ops[queue])
        self.ops[queue].append(o)
        self._record(o, reads, writes, ("dma", sem, sem[1]))
        return o

    def load(self, buf, src, queue="sp", dst=None):
        return self.dma(queue, buf.ap if dst is None else dst, src, [], [buf], buf)

    def store(self, dst, buf, src=None, queue="sp"):
        return self.dma(queue, dst, buf.ap if src is None else src, [buf], [], buf)

    def barrier(self):
        lasts = {}
        for e in ENGS:
            if e == "sp":
                continue
            s = len(self.ops[e]) - 1
            while s >= 0 and (self.ops[e][s].fn is None or self.ops[e][s].dma_sem is not None):
                s -= 1
            lasts[e] = s
        for e in ENGS:
            o = Op(e, None)
            o.seq = len(self.ops[e])
            for f, s in lasts.items():
                if s >= 0:
                    self._add_wait(o, ("eng", f, s))
            for sem in self.all_dma_sems:
                if sem[1] > 0:
                    self._add_wait(o, ("dma", sem, sem[1]))
            if o.waits:
                self.ops[e].append(o)

    def emit(self):
        nc = self.nc
        cum = {}
        for e in ENGS:
            if e == "sp":
                continue
            c = 0
            arr = []
            for o in self.ops[e]:
                if o.signal:
                    c += 1
                arr.append(c)
            cum[e] = arr
        self.n_instr = {e: len(self.ops[e]) for e in ENGS}

        def run(e, eng):
            for o in self.ops[e]:
                for t in o.waits:
                    if t[0] == "eng":
                        eng.wait_ge(self.esem[t[1]], cum[t[1]][t[2]])
                    else:
                        eng.wait_ge(t[1][0], t[2])
                if o.fn is None:
                    continue
                ins = o.fn(eng)
                if o.dma_sem is not None:
                    ins.then_inc(o.dma_sem[0], 16)
                elif o.signal:
                    ins.then_inc(self.esem[e], 1)

        with nc.Block() as block:
            @block.tensor
            def _(eng):
                run("pe", eng)

            @block.scalar
            def _(eng):
                run("act", eng)

            @block.vector
            def _(eng):
                run("dve", eng)

            @block.gpsimd
            def _(eng):
                run("pool", eng)

            @block.sync
            def _(eng):
                run("sp", eng)

    def mm(self, ps, out_ap, lhsT, rhs, start, stop, reads):
        self.op("pe", lambda e: e.matmul(out_ap, lhsT, rhs, start=start, stop=stop), reads, [ps])

    def act(self, out_ap, in_ap, func, reads, writes, **kw):
        self.op("act", lambda e: e.activation(out=out_ap, in_=in_ap, func=func, **kw), reads, writes)

    def tt(self, eng, out, in0, in1, op, reads, writes):
        self.op(eng, lambda e: e.tensor_tensor(out=out, in0=in0, in1=in1, op=op), reads, writes)

    def stt(self, eng, out, in0, scalar, in1, op0, op1, reads, writes):
        self.op(eng, lambda e: e.scalar_tensor_tensor(out=out, in0=in0, scalar=scalar, in1=in1, op0=op0, op1=op1),
                reads, writes)

    def ts(self, eng, out, in0, s1, s2, op0, op1, reads, writes):
        self.op(eng, lambda e: e.tensor_scalar(out=out, in0=in0, scalar1=s1, scalar2=s2, op0=op0, op1=op1),
                reads, writes)

    def copy(self, eng, out, in_, reads, writes):
        if eng == "act":
            self.op(eng, lambda e: e.activation(out=out, in_=in_, func=AF.Copy), reads, writes)
        else:
            self.op(eng, lambda e: e.tensor_copy(out=out, in_=in_), reads, writes)

    def recip(self, eng, out, in_, reads, writes):
        self.op(eng, lambda e: e.reciprocal(out=out, in_=in_), reads, writes)

    def tsmax(self, eng, out, in0, val, reads, writes):
        self.op(eng, lambda e: e.tensor_scalar_max(out=out, in0=in0, scalar1=val), reads, writes)

    def transpose(self, ps, out, in_, ident, reads):
        self.op("pe", lambda e: e.transpose(out, in_, ident), reads, [ps])

    def reduce_add(self, eng, out, in_, reads, writes):
        self.op(eng, lambda e: e.tensor_reduce(out=out, in_=in_, axis=AX.X, op=ALU.add), reads, writes)

    def memset(self, eng, buf, val, ap=None):
        a = buf.ap if ap is None else ap
        self.op(eng, lambda e: e.memset(a, val), [], [buf])


def tokmaj(X, t0, n):
    return X[t0:t0 + n, :].rearrange("(s p) c -> p s c", p=128)


def wview(W, c0, c1):
    return W[:, c0:c1].rearrange("(k p) c -> p k c", p=128)


DEBUG_IMM = False
DEBUG_TILES = None
DEBUG_PAD = 0


def build_program(dbg=False, stop_after=None):
    nc = bass.Bass("TRN2", target_bir_lowering=False)

    def din(name, shape, dt=F32):
        return nc.dram_tensor(name, shape, dt, kind="ExternalInput").ap()

    def dscr(name, shape, dt):
        return nc.dram_tensor(name, shape, dt, kind="ExternalOutput" if dbg else "Internal").ap()

    xin = din("xin", [NT, D])
    cvec = din("cvec", [128, 16])
    rows = din("rows", [L, 1, NR])
    pp_d = din("pp", [L, 128, NPP])
    cst_d = din("cst", [128, NCST])
    w_ada = din("w_ada", [L, D, 6 * D])
    w_in = din("w_in", [L, D, 7712])
    w_decay = din("w_decay", [L, 2, 16, 512])
    w_gla_o = din("w_gla_o", [L, D, D])
    w_conv_o = din("w_conv_o", [L, 512, D])
    w_pool_g = din("w_pool_g", [L, 4, 128, 128])
    w_pool_o = din("w_pool_o", [L, 512, D])
    w_out = din("w_out", [L, D, D])
    w_mlp1 = din("w_mlp1", [L, D, 4 * D])
    w_mlp2 = din("w_mlp2", [L, 4 * D, D])
    out = nc.dram_tensor("out", [NLAT, D], F32, kind="ExternalOutput").ap()

    modrow = dscr("modrow", [L, 2, 6, D], F32)
    xres = dscr("xres", [NT, D], F32)
    hT = dscr("hT", [128, 8, NT], BF16)
    h2T = dscr("h2T", [128, 8, NT], BF16)
    qT = dscr("qT", [128, 4, NT], BF16)
    kT = dscr("kT", [128, 4, NT], BF16)
    kk = dscr("kk", [NT, 512], BF16)
    vv = dscr("vv", [NT, D], BF16)
    sg = dscr("sg", [NT, D], BF16)
    lrT = dscr("lrT", [2, 16, NT], BF16)
    cvT = dscr("cvT", [128, 4, NT], BF16)
    plT = dscr("plT", [128, 4, NT], BF16)
    of_d = dscr("of", [NT, D], F32)
    ogT = dscr("ogT", [128, 8, NT], BF16)

    k = K(nc)
    ps = k.ps

    cstb = k.alloc("cstb", [128, 1024], BF16)
    onesr = k.alloc("onesr", [1, 128], BF16)
    if DEBUG_PAD:
        k.alloc("pad", [128, DEBUG_PAD // 4], F32)
    k.load(cstb, cst_d[:, 0:1024], queue="pool")
    k.memset("dve", onesr, 1.0)
    ident = cstb.ap[:, 0:128]
    tri = {"f": cstb.ap[:, 128:256], "b": cstb.ap[:, 256:384]}
    UU = {"f": cstb.ap[:, 384:512], "b": cstb.ap[:, 512:640]}
    msk = {"f": cstb.ap[:, 640:768], "b": cstb.ap[:, 768:896]}
    onesm = cstb.ap[:, 896:1024]

    def phase_adaln():
        k.phase_begin()
        cv = k.alloc("cv", [128, 16], F32)
        sc = k.alloc("sc", [128, 16], F32)
        k.load(cv, cvec)
        k.act(sc.ap, cv.ap, AF.Silu, [cv], [sc])
        wa = [k.alloc(f"wa{i}", [128, 8, 512], F32) for i in range(3)]
        for l in range(L):
            k.phase_begin()
            rw = k.alloc(f"rw{l}", [2, 10240], F32)
            k.load(rw, rows[l, 0, 0:10240].partition_broadcast(2))
            modr = k.alloc(f"modr{l}", [2, 6 * D], F32)
            for blk in range(12):
                wb = wa[blk % 3]
                k.load(wb, wview(w_ada[l], blk * 512, (blk + 1) * 512))
                pb = k.bank()
                for kc in range(8):
                    k.mm(pb, pb.ap[0:2, :], sc.ap[:, kc:16:8], wb.ap[:, kc, :], kc == 0, kc == 7, [sc, wb])
                k.tt("dve", modr.ap[:, blk * 512:(blk + 1) * 512], pb.ap[0:2, :],
                     rw.ap[:, blk * 512:(blk + 1) * 512], ALU.add, [pb, rw], [modr])
            m = modr.ap
            o6 = k.alloc(f"o6{l}", [2, 6 * D], F32)
            g = lambda i: rw.ap[:, 6144 + i * D: 6144 + (i + 1) * D]
            sl = lambda a, i: a[:, i * D:(i + 1) * D]
            k.stt("dve", sl(o6.ap, 0), sl(m, 1), 1.0, g(0), ALU.add, ALU.mult, [modr, rw], [o6])
            k.copy("dve", sl(o6.ap, 1), sl(m, 0), [modr], [o6])
            k.tt("dve", sl(o6.ap, 2), sl(m, 2), g(1), ALU.mult, [modr, rw], [o6])
            k.stt("dve", sl(o6.ap, 3), sl(m, 4), 1.0, g(2), ALU.add, ALU.mult, [modr, rw], [o6])
            k.copy("dve", sl(o6.ap, 4), sl(m, 3), [modr], [o6])
            k.tt("dve", sl(o6.ap, 5), sl(m, 5), g(3), ALU.mult, [modr, rw], [o6])
            k.store(modrow[l].rearrange("w a d -> w (a d)"), o6)
            k.phase_end()
        k.phase_end()

    def load_bc(name, l, which, idx):
        b = k.alloc(name, [128, D], F32)
        k.load(b, modrow[l, which, idx, :].partition_broadcast(128))
        return b

    def norm_mod_T(xs, nsub, A, B, hstage, tmp):
        ss, rt, rstd, junk, t1, hb = tmp
        k.memset("pool", ss, 0.0)
        for s in range(nsub):
            k.act(junk.ap, xs.ap[:, s, :], AF.Square, [xs], [junk, ss], accum_out=ss.ap[:, s:s + 1])
        k.act(rt.ap[:, 0:nsub], ss.ap[:, 0:nsub], AF.Sqrt, [ss], [rt], scale=1.0 / D, bias=EPS)
        k.recip("dve", rstd.ap[:, 0:nsub], rt.ap[:, 0:nsub], [rt], [rstd])
        for s in range(nsub):
            k.stt("dve", t1.ap, xs.ap[:, s, :], rstd.ap[:, s:s + 1], A.ap, ALU.mult, ALU.mult, [xs, rstd, A], [t1])
            k.tt("dve", hb.ap, t1.ap, B.ap, ALU.add, [t1, B], [hb])
            pb = k.bank()
            pbb = pb.ap.bitcast(BF16)
            for j in range(8):
                k.transpose(pb, pbb[:, j * 128:(j + 1) * 128], hb.ap[:, j * 128:(j + 1) * 128], ident, [hb, cstb])
            k.copy("act", hstage.ap[:, :, s * 128:(s + 1) * 128],
                   pbb[:, 0:1024].rearrange("p (j t) -> p j t", j=8), [pb], [hstage])

    def norm_tmp(nsub):
        return (k.alloc("ss", [128, 8], F32), k.alloc("rt", [128, 8], F32), k.alloc("rstd", [128, 8], F32),
                k.alloc("junk", [128, D], BF16), k.alloc("t1", [128, D], F32), k.alloc("hb", [128, D], BF16))

    def epi_gen(xs, nsub, G, A, B, hstage, tmp, ss1, rt1, rs1, store_x, store_h, do_norm):
        ss, rt, rstd, junk, t1, hb = tmp
        k.memset("pool", ss1, 0.0)
        for s in range(nsub):
            for hh in range(2):
                pb = ps[4 + 2 * s + hh]
                k.act(junk.ap[:, hh * 512:(hh + 1) * 512], pb.ap, AF.Square, [pb], [junk, ss1],
                      accum_out=ss1.ap[:, 2 * s + hh: 2 * s + hh + 1])
            k.tt("dve", rt1.ap[:, s:s + 1], ss1.ap[:, 2 * s:2 * s + 1], ss1.ap[:, 2 * s + 1:2 * s + 2], ALU.add, [ss1], [rt1])
        yield
        for s in range(nsub):
            k.act(rt1.ap[:, s:s + 1], rt1.ap[:, s:s + 1], AF.Sqrt, [rt1], [rt1], scale=1.0 / D, bias=EPS)
            k.recip("dve", rs1.ap[:, s:s + 1], rt1.ap[:, s:s + 1], [rt1], [rs1])
        yield
        for s in range(nsub):
            for hh in range(2):
                pb = ps[4 + 2 * s + hh]
                hs_ = slice(hh * 512, (hh + 1) * 512)
                k.stt("dve", t1.ap[:, hs_], pb.ap, rs1.ap[:, s:s + 1], G.ap[:, hs_], ALU.mult, ALU.mult, [pb, rs1, G], [t1])
            k.tt("dve", xs.ap[:, s, :], xs.ap[:, s, :], t1.ap, ALU.add, [xs, t1], [xs])
            yield
        store_x()
        if not do_norm:
            return
        k.memset("pool", ss, 0.0)
        for s in range(nsub):
            k.act(junk.ap, xs.ap[:, s, :], AF.Square, [xs], [junk, ss], accum_out=ss.ap[:, s:s + 1])
        yield
        k.act(rt.ap[:, 0:nsub], ss.ap[:, 0:nsub], AF.Sqrt, [ss], [rt], scale=1.0 / D, bias=EPS)
        k.recip("dve", rstd.ap[:, 0:nsub], rt.ap[:, 0:nsub], [rt], [rstd])
        yield
        for s in range(nsub):
            k.stt("dve", t1.ap, xs.ap[:, s, :], rstd.ap[:, s:s + 1], A.ap, ALU.mult, ALU.mult, [xs, rstd, A], [t1])
            k.tt("dve", hb.ap, t1.ap, B.ap, ALU.add, [t1, B], [hb])
            yield
            pb = k.bank()
            pbb = pb.ap.bitcast(BF16)
            for j in range(8):
                k.transpose(pb, pbb[:, j * 128:(j + 1) * 128], hb.ap[:, j * 128:(j + 1) * 128], ident, [hb, cstb])
            k.copy("act", hstage.ap[:, :, s * 128:(s + 1) * 128],
                   pbb[:, 0:1024].rearrange("p (j t) -> p j t", j=8), [pb], [hstage])
            yield
        store_h()

    def phase_prenorm0():
        k.phase_begin()
        AB = {}
        for w, nm in ((0, "lat"), (1, "ctx")):
            AB[nm] = (load_bc(f"A1{nm}", 0, w, 0), load_bc(f"B1{nm}", 0, w, 1))
        tmp = norm_tmp(4)
        xs = [k.alloc(f"xs{i}", [128, 4, D], F32) for i in range(2)]
        hs = [k.alloc(f"hs{i}", [128, 8, 512], BF16) for i in range(2)]
        for ti, (t0, n, kind) in enumerate(TILES512):
            nsub = n // 128
            x_ = xs[ti % 2]
            h_ = hs[ti % 2]
            k.load(x_, tokmaj(xin, t0, n), dst=x_.ap[:, 0:nsub, :])
            norm_mod_T(x_, nsub, AB[kind][0], AB[kind][1], h_, tmp)
            k.store(hT[:, :, t0:t0 + n], h_, src=h_.ap[:, :, 0:n])
        k.phase_end()

    def phase_inproj_gla(l):
        k.phase_begin()
        W = w_in[l]
        wq = k.alloc("wq", [128, 8, 512], BF16)
        wk = k.alloc("wk", [128, 8, 512], BF16)
        wv = k.alloc("wv", [128, 8, 1024], BF16)
        wg = k.alloc("wg", [128, 8, 1024], BF16)
        wl = k.alloc("wl", [128, 8, 32], BF16)
        k.load(wq, wview(W, C_Q, C_Q + 512), queue="pool")
        k.load(wk, wview(W, C_K, C_K + 512), queue="pool")
        k.load(wl, wview(W, C_LF, C_LF + 32), queue="pool")
        k.load(wv, wview(W, C_V, C_V + 1024), queue="pool")
        k.load(wg, wview(W, C_G, C_G + 1024), queue="pool")
        ggla = k.alloc("ggla", [128, D], F32)
        k.load(ggla, rows[l, 0, 10240:11264].partition_broadcast(128))
        hts = [k.alloc(f"ht{i}", [128, 8, 512], BF16) for i in range(2)]
        qs = [k.alloc(f"qs{i}", [128, 4, 512], BF16) for i in range(2)]
        ks_ = [k.alloc(f"ks{i}", [128, 4, 512], BF16) for i in range(2)]
        ls = [k.alloc(f"ls{i}", [32, 512], BF16) for i in range(2)]
        kks = [k.alloc(f"kks{i}", [128, 4, 512], BF16) for i in range(2)]
        vvs = [k.alloc(f"vvs{i}", [128, 4, D], BF16) for i in range(2)]
        sgs = [k.alloc(f"sgs{i}", [128, 4, D], BF16) for i in range(2)]
        stmp = [k.alloc(f"stmp{i}", [128, 512], F32) for i in range(2)]
        for ti, (t0, n, kind) in enumerate(TILES512):
            nsub = n // 128
            ht = hts[ti % 2]
            q_, k_, l_, kk_, vv_, sg_ = qs[ti % 2], ks_[ti % 2], ls[ti % 2], kks[ti % 2], vvs[ti % 2], sgs[ti % 2]
            k.load(ht, hT[:, :, t0:t0 + n], dst=ht.ap[:, :, 0:n])
            for j in range(4):
                pb = k.bank()
                for kc in range(8):
                    k.mm(pb, pb.ap[:, 0:n], wq.ap[:, kc, j * 128:(j + 1) * 128], ht.ap[:, kc, 0:n], kc == 0, kc == 7, [wq, ht])
                k.act(q_.ap[:, j, 0:n], pb.ap[:, 0:n], AF.Copy, [pb], [q_], scale=128.0 ** -0.5)
            for j in range(4):
                pb = k.bank()
                for kc in range(8):
                    k.mm(pb, pb.ap[:, 0:n], wk.ap[:, kc, j * 128:(j + 1) * 128], ht.ap[:, kc, 0:n], kc == 0, kc == 7, [wk, ht])
                k.copy("dve", k_.ap[:, j, 0:n], pb.ap[:, 0:n], [pb], [k_])
            pb = k.bank()
            for kc in range(8):
                k.mm(pb, pb.ap[0:32, 0:n], wl.ap[:, kc, :], ht.ap[:, kc, 0:n], kc == 0, kc == 7, [wl, ht])
            k.copy("dve", l_.ap[:, 0:n], pb.ap[0:32, 0:n], [pb], [l_])
            for s in range(nsub):
                hs_ = lambda kc: ht.ap[:, kc, s * 128:(s + 1) * 128]
                pb = k.bank()
                for kc in range(8):
                    k.mm(pb, pb.ap, hs_(kc), wk.ap[:, kc, :], kc == 0, kc == 7, [wk, ht])
                k.copy("act", kk_.ap[:, s, :], pb.ap, [pb], [kk_])
                for hh in range(2):
                    pb = k.bank()
                    for kc in range(8):
                        k.mm(pb, pb.ap, hs_(kc), wv.ap[:, kc, hh * 512:(hh + 1) * 512], kc == 0, kc == 7, [wv, ht])
                    k.copy("dve" if hh == 0 else "act", vv_.ap[:, s, hh * 512:(hh + 1) * 512], pb.ap, [pb], [vv_])
                for hh in range(2):
                    pb = k.bank()
                    for kc in range(8):
                        k.mm(pb, pb.ap, hs_(kc), wg.ap[:, kc, hh * 512:(hh + 1) * 512], kc == 0, kc == 7, [wg, ht])
                    st = stmp[hh]
                    k.act(st.ap, pb.ap, AF.Silu, [pb], [st])
                    k.tt("dve", sg_.ap[:, s, hh * 512:(hh + 1) * 512], st.ap, ggla.ap[:, hh * 512:(hh + 1) * 512],
                         ALU.mult, [st, ggla], [sg_])
            k.store(qT[:, :, t0:t0 + n], q_, src=q_.ap[:, :, 0:n])
            k.store(kT[:, :, t0:t0 + n], k_, src=k_.ap[:, :, 0:n])
            k.store(lrT[0, :, t0:t0 + n], l_, src=l_.ap[0:16, 0:n])
            k.store(lrT[1, :, t0:t0 + n], l_, src=l_.ap[16:32, 0:n])
            k.store(tokmaj(kk, t0, n), kk_, src=kk_.ap[:, 0:nsub, :])
            k.store(tokmaj(vv, t0, n), vv_, src=vv_.ap[:, 0:nsub, :])
            k.store(tokmaj(sg, t0, n), sg_, src=sg_.ap[:, 0:nsub, :])
        k.phase_end()

    def phase_conv_pool(l):
        k.phase_begin()
        W = w_in[l]
        wa_ = k.alloc("wa_", [128, 8, 512], BF16)
        wb_ = k.alloc("wb_", [128, 8, 512], BF16)
        wp_ = k.alloc("wp_", [128, 8, 512], BF16)
        wpg = k.alloc("wpg", [128, 4, 128], BF16)
        ppb = k.alloc("ppb", [128, NPP], F32)
        rcf = k.alloc("rcf", [128, 1280], F32)
        k.load(rcf, cst_d[:, 1024:2304])
        k.load(wa_, wview(W, C_PA, C_PA + 512), queue="pool")
        k.load(wb_, wview(W, C_PB, C_PB + 512), queue="pool")
        k.load(wp_, wview(W, C_PL, C_PL + 512), queue="pool")
        k.load(wpg, w_pool_g[l].rearrange("g c d -> c g d"), queue="pool")
        k.load(ppb, pp_d[l])
        wdw = lambda c, t: ppb.ap[:, 24 + c * 31 + t: 24 + c * 31 + t + 1]
        pcol = lambda base, c: ppb.ap[:, base + c: base + c + 1]
        B_DW, G_LN, B_LN, S_PL = 148, 152, 156, 160
        PLl = k.alloc("PLl", [128, 4, 80 * 64], BF16)
        PLc = k.alloc("PLc", [128, 4, 272], BF16)
        k.memset("pool", PLl, 0.0)
        k.memset("pool", PLc, 0.0)
        k.phase_begin()
        upls = [k.alloc(f"upl{i}", [128, 4, 8 * 94], BF16) for i in range(2)]
        upc = k.alloc("upc", [128, 4, 286], BF16)
        for u_ in upls:
            k.memset("dve", u_, 0.0)
        k.memset("dve", upc, 0.0)
        dg = k.alloc("dg", [128, 4, 31, 128], BF16)
        for c in range(4):
            k.tt("dve" if c % 2 == 0 else "pool", dg.ap[:, c, :, :], ident.unsqueeze(1).broadcast_to([128, 31, 128]),
                 ppb.ap[:, 24 + c * 31: 24 + (c + 1) * 31].unsqueeze(2).broadcast_to([128, 31, 128]), ALU.mult, [cstb, ppb], [dg])
        hts = [k.alloc(f"ht{i}", [128, 8, 512], BF16) for i in range(2)]
        sgm = [k.alloc(f"sgm{i}", [128, 512], F32) for i in range(2)]
        accs = [k.alloc(f"acc{i}", [128, 4, 512], F32) for i in range(2)]
        ybf = k.alloc("ybf", [128, 4, 512], BF16)
        ysq = k.alloc("ysq", [128, 4, 512], BF16)
        mean = k.alloc("mean", [128, 512], F32)
        m2 = k.alloc("m2", [128, 512], F32)
        var = k.alloc("var", [128, 512], F32)
        rs = k.alloc("rs", [128, 512], F32)
        tn = [k.alloc(f"tn{i}", [128, 512], F32) for i in range(2)]
        cvs = [k.alloc(f"cvs{i}", [128, 4, 512], BF16) for i in range(2)]

        def ln_gen(acc, cv_, t0, n):
            for c in range(4):
                k.copy("act", ybf.ap[:, c, 0:n], acc.ap[:, c, 0:n], [acc], [ybf])
                k.act(ysq.ap[:, c, 0:n], acc.ap[:, c, 0:n], AF.Square, [acc], [ysq])
            yield
            pm = k.bank()
            pq = k.bank()
            for c in range(4):
                k.mm(pm, pm.ap[:, 0:n], onesm, ybf.ap[:, c, 0:n], c == 0, c == 3, [cstb, ybf])
            for c in range(4):
                k.mm(pq, pq.ap[:, 0:n], onesm, ysq.ap[:, c, 0:n], c == 0, c == 3, [cstb, ysq])
            k.copy("act", mean.ap[:, 0:n], pm.ap[:, 0:n], [pm], [mean])
            k.act(m2.ap[:, 0:n], pm.ap[:, 0:n], AF.Square, [pm], [m2])
            k.tt("dve", var.ap[:, 0:n], pq.ap[:, 0:n], m2.ap[:, 0:n], ALU.subtract, [pq, m2], [var])
            k.tsmax("dve", var.ap[:, 0:n], var.ap[:, 0:n], 0.0, [var], [var])
            yield
            k.act(m2.ap[:, 0:n], var.ap[:, 0:n], AF.Sqrt, [var], [m2], bias=EPS)
            k.recip("dve", rs.ap[:, 0:n], m2.ap[:, 0:n], [m2], [rs])
            yield
            for c in range(4):
                t_ = tn[c % 2]
                k.tt("dve", t_.ap[:, 0:n], acc.ap[:, c, 0:n], mean.ap[:, 0:n], ALU.subtract, [acc, mean], [t_])
                k.tt("pool", t_.ap[:, 0:n], t_.ap[:, 0:n], rs.ap[:, 0:n], ALU.mult, [t_, rs], [t_])
                k.act(cv_.ap[:, c, 0:n], t_.ap[:, 0:n], AF.Silu, [t_, ppb], [cv_], scale=pcol(G_LN, c), bias=pcol(B_LN, c))
            k.store(cvT[:, :, t0:t0 + n], cv_, src=cv_.ap[:, :, 0:n])

        pending = None
        for ti, (t0, n, kind) in enumerate(TILES512):
            ht = hts[ti % 2]
            acc = accs[ti % 2]
            k.load(ht, hT[:, :, t0:t0 + n], dst=ht.ap[:, :, 0:n])
            lat = kind == "lat"
            up = upls[ti % 2] if lat else upc
            r0 = (t0 - NCTX) // 64
            for c in range(4):
                pa = k.bank()
                pb = k.bank()
                pl = k.bank()
                for (pbk, wsrc) in ((pa, wa_), (pb, wb_), (pl, wp_)):
                    for kc in range(8):
                        k.mm(pbk, pbk.ap[:, 0:n], wsrc.ap[:, kc, c * 128:(c + 1) * 128], ht.ap[:, kc, 0:n],
                             kc == 0, kc == 7, [wsrc, ht])
                sg_ = sgm[c % 2]
                k.act(sg_.ap[:, 0:n], pb.ap[:, 0:n], AF.Sigmoid, [pb], [sg_])
                if lat:
                    uint = up.ap[:, c, :].rearrange("p (r w) -> p r w", w=94)[:, :, 15:79]
                    k.tt("dve", uint, pa.ap.rearrange("p (r w) -> p r w", w=64), sg_.ap.rearrange("p (r w) -> p r w", w=64),
                         ALU.mult, [pa, sg_], [up])
                    k.copy("act", PLl.ap[:, c, (8 + r0) * 64:(8 + r0) * 64 + 512], pl.ap, [pl], [PLl])
                else:
                    k.tt("dve", up.ap[:, c, 15:15 + 256], pa.ap[:, 0:256], sg_.ap[:, 0:256], ALU.mult, [pa, sg_], [up])
                    k.copy("act", PLc.ap[:, c, 8:8 + 256], pl.ap[:, 0:256], [pl], [PLc])
                if pending is not None:
                    next(pending, None)
            if pending is not None:
                for _ in pending:
                    pass
                pending = None
            for c in range(4):
                pc = k.bank()
                for tap in range(31):
                    if lat:
                        src = up.ap[:, c, :].rearrange("p (r w) -> p r w", w=94)[:, :, tap:tap + 64]
                        dst = pc.ap.rearrange("p (r w) -> p r w", w=64)
                    else:
                        src = up.ap[:, c, tap:tap + 256]
                        dst = pc.ap[:, 0:256]
                    k.mm(pc, dst, dg.ap[:, c, tap, :], src, tap == 0, tap == 30, [dg, up])
                k.act(acc.ap[:, c, 0:n], pc.ap[:, 0:n], AF.Identity, [pc, ppb], [acc], bias=pcol(B_DW, c))
            pending = ln_gen(acc, cvs[ti % 2], t0, n)
        for _ in pending:
            pass

        k.phase_end()
        tA = k.alloc("tA", [128, 80 * 64], F32)
        tB = k.alloc("tB", [128, 80 * 64], F32)
        dTb = [k.alloc(f"dTb{i}", [128, 4096], BF16) for i in range(2)]
        pls = [k.alloc(f"pls{i}", [128, 4096], BF16) for i in range(2)]
        for (PL, R, Wd, rc0, tok0) in ((PLc, 256, 1, 256, 0), (PLl, 64, 64, 0, NCTX)):
            ntok = R * Wd
            for g in range(4):
                u = PL.ap[:, g, :]
                sl = lambda a, lo, hi: a[:, lo * Wd: hi * Wd]
                k.tt("dve", sl(tA.ap, 1, R + 16), sl(u, 0, R + 15), sl(u, 1, R + 16), ALU.add, [PL], [tA])
                cur = tA
                if g >= 1:
                    k.tt("dve", sl(tB.ap, 2, R + 15), sl(tA.ap, 1, R + 14), sl(tA.ap, 3, R + 16), ALU.add, [tA], [tB])
                    cur = tB
                if g >= 2:
                    k.tt("dve", sl(tA.ap, 4, R + 13), sl(tB.ap, 2, R + 11), sl(tB.ap, 6, R + 15), ALU.add, [tB], [tA])
                    cur = tA
                if g >= 3:
                    k.tt("dve", sl(tB.ap, 8, R + 9), sl(tA.ap, 4, R + 5), sl(tA.ap, 12, R + 13), ALU.add, [tA], [tB])
                    cur = tB
                oth = tB if cur is tA else tA
                S_ = sl(cur.ap, 8, 8 + R)
                rc = rcf.ap[:, rc0 + g * R: rc0 + (g + 1) * R]
                if Wd > 1:
                    S3 = S_.rearrange("p (r w) -> p r w", w=Wd)
                    O3 = sl(oth.ap, 8, 8 + R).rearrange("p (r w) -> p r w", w=Wd)
                    k.tt("dve", O3, S3, rc.unsqueeze(2).broadcast_to([128, R, Wd]), ALU.mult, [cur, rcf], [oth])
                else:
                    k.tt("dve", sl(oth.ap, 8, 8 + R), S_, rc, ALU.mult, [cur, rcf], [oth])
                db = dTb[g % 2]
                k.tt("pool", db.ap[:, 0:ntok], sl(oth.ap, 8, 8 + R), sl(u, 8, 8 + R), ALU.subtract, [oth, PL], [db])
                po = pls[g % 2]
                for c0 in range(0, ntok, 512):
                    nn = min(512, ntok - c0)
                    pb = k.bank()
                    k.mm(pb, pb.ap[:, 0:nn], wpg.ap[:, g, :], db.ap[:, c0:c0 + nn], True, True, [wpg, db])
                    k.act(po.ap[:, c0:c0 + nn], pb.ap[:, 0:nn], AF.Copy, [pb, ppb], [po], scale=pcol(S_PL, g))
                k.store(plT[:, g, tok0:tok0 + ntok], po, src=po.ap[:, 0:ntok])
        k.phase_end()

    def phase_gla(l, d):
        fwd = d == "f"
        k.phase_begin()
        di = 0 if fwd else 1
        wdec = k.alloc("wdec", [16, 512], BF16)
        bdec = k.alloc("bdec", [1, 512], BF16)
        k.load(wdec, w_decay[l, di], queue="pool")
        k.load(bdec, rows[l, :, 11264 + di * 512: 11264 + (di + 1) * 512], queue="pool")
        Sbs = [k.alloc(f"Sb{i}", [128, 4, 256], BF16) for i in range(2)]
        k.memset("pool", Sbs[0], 0.0)
        k.memset("pool", Sbs[1], 0.0)
        dgam = [k.alloc(f"dgam{i}", [128, 4, 128], BF16) for i in range(3)]
        NS = 2
        qTt = [k.alloc(f"qTt{i}", [128, 4, 512], BF16) for i in range(NS)]
        kTt = [k.alloc(f"kTt{i}", [128, 4, 512], BF16) for i in range(NS)]
        kkt = [k.alloc(f"kkt{i}", [128, 4, 512], BF16) for i in range(NS)]
        vvt = [k.alloc(f"vvt{i}", [128, 4, D], BF16) for i in range(NS)]
        lrt = [k.alloc(f"lrt{i}", [16, 512], BF16) for i in range(NS)]
        oft = [k.alloc(f"oft{i}", [128, 4, D], F32) for i in range(NS)]
        if not fwd:
            sgt = [k.alloc(f"sgt{i}", [128, 4, D], BF16) for i in range(NS)]
            ogs = [k.alloc(f"ogs{i}", [128, 8, 512], BF16) for i in range(NS)]
            otot = k.alloc("otot", [128, D], F32)
            sqj = k.alloc("sqj", [128, 256], BF16)
            ssq = k.alloc("ssq", [128, 4], F32)
            rt4 = k.alloc("rt4", [128, 4], F32)
            rs4 = k.alloc("rs4", [128, 4], F32)
            ogb = [k.alloc(f"og{i}", [128, D], BF16) for i in range(2)]
        et = [k.alloc(f"et{i}", [128, 512], F32) for i in range(2)]
        spb = [k.alloc(f"spb{i}", [128, 512], BF16) for i in range(2)]
        eq = [k.alloc(f"eq{i}", [128, 512], BF16) for i in range(2)]
        ek = [k.alloc(f"ek{i}", [128, 512], BF16) for i in range(2)]
        eke = [k.alloc(f"eke{i}", [128, 512], BF16) for i in range(2)]
        qi = [k.alloc(f"qi{i}", [128, 4, 128], BF16) for i in range(3)]
        ki = [k.alloc(f"ki{i}", [128, 4, 128], BF16) for i in range(3)]
        ke = [k.alloc(f"ke{i}", [128, 512], BF16) for i in range(3)]
        gam = [k.alloc(f"gam{i}", [128, 4], F32) for i in range(3)]
        am = k.alloc("am", [128, 4, 128], BF16)
        zt, bT, bk, at = ps[0], ps[1], ps[2], ps[3]

        tiles = list(TILES512)
        if not fwd:
            tiles = [tiles[0]] + tiles[:0:-1]
        seq = []
        for si, (t0, n, kind) in enumerate(tiles):
            cs_ = list(range(n // 128))
            if not fwd:
                cs_ = cs_[::-1]
            for ci, c in enumerate(cs_):
                seq.append((si, t0, n, c, ci == 0, ci == len(cs_) - 1))
        NCH = len(seq)

        def load_tile(si, t0, n):
            s_ = si % NS
            nsub = n // 128
            k.load(lrt[s_], lrT[di, :, t0:t0 + n], dst=lrt[s_].ap[:, 0:n])
            k.load(qTt[s_], qT[:, :, t0:t0 + n], dst=qTt[s_].ap[:, :, 0:n])
            k.load(kTt[s_], kT[:, :, t0:t0 + n], dst=kTt[s_].ap[:, :, 0:n])
            k.load(kkt[s_], tokmaj(kk, t0, n), dst=kkt[s_].ap[:, 0:nsub, :])
            k.load(vvt[s_], tokmaj(vv, t0, n), dst=vvt[s_].ap[:, 0:nsub, :])
            if not fwd:
                k.load(oft[s_], tokmaj(of_d, t0, n), dst=oft[s_].ap[:, 0:nsub, :])
                k.load(sgt[s_], tokmaj(sg, t0, n), dst=sgt[s_].ap[:, 0:nsub, :])

        def A_pe(i):
            si, t0, n, c, first, last = seq[i]
            s_ = si % NS
            if first:
                load_tile(si, t0, n)
            cs = slice(c * 128, (c + 1) * 128)
            k.mm(zt, zt.ap, lrt[s_].ap[:, cs], wdec.ap, True, False, [lrt[s_], wdec])
            k.mm(zt, zt.ap, onesr.ap, bdec.ap, False, True, [onesr, bdec])

        def A_act(i):
            p_ = i % 2
            k.act(et[p_].ap, zt.ap, AF.Exp, [zt], [et[p_]], scale=-1.0)
            k.act(spb[p_].ap, et[p_].ap, AF.Ln, [et[p_]], [spb[p_]], bias=1.0)

        def B_pe(i):
            p_ = i % 2
            for j in range(4):
                k.mm(bT, bT.ap[:, j * 128:(j + 1) * 128], spb[p_].ap[:, j * 128:(j + 1) * 128], tri[d], True, True, [spb[p_], cstb])
            k.mm(bk, bk.ap, UU[d], spb[p_].ap, True, True, [spb[p_], cstb])

        def B_act(i):
            p_ = i % 2
            k.act(eq[p_].ap, bT.ap, AF.Exp, [bT], [eq[p_]])
            k.act(ek[p_].ap, bT.ap, AF.Exp, [bT], [ek[p_]], scale=-1.0)
            col = 127 if fwd else 0
            k.act(gam[i % 3].ap, bT.ap.rearrange("p (j t) -> p j t", j=4)[:, :, col], AF.Exp, [bT], [gam[i % 3]])
            k.act(eke[p_].ap, bk.ap, AF.Exp, [bk], [eke[p_]])

        def B_vec(i):
            si, t0, n, c, first, last = seq[i]
            s_ = si % NS
            p_ = i % 2
            q_ = i % 3
            cs = slice(c * 128, (c + 1) * 128)
            k.tt("pool", qi[q_].ap, qTt[s_].ap[:, :, cs], eq[p_].ap.rearrange("p (j t) -> p j t", j=4), ALU.mult,
                 [qTt[s_], eq[p_]], [qi[q_]])
            k.tt("pool", ki[q_].ap, kTt[s_].ap[:, :, cs], ek[p_].ap.rearrange("p (j t) -> p j t", j=4), ALU.mult,
                 [kTt[s_], ek[p_]], [ki[q_]])
            k.tt("dve", ke[q_].ap, kkt[s_].ap[:, c, :], eke[p_].ap, ALU.mult, [kkt[s_], eke[p_]], [ke[q_]])
            k.tt("dve", dgam[q_].ap, ident.unsqueeze(1).broadcast_to([128, 4, 128]),
                 gam[q_].ap.unsqueeze(2).broadcast_to([128, 4, 128]), ALU.mult, [cstb, gam[q_]], [dgam[q_]])

        def C_att(i):
            q_ = i % 3
            for j in range(4):
                k.mm(at, at.ap[:, j * 128:(j + 1) * 128], ki[q_].ap[:, j, :], qi[q_].ap[:, j, :], True, True, [ki[q_], qi[q_]])

        def C_mask(i):
            k.tt("dve", am.ap, at.ap.rearrange("p (j t) -> p j t", j=4), msk[d].unsqueeze(1).broadcast_to([128, 4, 128]),
                 ALU.mult, [at, cstb], [am])

        def C_pe2(i):
            si, t0, n, c, first, last = seq[i]
            s_ = si % NS
            p_ = i % 3
            Sb = Sbs[i % 2]
            for j in range(4):
                pd = ps[6 + j // 2]
                dd = pd.ap[:, (j % 2) * 256:(j % 2 + 1) * 256]
                k.mm(pd, dd, dgam[p_].ap[:, j, :], Sb.ap[:, j, :], True, False, [dgam[p_], Sb])
                k.mm(pd, dd, ke[p_].ap[:, j * 128:(j + 1) * 128], vvt[s_].ap[:, c, j * 256:(j + 1) * 256], False, True,
                     [ke[p_], vvt[s_]])
            for j in range(4):
                po = ps[4 + j // 2]
                oo = po.ap[:, (j % 2) * 256:(j % 2 + 1) * 256]
                k.mm(po, oo, am.ap[:, j, :], vvt[s_].ap[:, c, j * 256:(j + 1) * 256], True, False, [am, vvt[s_]])
                k.mm(po, oo, qi[p_].ap[:, j, :], Sb.ap[:, j, :], False, True, [qi[p_], Sb])

        def S_copy(i):
            Sn = Sbs[(i + 1) % 2]
            k.copy("dve", Sn.ap.rearrange("p j v -> p (j v)").rearrange("p (a b) -> p a b", a=2), k.psum_t[:, 6:8, :],
                   [ps[6], ps[7]], [Sn])

        o2 = k.psum_t[:, 4:6, :]

        def D_evac(i):
            si, t0, n, c, first, last = seq[i]
            s_ = si % NS
            nsub = n // 128
            if fwd:
                k.copy("act", oft[s_].ap[:, c, :].rearrange("p (a b) -> p a b", a=2), o2, [ps[4], ps[5]], [oft[s_]])
                if last:
                    k.store(tokmaj(of_d, t0, n), oft[s_], src=oft[s_].ap[:, 0:nsub, :])
            else:
                k.tt("dve", otot.ap.rearrange("p (a b) -> p a b", a=2), o2, oft[s_].ap[:, c, :].rearrange("p (a b) -> p a b", a=2),
                     ALU.add, [ps[4], ps[5], oft[s_]], [otot])

        def D_epi_act(i):
            k.memset("pool", ssq, 0.0)
            for j in range(4):
                k.act(sqj.ap, otot.ap[:, j * 256:(j + 1) * 256], AF.Square, [otot], [sqj, ssq], accum_out=ssq.ap[:, j:j + 1])
            k.act(rt4.ap, ssq.ap, AF.Ln, [ssq], [rt4], scale=1.0 / 256, bias=EPS)
            k.act(rs4.ap, rt4.ap, AF.Exp, [rt4], [rs4], scale=-0.5)

        def D_epi_vec(i):
            si, t0, n, c, first, last = seq[i]
            s_ = si % NS
            for j in range(4):
                js = slice(j * 256, (j + 1) * 256)
                k.stt("dve", ogb[i % 2].ap[:, js], otot.ap[:, js], rs4.ap[:, j:j + 1], sgt[s_].ap[:, c, js],
                      ALU.mult, ALU.mult, [otot, rs4, sgt[s_]], [ogb[i % 2]])

        def E_all(i):
            si, t0, n, c, first, last = seq[i]
            s_ = si % NS
            pbb = at.ap.bitcast(BF16)
            og = ogb[i % 2]
            for j in range(8):
                k.transpose(at, pbb[:, j * 128:(j + 1) * 128], og.ap[:, j * 128:(j + 1) * 128], ident, [og, cstb])
            k.copy("act", ogs[s_].ap[:, :, c * 128:(c + 1) * 128], pbb[:, 0:1024].rearrange("p (j t) -> p j t", j=8),
                   [at], [ogs[s_]])
            if last:
                k.store(ogT[:, :, t0:t0 + n], ogs[s_], src=ogs[s_].ap[:, :, 0:n])

        ok = lambda i: 0 <= i < NCH
        if not fwd:
            k.memset("pool", ssq, 0.0)
        for s_ in range(NCH + 6):
            iA, iB, iC, iD, iE = s_, s_ - 1, s_ - 3, s_ - 4, s_ - 5
            if ok(iD):
                D_evac(iD)
            if ok(iC):
                C_att(iC)
                C_mask(iC)
            if ok(iB):
                B_pe(iB)
            if ok(iA):
                A_pe(iA)
            if ok(iC):
                C_pe2(iC)
                S_copy(iC)
            if ok(iB):
                B_act(iB)
            if ok(iA):
                A_act(iA)
            if ok(iB):
                B_vec(iB)
            if (not fwd) and ok(iD):
                D_epi_act(iD)
                D_epi_vec(iD)
            if (not fwd) and ok(iE):
                E_all(iE)
        k.phase_end()

    def phase_merge(l):
        k.phase_begin()
        TS = 256
        wgo = k.alloc("wgo", [128, 8, D], BF16)
        wco = k.alloc("wco", [128, 4, D], BF16)
        wpo = k.alloc("wpo", [128, 4, D], BF16)
        wgt = k.alloc("wgt", [128, 8, 3 * D], BF16)
        wo = k.alloc("wo", [128, 8, D], BF16)
        ppb = k.alloc("ppb", [128, 24], F32)
        k.load(wgt, wview(w_in[l], C_GT, C_GT + 3 * D), queue="pool")
        k.load(wgo, wview(w_gla_o[l], 0, D), queue="pool")
        k.load(wco, wview(w_conv_o[l], 0, D), queue="pool")
        k.load(wpo, wview(w_pool_o[l], 0, D), queue="pool")
        k.load(wo, wview(w_out[l], 0, D), queue="pool")
        k.load(ppb, pp_d[l, :, 0:24])
        G1 = k.alloc("G1", [128, D], F32)
        A2 = k.alloc("A2", [128, D], F32)
        B2 = k.alloc("B2", [128, D], F32)
        hts = [k.alloc(f"ht{i}", [128, 8, TS], BF16) for i in range(2)]
        ogs_ = [k.alloc(f"og_{i}", [128, 8, TS], BF16) for i in range(2)]
        cvs_ = [k.alloc(f"cv_{i}", [128, 4, TS], BF16) for i in range(2)]
        pls_ = [k.alloc(f"pl_{i}", [128, 4, TS], BF16) for i in range(2)]
        xss = [k.alloc(f"xs{i}", [128, 2, D], F32) for i in range(2)]
        h2ss = [k.alloc(f"h2s{i}", [128, 8, TS], BF16) for i in range(2)]
        gts = [k.alloc(f"gts{i}", [128, 3, TS], BF16) for i in range(2)]
        ysb = [k.alloc(f"ysb{i}", [128, 3, TS], BF16) for i in range(2)]
        mixs = [k.alloc(f"mix{i}", [128, 8, TS], BF16) for i in range(2)]
        tmp = norm_tmp(2)
        ss1 = k.alloc("ss1", [128, 8], F32)
        rt1 = k.alloc("rt1", [128, 4], F32)
        rs1 = k.alloc("rs1", [128, 4], F32)
        junk, t1 = tmp[3], tmp[4]
        xsrc = xin if l == 0 else xres
        state = {"kind": None}
        k.bank_set = [0, 1, 2, 3]

        def epilogue(t0, n, kind, xs, h2s):
            if kind != state["kind"]:
                w = 0 if kind == "lat" else 1
                k.load(G1, modrow[l, w, 2, :].partition_broadcast(128))
                k.load(A2, modrow[l, w, 3, :].partition_broadcast(128))
                k.load(B2, modrow[l, w, 4, :].partition_broadcast(128))
                state["kind"] = kind
            return epi_gen(xs, n // 128, G1, A2, B2, h2s, tmp, ss1, rt1, rs1,
                           lambda: k.store(tokmaj(xres, t0, n), xs),
                           lambda: k.store(h2T[:, :, t0:t0 + n], h2s), True)

        def drain(g):
            if g is not None:
                for _ in g:
                    pass

        pending = None
        for ti, (t0, n, kind) in enumerate(TILES256):
            nsub = n // 128
            ht, og_, cv_, pl_, xs, h2s, mix = (hts[ti % 2], ogs_[ti % 2], cvs_[ti % 2], pls_[ti % 2], xss[ti % 2],
                                               h2ss[ti % 2], mixs[ti % 2])
            k.load(ht, hT[:, :, t0:t0 + n])
            k.load(og_, ogT[:, :, t0:t0 + n])
            k.load(cv_, cvT[:, :, t0:t0 + n])
            k.load(pl_, plT[:, :, t0:t0 + n])
            k.load(xs, tokmaj(xsrc, t0, n))
            for j in range(8):
                gt = gts[j % 2]
                yb = ysb[j % 2]
                for br in range(3):
                    pb = k.bank()
                    for kc in range(8):
                        k.mm(pb, pb.ap[:, 0:n], wgt.ap[:, kc, br * D + j * 128: br * D + (j + 1) * 128], ht.ap[:, kc, :],
                             kc == 0, kc == 7, [wgt, ht])
                    k.act(gt.ap[:, br, :], pb.ap[:, 0:n], AF.Sigmoid, [pb, ppb], [gt], bias=ppb.ap[:, br * 8 + j: br * 8 + j + 1])
                for br, (wsrc, asrc, nk) in enumerate(((wgo, og_, 8), (wco, cv_, 4), (wpo, pl_, 4))):
                    pb = k.bank()
                    for kc in range(nk):
                        k.mm(pb, pb.ap[:, 0:n], wsrc.ap[:, kc, j * 128:(j + 1) * 128], asrc.ap[:, kc, :],
                             kc == 0, kc == nk - 1, [wsrc, asrc])
                    k.copy("dve", yb.ap[:, br, :], pb.ap[:, 0:n], [pb], [yb])
                k.tt("dve", yb.ap, yb.ap, gt.ap, ALU.mult, [yb, gt], [yb])
                k.tt("dve", yb.ap[:, 0, :], yb.ap[:, 0, :], yb.ap[:, 1, :], ALU.add, [yb], [yb])
                k.tt("dve", mix.ap[:, j, :], yb.ap[:, 0, :], yb.ap[:, 2, :], ALU.add, [yb], [mix])
                if pending is not None:
                    next(pending, None)
                    if j in (3, 6):
                        next(pending, None)
            drain(pending)
            for s in range(nsub):
                for hh in range(2):
                    pb = ps[4 + 2 * s + hh]
                    for kc in range(8):
                        k.mm(pb, pb.ap, mix.ap[:, kc, s * 128:(s + 1) * 128], wo.ap[:, kc, hh * 512:(hh + 1) * 512],
                             kc == 0, kc == 7, [mix, wo])
            pending = epilogue(t0, n, kind, xs, h2s)
        drain(pending)
        k.bank_set = list(range(8))
        k.phase_end()

    def phase_mlp(l):
        k.phase_begin()
        last = l == L - 1
        w1 = k.alloc("w1", [128, 8, 4 * D], BF16)
        w2 = k.alloc("w2", [128, 32, D], BF16)
        for q4 in range(4):
            k.dma("pool", w1.ap[:, :, q4 * D:(q4 + 1) * D], wview(w_mlp1[l], q4 * D, (q4 + 1) * D), [], [w1], w1)
        for q4 in range(4):
            k.dma("pool", w2.ap[:, q4 * 8:(q4 + 1) * 8, :], w_mlp2[l][q4 * D:(q4 + 1) * D, :].rearrange("(k p) c -> p k c", p=128),
                  [], [w2], w2)
        G2 = k.alloc("G2", [128, D], F32)
        if not last:
            A1 = k.alloc("A1", [128, D], F32)
            B1 = k.alloc("B1", [128, D], BF16)
        h2 = [k.alloc(f"h2_{i}", [128, 8, 256], BF16) for i in range(2)]
        hidA = k.alloc("hidA", [128, 24, 256], BF16)
        hidB = k.alloc("hidB", [128, 8, 256], BF16)
        hidf = lambda f: (hidA, hidA.ap[:, f, :]) if f < 24 else (hidB, hidB.ap[:, f - 24, :])
        xs = k.alloc("xs", [128, 2, D], F32)
        hs = None if last else hidB
        trl = [k.alloc(f"trl{i}", [128, 256], BF16) for i in range(2)]
        hb_t = k.alloc("hb", [128, D], BF16)
        tmp = (k.alloc("ss", [128, 8], F32), k.alloc("rt", [128, 8], F32), k.alloc("rstd", [128, 8], F32),
               hb_t, k.alloc("t1", [128, D], F32), hb_t)
        ss1 = k.alloc("ss1", [128, 8], F32)
        rt1 = k.alloc("rt1", [128, 4], F32)
        rs1 = k.alloc("rs1", [128, 4], F32)
        junk, t1 = tmp[3], tmp[4]
        state = {"kind": None}
        k.bank_set = [0, 1, 2, 3]

        def epilogue(t0, n, kind):
            if kind != state["kind"]:
                w = 0 if kind == "lat" else 1
                k.load(G2, modrow[l, w, 5, :].partition_broadcast(128))
                if not last:
                    k.load(A1, modrow[l + 1, w, 0, :].partition_broadcast(128))
                    k.load(B1, modrow[l + 1, w, 1, :].partition_broadcast(128), queue="pool")
                state["kind"] = kind
            if last:
                return epi_gen(xs, n // 128, G2, None, None, None, tmp, ss1, rt1, rs1,
                               lambda: k.store(tokmaj(out, t0 - NCTX, n), xs), None, False)
            return epi_gen(xs, n // 128, G2, A1, B1, hs, tmp, ss1, rt1, rs1,
                           lambda: k.store(tokmaj(xres, t0, n), xs),
                           lambda: k.store(hT[:, :, t0:t0 + n], hs), True)

        def drain(g):
            if g is not None:
                for _ in g:
                    pass

        pending = None
        for ti, (t0, n, kind) in enumerate(TILES256):
            if last and kind == "ctx":
                continue
            nsub = n // 128
            h_ = h2[ti % 2]
            k.load(h_, h2T[:, :, t0:t0 + n])
            if pending is None:
                k.load(xs, tokmaj(xres, t0, n))
            for f in range(32):
                pb = k.bank()
                for kc in range(8):
                    k.mm(pb, pb.ap[:, 0:n], w1.ap[:, kc, f * 128:(f + 1) * 128], h_.ap[:, kc, :], kc == 0, kc == 7, [w1, h_])
                tr_ = trl[f % 2]
                k.act(tr_.ap[:, 0:n], pb.ap[:, 0:n], AF.Relu, [pb], [tr_])
                hb_, ha_ = hidf(f)
                k.tt("dve" if f % 2 == 0 else "pool", ha_, tr_.ap[:, 0:n], tr_.ap[:, 0:n], ALU.mult, [tr_], [hb_])
                if pending is not None and f % 2 == 1 and f <= 21:
                    if next(pending, "done") == "done":
                        pending = None
                        k.load(xs, tokmaj(xres, t0, n))
            if pending is not None:
                drain(pending)
                pending = None
                k.load(xs, tokmaj(xres, t0, n))
            for s in range(nsub):
                for hh in range(2):
                    pb = ps[4 + 2 * s + hh]
                    for f in range(32):
                        hb_, ha_ = hidf(f)
                        k.mm(pb, pb.ap, ha_[:, s * 128:(s + 1) * 128], w2.ap[:, f, hh * 512:(hh + 1) * 512],
                             f == 0, f == 31, [hb_, w2])
            pending = epilogue(t0, n, kind)
        drain(pending)
        k.bank_set = list(range(8))
        k.phase_end()

    stages = []
    stages.append(("adaln", phase_adaln))
    stages.append(("prenorm0", phase_prenorm0))
    for l in range(L):
        stages.append((f"inproj{l}", lambda l=l: phase_inproj_gla(l)))
        stages.append((f"convpool{l}", lambda l=l: phase_conv_pool(l)))
        stages.append((f"glaf{l}", lambda l=l: phase_gla(l, "f")))
        stages.append((f"glab{l}", lambda l=l: phase_gla(l, "b")))
        stages.append((f"merge{l}", lambda l=l: phase_merge(l)))
        stages.append((f"mlp{l}", lambda l=l: phase_mlp(l)))
    for name, fn in stages:
        fn()
        if stop_after == name:
            break
    k.barrier()
    k.emit()
    nc._k_stats = (k.n_instr, k.nsem)
    return nc


def make_consts():
    cst = np.zeros((128, NCST), np.float32)
    s = np.arange(128)[:, None]
    t = np.arange(128)[None, :]
    cst[:, 0:128] = (s == t)
    cst[:, 128:256] = np.where(s <= t, -1.0 / 16, 0.0)
    cst[:, 256:384] = np.where(s >= t, -1.0 / 16, 0.0)
    cst[:, 384:512] = np.where(s > t, -1.0 / 16, 0.0)
    cst[:, 512:640] = np.where(s < t, -1.0 / 16, 0.0)
    cst[:, 640:768] = (s <= t)
    cst[:, 768:896] = (s >= t)
    cst[:, 896:1024] = 1.0 / 512
    for (Ln, off) in ((64, 1024), (256, 1280)):
        for g, w in enumerate((2, 4, 8, 16)):
            left = w // 2
            right = w - 1 - left
            tt_ = np.arange(Ln)
            lo = np.clip(tt_ - left, 0, Ln)
            hi = np.clip(tt_ + right + 1, 0, Ln)
            cst[:, off + g * Ln: off + (g + 1) * Ln] = (1.0 / (hi - lo).astype(np.float32))[None, :]
    return cst


def pack_inputs(inp):
    f = lambda a: np.ascontiguousarray(np.asarray(a, dtype=np.float32))
    rows = np.zeros((L, 1, NR), np.float32)
    pp = np.zeros((L, 128, NPP), np.float32)
    for l in range(L):
        rows[l, 0, 0:6144] = inp["b_ada"][l]
        rows[l, 0, 6144:7168] = inp["g_pre_mix"][l]
        rows[l, 0, 7168:8192] = inp["g_post_mix"][l]
        rows[l, 0, 8192:9216] = inp["g_pre_mlp"][l]
        rows[l, 0, 9216:10240] = inp["g_post_mlp"][l]
        rows[l, 0, 10240:11264] = inp["g_gla"][l]
        rows[l, 0, 11264:11776] = inp["b_decay"][l, 0]
        rows[l, 0, 11776:12288] = inp["b_decay"][l, 1]
        pp[l, :, 0:24] = np.asarray(inp["b_gate"][l]).reshape(24, 128).T
        wd = np.asarray(inp["w_dw"][l])
        pp[l, :, 24:148] = wd.T.reshape(4, 128, 31).transpose(1, 0, 2).reshape(128, 124)
        pp[l, :, 148:152] = np.asarray(inp["b_dw"][l]).reshape(4, 128).T
        pp[l, :, 152:156] = np.asarray(inp["g_conv_ln"][l]).reshape(4, 128).T
        pp[l, :, 156:160] = np.asarray(inp["b_conv_ln"][l]).reshape(4, 128).T
        pp[l, :, 160:164] = np.asarray(inp["s_pool"][l]).reshape(4, 128).T
    shared = {
        "rows": rows, "pp": pp, "cst": make_consts(),
        "w_ada": f(inp["w_ada"]), "w_in": f(inp["w_in"]), "w_decay": f(inp["w_decay"]),
        "w_gla_o": f(inp["w_gla_o"]), "w_conv_o": f(inp["w_conv_o"]), "w_pool_g": f(inp["w_pool_g"]),
        "w_pool_o": f(inp["w_pool_o"]), "w_out": f(inp["w_out"]), "w_mlp1": f(inp["w_mlp1"]), "w_mlp2": f(inp["w_mlp2"]),
    }
    maps = []
    B = inp["x"].shape[0]
    for b in range(B):
        m = dict(shared)
        m["xin"] = np.ascontiguousarray(np.concatenate([inp["ctx"][b], inp["x"][b]], axis=0).astype(np.float32))
        cv = np.zeros((128, 16), np.float32)
        cv[:, 0:8] = np.asarray(inp["c"][b]).reshape(8, 128).T
        cv[:, 8:16] = np.asarray(inp["c_ctx"]).reshape(8, 128).T
        m["cvec"] = cv
        maps.append(m)
    return maps


_NC_CACHE = {}


def kernel(**inputs):
    inp = {k_: np.asarray(v) for k_, v in inputs.items()}
    maps = pack_inputs(inp)
    if "nc" not in _NC_CACHE:
        _NC_CACHE["nc"] = build_program()
    nc = _NC_CACHE["nc"]
    res = run_bass_kernel_spmd(nc, maps, core_ids=list(range(8)))
    return np.stack([np.asarray(r["out"], dtype=np.float32) for r in res.results], axis=0)
```

```python
import numpy as np
import concourse.bass as bass
import concourse.mybir as mybir
from concourse.bass_utils import run_bass_kernel_spmd

F32 = mybir.dt.float32
BF16 = mybir.dt.bfloat16
AF = mybir.ActivationFunctionType
ALU = mybir.AluOpType
AX = mybir.AxisListType
DT_BYTES = {F32: 4, BF16: 2}
ENGS = ("pe", "act", "dve", "pool", "sp")

D = 1024
NCTX = 256
NLAT = 4096
NT = NCTX + NLAT
L = 2
EPS = 1e-6
NR = 12288
NPP = 164
NCST = 2304
C_Q, C_K, C_V, C_G, C_LF, C_LB, C_PA, C_PB, C_PL, C_GT = 0, 512, 1024, 2048, 3072, 3088, 3104, 3616, 4128, 4640
TILES512 = [(0, 256, "ctx")] + [(256 + 512 * i, 512, "lat") for i in range(8)]
TILES256 = [(256 * i, 256, "ctx" if i == 0 else "lat") for i in range(17)]


class Buf:
    __slots__ = ("name", "ap", "lw", "rd", "sem")

    def __init__(self, name, ap=None):
        self.name = name
        self.ap = ap
        self.lw = None
        self.rd = {}
        self.sem = {}


class Op:
    __slots__ = ("eng", "fn", "waits", "signal", "dma_sem", "seq")

    def __init__(self, eng, fn):
        self.eng = eng
        self.fn = fn
        self.waits = []
        self.signal = False
        self.dma_sem = None


class K:
    def __init__(self, nc, arena_bytes=186 * 1024):
        self.nc = nc
        self.ops = {e: [] for e in ENGS}
        self.waited = {e: {} for e in ENGS}
        self.arena_bytes = arena_bytes
        self.arena = nc.alloc_sbuf_tensor("arena", [128, arena_bytes // 2], BF16)
        self.top = 0
        self.psum_t = nc.alloc_psum_tensor("psum_all", [128, 8, 512], F32)
        self.ps = [Buf(f"ps{i}", self.psum_t[:, i, :]) for i in range(8)]
        self.esem = {e: nc.alloc_semaphore(f"sem_{e}") for e in ENGS if e != "sp"}
        self.dsems_free = {"hw": [], "sw": []}
        self.nsem = 0
        self.phase_bufs = []
        self.phase_marks = []
        self.all_dma_sems = []
        self._bank = 0
        self.bank_set = list(range(8))

    def bank(self):
        self._bank = (self._bank + 1) % len(self.bank_set)
        return self.ps[self.bank_set[self._bank]]

    def alloc(self, name, shape, dtype):
        free = int(np.prod(shape[1:]))
        nbytes = (free * DT_BYTES[dtype] + 63) // 64 * 64
        off = self.top
        self.top += nbytes
        assert self.top <= self.arena_bytes, f"SBUF arena overflow at {name}: {self.top}"
        ap = self.arena[0:shape[0], off // 2: off // 2 + free * DT_BYTES[dtype] // 2]
        if dtype != BF16:
            ap = ap.bitcast(dtype)
        if len(shape) > 2:
            names = " ".join(f"d{i}" for i in range(len(shape) - 1))
            kw = {f"d{i}": shape[i + 1] for i in range(len(shape) - 1)}
            ap = ap.rearrange(f"p ({names}) -> p {names}", **kw)
        b = Buf(name, ap)
        self.phase_bufs.append(b)
        return b

    def phase_begin(self):
        self.phase_marks.append((self.top, len(self.phase_bufs)))

    def phase_end(self):
        self.barrier()
        top, nb = self.phase_marks.pop()
        for b in self.phase_bufs[nb:]:
            for cls, sm in b.sem.items():
                self.dsems_free[cls].append(sm)
            b.sem = {}
        del self.phase_bufs[nb:]
        self.top = top

    def _getsem(self, buf, cls):
        if cls not in buf.sem:
            if self.dsems_free[cls]:
                buf.sem[cls] = self.dsems_free[cls].pop()
            else:
                h = self.nc.alloc_semaphore(f"dsem{cls}{self.nsem}")
                self.nsem += 1
                buf.sem[cls] = [h, 0]
                self.all_dma_sems.append(buf.sem[cls])
        return buf.sem[cls]

    def _add_wait(self, op, tok):
        e = op.eng
        if tok[0] == "eng":
            _, f, seq = tok
            if f == e and e == "pe":
                return
            key = ("eng", f)
            if self.waited[e].get(key, -1) >= seq:
                return
            self.waited[e][key] = seq
            self.ops[f][seq].signal = True
            op.waits.append(tok)
        else:
            _, sem, cnt = tok
            key = ("dma", id(sem))
            if self.waited[e].get(key, -1) >= cnt:
                return
            self.waited[e][key] = cnt
            op.waits.append(tok)

    def _record(self, op, reads, writes, tok):
        deps = []
        for b in reads:
            if b.lw is not None:
                deps.append(b.lw)
        for b in writes:
            if b.lw is not None and not (op.dma_sem is not None and b.lw[0] == "dma" and b.lw[1] is op.dma_sem):
                deps.append(b.lw)
            deps.extend(b.rd.values())
        for t in deps:
            self._add_wait(op, t)
        for b in writes:
            b.lw = tok
            b.rd = {}
        for b in reads:
            if b in writes:
                continue
            kk = (tok[0], tok[1]) if tok[0] == "eng" else ("dma", id(tok[1]))
            b.rd[kk] = tok

    def op(self, eng, fn, reads=(), writes=()):
        o = Op(eng, fn)
        o.seq = len(self.ops[eng])
        self.ops[eng].append(o)
        self._record(o, reads, writes, ("eng", eng, o.seq))
        return o

    def dma(self, queue, out_ap, in_ap, reads, writes, sbuf_side):
        sem = self._getsem(sbuf_side, "sw" if queue == "pool" else "hw")
        sem[1] += 16
        o = Op(queue, lambda e, o_=out_ap, i_=in_ap: e.dma_start(out=o_, in_=i_))
        o.dma_sem = sem
        o.seq = len(self.ops[queue])
        self.ops[queue].append(o)
        self._record(o, reads, writes, ("dma", sem, sem[1]))
        return o

    def load(self, buf, src, queue="sp", dst=None):
        return self.dma(queue, buf.ap if dst is None else dst, src, [], [buf], buf)

    def store(self, dst, buf, src=None, queue="sp"):
        return self.dma(queue, dst, buf.ap if src is None else src, [buf], [], buf)

    def barrier(self):
        lasts = {}
        for e in ENGS:
            if e == "sp":
                continue
            s = len(self.ops[e]) - 1
            while s >= 0 and (self.ops[e][s].fn is None or self.ops[e][s].dma_sem is not None):
                s -= 1
            lasts[e] = s
        for e in ENGS:
            o = Op(e, None)
            o.seq = len(self.ops[e])
            for f, s in lasts.items():
                if s >= 0:
                    self._add_wait(o, ("eng", f, s))
            for sem in self.all_dma_sems:
                if sem[1] > 0:
                    self._add_wait(o, ("dma", sem, sem[1]))
            if o.waits:
                self.ops[e].append(o)

    def emit(self):
        nc = self.nc
        cum = {}
        for e in ENGS:
            if e == "sp":
                continue
            c = 0
            arr = []
            for o in self.ops[e]:
                if o.signal:
                    c += 1
                arr.append(c)
            cum[e] = arr
        self.n_instr = {e: len(self.ops[e]) for e in ENGS}

        def run(e, eng):
            for o in self.ops[e]:
                for t in o.waits:
                    if t[0] == "eng":
                        eng.wait_ge(self.esem[t[1]], cum[t[1]][t[2]])
                    else:
                        eng.wait_ge(t[1][0], t[2])
                if o.fn is None:
                    continue
                ins = o.fn(eng)
                if o.dma_sem is not None:
                    ins.then_inc(o.dma_sem[0], 16)
                elif o.signal:
                    ins.then_inc(self.esem[e], 1)

        with nc.Block() as block:
            @block.tensor
            def _(eng):
                run("pe", eng)

            @block.scalar
            def _(eng):
                run("act", eng)

            @block.vector
            def _(eng):
                run("dve", eng)

            @block.gpsimd
            def _(eng):
                run("pool", eng)

            @block.sync
            def _(eng):
                run("sp", eng)

    def mm(self, ps, out_ap, lhsT, rhs, start, stop, reads):
        self.op("pe", lambda e: e.matmul(out_ap, lhsT, rhs, start=start, stop=stop), reads, [ps])

    def act(self, out_ap, in_ap, func, reads, writes, **kw):
        self.op("act", lambda e: e.activation(out=out_ap, in_=in_ap, func=func, **kw), reads, writes)

    def tt(self, eng, out, in0, in1, op, reads, writes):
        self.op(eng, lambda e: e.tensor_tensor(out=out, in0=in0, in1=in1, op=op), reads, writes)

    def stt(self, eng, out, in0, scalar, in1, op0, op1, reads, writes):
        self.op(eng, lambda e: e.scalar_tensor_tensor(out=out, in0=in0, scalar=scalar, in1=in1, op0=op0, op1=op1),
                reads, writes)

    def ts(self, eng, out, in0, s1, s2, op0, op1, reads, writes):
        self.op(eng, lambda e: e.tensor_scalar(out=out, in0=in0, scalar1=s1, scalar2=s2, op0=op0, op1=op1),
                reads, writes)

    def copy(self, eng, out, in_, reads, writes):
        if eng == "act":
            self.op(eng, lambda e: e.activation(out=out, in_=in_, func=AF.Copy), reads, writes)
        else:
            self.op(eng, lambda e: e.tensor_copy(out=out, in_=in_), reads, writes)

    def recip(self, eng, out, in_, reads, writes):
        self.op(eng, lambda e: e.reciprocal(out=out, in_=in_), reads, writes)

    def tsmax(self, eng, out, in0, val, reads, writes):
        self.op(eng, lambda e: e.tensor_scalar_max(out=out, in0=in0, scalar1=val), reads, writes)

    def transpose(self, ps, out, in_, ident, reads):
        self.op("pe", lambda e: e.transpose(out, in_, ident), reads, [ps])

    def reduce_add(self, eng, out, in_, reads, writes):
        self.op(eng, lambda e: e.tensor_reduce(out=out, in_=in_, axis=AX.X, op=ALU.add), reads, writes)

    def memset(self, eng, buf, val, ap=None):
        a = buf.ap if ap is None else ap
        self.op(eng, lambda e: e.memset(a, val), [], [buf])


def tokmaj(X, t0, n):
    return X[t0:t0 + n, :].rearrange("(s p) c -> p s c", p=128)


def wview(W, c0, c1):
    return W[:, c0:c1].rearrange("(k p) c -> p k c", p=128)


DEBUG_IMM = False
DEBUG_TILES = None
DEBUG_PAD = 0


def build_program(dbg=False, stop_after=None):
    nc = bass.Bass("TRN2", target_bir_lowering=False)

    def din(name, shape, dt=F32):
        return nc.dram_tensor(name, shape, dt, kind="ExternalInput").ap()

    def dscr(name, shape, dt):
        return nc.dram_tensor(name, shape, dt, kind="ExternalOutput" if dbg else "Internal").ap()

    xin = din("xin", [NT, D])
    cvec = din("cvec", [128, 16])
    rows = din("rows", [L, 1, NR])
    pp_d = din("pp", [L, 128, NPP])
    cst_d = din("cst", [128, NCST])
    w_ada = din("w_ada", [L, D, 6 * D])
    w_in = din("w_in", [L, D, 7712])
    w_decay = din("w_decay", [L, 2, 16, 512])
    w_gla_o = din("w_gla_o", [L, D, D])
    w_conv_o = din("w_conv_o", [L, 512, D])
    w_pool_g = din("w_pool_g", [L, 4, 128, 128])
    w_pool_o = din("w_pool_o", [L, 512, D])
    w_out = din("w_out", [L, D, D])
    w_mlp1 = din("w_mlp1", [L, D, 4 * D])
    w_mlp2 = din("w_mlp2", [L, 4 * D, D])
    out = nc.dram_tensor("out", [NLAT, D], F32, kind="ExternalOutput").ap()

    modrow = dscr("modrow", [L, 2, 6, D], F32)
    xres = dscr("xres", [NT, D], F32)
    hT = dscr("hT", [128, 8, NT], BF16)
    h2T = dscr("h2T", [128, 8, NT], BF16)
    qT = dscr("qT", [128, 4, NT], BF16)
    kT = dscr("kT", [128, 4, NT], BF16)
    kk = dscr("kk", [NT, 512], BF16)
    vv = dscr("vv", [NT, D], BF16)
    sg = dscr("sg", [NT, D], BF16)
    lrT = dscr("lrT", [2, 16, NT], BF16)
    cvT = dscr("cvT", [128, 4, NT], BF16)
    plT = dscr("plT", [128, 4, NT], BF16)
    of_d = dscr("of", [NT, D], F32)
    ogT = dscr("ogT", [128, 8, NT], BF16)

    k = K(nc)
    ps = k.ps

    cstb = k.alloc("cstb", [128, 1024], BF16)
    onesr = k.alloc("onesr", [1, 128], BF16)
    if DEBUG_PAD:
        k.alloc("pad", [128, DEBUG_PAD // 4], F32)
    k.load(cstb, cst_d[:, 0:1024], queue="pool")
    k.memset("dve", onesr, 1.0)
    ident = cstb.ap[:, 0:128]
    tri = {"f": cstb.ap[:, 128:256], "b": cstb.ap[:, 256:384]}
    UU = {"f": cstb.ap[:, 384:512], "b": cstb.ap[:, 512:640]}
    msk = {"f": cstb.ap[:, 640:768], "b": cstb.ap[:, 768:896]}
    onesm = cstb.ap[:, 896:1024]

    def phase_adaln():
        k.phase_begin()
        cv = k.alloc("cv", [128, 16], F32)
        sc = k.alloc("sc", [128, 16], F32)
        k.load(cv, cvec)
        k.act(sc.ap, cv.ap, AF.Silu, [cv], [sc])
        wa = [k.alloc(f"wa{i}", [128, 8, 512], F32) for i in range(3)]
        for l in range(L):
            k.phase_begin()
            rw = k.alloc(f"rw{l}", [2, 10240], F32)
            k.load(rw, rows[l, 0, 0:10240].partition_broadcast(2))
            modr = k.alloc(f"modr{l}", [2, 6 * D], F32)
            for blk in range(12):
                wb = wa[blk % 3]
                k.load(wb, wview(w_ada[l], blk * 512, (blk + 1) * 512))
                pb = k.bank()
                for kc in range(8):
                    k.mm(pb, pb.ap[0:2, :], sc.ap[:, kc:16:8], wb.ap[:, kc, :], kc == 0, kc == 7, [sc, wb])
                k.tt("dve", modr.ap[:, blk * 512:(blk + 1) * 512], pb.ap[0:2, :],
                     rw.ap[:, blk * 512:(blk + 1) * 512], ALU.add, [pb, rw], [modr])
            m = modr.ap
            o6 = k.alloc(f"o6{l}", [2, 6 * D], F32)
            g = lambda i: rw.ap[:, 6144 + i * D: 6144 + (i + 1) * D]
            sl = lambda a, i: a[:, i * D:(i + 1) * D]
            k.stt("dve", sl(o6.ap, 0), sl(m, 1), 1.0, g(0), ALU.add, ALU.mult, [modr, rw], [o6])
            k.copy("dve", sl(o6.ap, 1), sl(m, 0), [modr], [o6])
            k.tt("dve", sl(o6.ap, 2), sl(m, 2), g(1), ALU.mult, [modr, rw], [o6])
            k.stt("dve", sl(o6.ap, 3), sl(m, 4), 1.0, g(2), ALU.add, ALU.mult, [modr, rw], [o6])
            k.copy("dve", sl(o6.ap, 4), sl(m, 3), [modr], [o6])
            k.tt("dve", sl(o6.ap, 5), sl(m, 5), g(3), ALU.mult, [modr, rw], [o6])
            k.store(modrow[l].rearrange("w a d -> w (a d)"), o6)
            k.phase_end()
        k.phase_end()

    def load_bc(name, l, which, idx):
        b = k.alloc(name, [128, D], F32)
        k.load(b, modrow[l, which, idx, :].partition_broadcast(128))
        return b

    def norm_mod_T(xs, nsub, A, B, hstage, tmp):
        ss, rt, rstd, junk, t1, hb = tmp
        k.memset("pool", ss, 0.0)
        for s in range(nsub):
            k.act(junk.ap, xs.ap[:, s, :], AF.Square, [xs], [junk, ss], accum_out=ss.ap[:, s:s + 1])
        k.act(rt.ap[:, 0:nsub], ss.ap[:, 0:nsub], AF.Sqrt, [ss], [rt], scale=1.0 / D, bias=EPS)
        k.recip("dve", rstd.ap[:, 0:nsub], rt.ap[:, 0:nsub], [rt], [rstd])
        for s in range(nsub):
            k.stt("dve", t1.ap, xs.ap[:, s, :], rstd.ap[:, s:s + 1], A.ap, ALU.mult, ALU.mult, [xs, rstd, A], [t1])
            k.tt("dve", hb.ap, t1.ap, B.ap, ALU.add, [t1, B], [hb])
            pb = k.bank()
            pbb = pb.ap.bitcast(BF16)
            for j in range(8):
                k.transpose(pb, pbb[:, j * 128:(j + 1) * 128], hb.ap[:, j * 128:(j + 1) * 128], ident, [hb, cstb])
            k.copy("act", hstage.ap[:, :, s * 128:(s + 1) * 128],
                   pbb[:, 0:1024].rearrange("p (j t) -> p j t", j=8), [pb], [hstage])

    def norm_tmp(nsub):
        return (k.alloc("ss", [128, 8], F32), k.alloc("rt", [128, 8], F32), k.alloc("rstd", [128, 8], F32),
                k.alloc("junk", [128, D], BF16), k.alloc("t1", [128, D], F32), k.alloc("hb", [128, D], BF16))

    def epi_gen(xs, nsub, G, A, B, hstage, tmp, ss1, rt1, rs1, store_x, store_h, do_norm):
        ss, rt, rstd, junk, t1, hb = tmp
        k.memset("pool", ss1, 0.0)
        for s in range(nsub):
            for hh in range(2):
                pb = ps[4 + 2 * s + hh]
                k.act(junk.ap[:, hh * 512:(hh + 1) * 512], pb.ap, AF.Square, [pb], [junk, ss1],
                      accum_out=ss1.ap[:, 2 * s + hh: 2 * s + hh + 1])
            k.tt("dve", rt1.ap[:, s:s + 1], ss1.ap[:, 2 * s:2 * s + 1], ss1.ap[:, 2 * s + 1:2 * s + 2], ALU.add, [ss1], [rt1])
        yield
        for s in range(nsub):
            k.act(rt1.ap[:, s:s + 1], rt1.ap[:, s:s + 1], AF.Sqrt, [rt1], [rt1], scale=1.0 / D, bias=EPS)
            k.recip("dve", rs1.ap[:, s:s + 1], rt1.ap[:, s:s + 1], [rt1], [rs1])
        yield
        for s in range(nsub):
            for hh in range(2):
                pb = ps[4 + 2 * s + hh]
                hs_ = slice(hh * 512, (hh + 1) * 512)
                k.stt("dve", t1.ap[:, hs_], pb.ap, rs1.ap[:, s:s + 1], G.ap[:, hs_], ALU.mult, ALU.mult, [pb, rs1, G], [t1])
            k.tt("dve", xs.ap[:, s, :], xs.ap[:, s, :], t1.ap, ALU.add, [xs, t1], [xs])
            yield
        store_x()
        if not do_norm:
            return
        k.memset("pool", ss, 0.0)
        for s in range(nsub):
            k.act(junk.ap, xs.ap[:, s, :], AF.Square, [xs], [junk, ss], accum_out=ss.ap[:, s:s + 1])
        yield
        k.act(rt.ap[:, 0:nsub], ss.ap[:, 0:nsub], AF.Sqrt, [ss], [rt], scale=1.0 / D, bias=EPS)
        k.recip("dve", rstd.ap[:, 0:nsub], rt.ap[:, 0:nsub], [rt], [rstd])
        yield
        for s in range(nsub):
            k.stt("dve", t1.ap, xs.ap[:, s, :], rstd.ap[:, s:s + 1], A.ap, ALU.mult, ALU.mult, [xs, rstd, A], [t1])
            k.tt("dve", hb.ap, t1.ap, B.ap, ALU.add, [t1, B], [hb])
            yield
            pb = k.bank()
            pbb = pb.ap.bitcast(BF16)
            for j in range(8):
                k.transpose(pb, pbb[:, j * 128:(j + 1) * 128], hb.ap[:, j * 128:(j + 1) * 128], ident, [hb, cstb])
            k.copy("act", hstage.ap[:, :, s * 128:(s + 1) * 128],
                   pbb[:, 0:1024].rearrange("p (j t) -> p j t", j=8), [pb], [hstage])
            yield
        store_h()

    def phase_prenorm0():
        k.phase_begin()
        AB = {}
        for w, nm in ((0, "lat"), (1, "ctx")):
            AB[nm] = (load_bc(f"A1{nm}", 0, w, 0), load_bc(f"B1{nm}", 0, w, 1))
        sss = [k.alloc(f"ss{i}", [128, 8], F32) for i in range(2)]
        rts = [k.alloc(f"rt{i}", [128, 8], F32) for i in range(2)]
        rstds = [k.alloc(f"rstd{i}", [128, 8], F32) for i in range(2)]
        junk = k.alloc("junk", [128, D], BF16)
        t1s = [k.alloc(f"t1{i}", [128, D], F32) for i in range(2)]
        hbs = [k.alloc(f"hb{i}", [128, D], BF16) for i in range(2)]
        xs = [k.alloc(f"xs{i}", [128, 4, D], F32) for i in range(2)]
        hs = [k.alloc(f"hs{i}", [128, 8, 512], BF16) for i in range(2)]

        def part1(ti):
            t0, n, kind = TILES512[ti]
            nsub = n // 128
            x_, ss, rt, rstd = xs[ti % 2], sss[ti % 2], rts[ti % 2], rstds[ti % 2]
            k.load(x_, tokmaj(xin, t0, n), dst=x_.ap[:, 0:nsub, :])
            k.memset("pool", ss, 0.0)
            for s in range(nsub):
                k.act(junk.ap, x_.ap[:, s, :], AF.Square, [x_], [junk, ss], accum_out=ss.ap[:, s:s + 1])
            k.act(rt.ap[:, 0:nsub], ss.ap[:, 0:nsub], AF.Sqrt, [ss], [rt], scale=1.0 / D, bias=EPS)
            k.recip("dve", rstd.ap[:, 0:nsub], rt.ap[:, 0:nsub], [rt], [rstd])

        def part2(ti):
            t0, n, kind = TILES512[ti]
            nsub = n // 128
            x_, rstd, h_ = xs[ti % 2], rstds[ti % 2], hs[ti % 2]
            A, B = AB[kind]
            for s in range(nsub):
                t1, hb = t1s[s % 2], hbs[s % 2]
                k.stt("dve", t1.ap, x_.ap[:, s, :], rstd.ap[:, s:s + 1], A.ap, ALU.mult, ALU.mult, [x_, rstd, A], [t1])
                k.tt("pool", hb.ap, t1.ap, B.ap, ALU.add, [t1, B], [hb])
                pb = k.bank()
                pbb = pb.ap.bitcast(BF16)
                for j in range(8):
                    k.transpose(pb, pbb[:, j * 128:(j + 1) * 128], hb.ap[:, j * 128:(j + 1) * 128], ident, [hb, cstb])
                k.copy("act", h_.ap[:, :, s * 128:(s + 1) * 128],
                       pbb[:, 0:1024].rearrange("p (j t) -> p j t", j=8), [pb], [h_])
            k.store(hT[:, :, t0:t0 + n], h_, src=h_.ap[:, :, 0:n])

        part1(0)
        for ti in range(len(TILES512)):
            if ti + 1 < len(TILES512):
                part1(ti + 1)
            part2(ti)
        k.phase_end()

    def phase_inproj_gla(l):
        k.phase_begin()
        W = w_in[l]
        wq = k.alloc("wq", [128, 8, 512], BF16)
        wk = k.alloc("wk", [128, 8, 512], BF16)
        wv = k.alloc("wv", [128, 8, 1024], BF16)
        wg = k.alloc("wg", [128, 8, 1024], BF16)
        wl = k.alloc("wl", [128, 8, 32], BF16)
        k.load(wq, wview(W, C_Q, C_Q + 512), queue="pool")
        k.load(wk, wview(W, C_K, C_K + 512), queue="pool")
        k.load(wl, wview(W, C_LF, C_LF + 32), queue="pool")
        k.load(wv, wview(W, C_V, C_V + 1024), queue="pool")
        k.load(wg, wview(W, C_G, C_G + 1024), queue="pool")
        ggla = k.alloc("ggla", [128, D], F32)
        k.load(ggla, rows[l, 0, 10240:11264].partition_broadcast(128))
        hts = [k.alloc(f"ht{i}", [128, 8, 512], BF16) for i in range(2)]
        qs = [k.alloc(f"qs{i}", [128, 4, 512], BF16) for i in range(2)]
        ks_ = [k.alloc(f"ks{i}", [128, 4, 512], BF16) for i in range(2)]
        ls = [k.alloc(f"ls{i}", [32, 512], BF16) for i in range(2)]
        kks = [k.alloc(f"kks{i}", [128, 4, 512], BF16) for i in range(2)]
        vvs = [k.alloc(f"vvs{i}", [128, 4, D], BF16) for i in range(2)]
        sgs = [k.alloc(f"sgs{i}", [128, 4, D], BF16) for i in range(2)]
        stmp = [k.alloc(f"stmp{i}", [128, 512], F32) for i in range(2)]
        for ti, (t0, n, kind) in enumerate(TILES512):
            nsub = n // 128
            ht = hts[ti % 2]
            q_, k_, l_, kk_, vv_, sg_ = qs[ti % 2], ks_[ti % 2], ls[ti % 2], kks[ti % 2], vvs[ti % 2], sgs[ti % 2]
            k.load(ht, hT[:, :, t0:t0 + n], dst=ht.ap[:, :, 0:n])
            for j in range(4):
                pb = k.bank()
                for kc in range(8):
                    k.mm(pb, pb.ap[:, 0:n], wq.ap[:, kc, j * 128:(j + 1) * 128], ht.ap[:, kc, 0:n], kc == 0, kc == 7, [wq, ht])
                k.act(q_.ap[:, j, 0:n], pb.ap[:, 0:n], AF.Copy, [pb], [q_], scale=128.0 ** -0.5)
            for j in range(4):
                pb = k.bank()
                for kc in range(8):
                    k.mm(pb, pb.ap[:, 0:n], wk.ap[:, kc, j * 128:(j + 1) * 128], ht.ap[:, kc, 0:n], kc == 0, kc == 7, [wk, ht])
                k.copy("dve", k_.ap[:, j, 0:n], pb.ap[:, 0:n], [pb], [k_])
            pb = k.bank()
            for kc in range(8):
                k.mm(pb, pb.ap[0:32, 0:n], wl.ap[:, kc, :], ht.ap[:, kc, 0:n], kc == 0, kc == 7, [wl, ht])
            k.copy("dve", l_.ap[:, 0:n], pb.ap[0:32, 0:n], [pb], [l_])
            for s in range(nsub):
                hs_ = lambda kc: ht.ap[:, kc, s * 128:(s + 1) * 128]
                pb = k.bank()
                for kc in range(8):
                    k.mm(pb, pb.ap, hs_(kc), wk.ap[:, kc, :], kc == 0, kc == 7, [wk, ht])
                k.copy("act", kk_.ap[:, s, :], pb.ap, [pb], [kk_])
                for hh in range(2):
                    pb = k.bank()
                    for kc in range(8):
                        k.mm(pb, pb.ap, hs_(kc), wv.ap[:, kc, hh * 512:(hh + 1) * 512], kc == 0, kc == 7, [wv, ht])
                    k.copy("dve" if hh == 0 else "act", vv_.ap[:, s, hh * 512:(hh + 1) * 512], pb.ap, [pb], [vv_])
                for hh in range(2):
                    pb = k.bank()
                    for kc in range(8):
                        k.mm(pb, pb.ap, hs_(kc), wg.ap[:, kc, hh * 512:(hh + 1) * 512], kc == 0, kc == 7, [wg, ht])
                    st = stmp[hh]
                    k.act(st.ap, pb.ap, AF.Silu, [pb], [st])
                    k.tt("dve", sg_.ap[:, s, hh * 512:(hh + 1) * 512], st.ap, ggla.ap[:, hh * 512:(hh + 1) * 512],
                         ALU.mult, [st, ggla], [sg_])
            k.store(qT[:, :, t0:t0 + n], q_, src=q_.ap[:, :, 0:n])
            k.store(kT[:, :, t0:t0 + n], k_, src=k_.ap[:, :, 0:n])
            k.store(lrT[0, :, t0:t0 + n], l_, src=l_.ap[0:16, 0:n])
            k.store(lrT[1, :, t0:t0 + n], l_, src=l_.ap[16:32, 0:n])
            k.store(tokmaj(kk, t0, n), kk_, src=kk_.ap[:, 0:nsub, :])
            k.store(tokmaj(vv, t0, n), vv_, src=vv_.ap[:, 0:nsub, :])
            k.store(tokmaj(sg, t0, n), sg_, src=sg_.ap[:, 0:nsub, :])
        k.phase_end()

    def phase_conv_pool(l):
        k.phase_begin()
        W = w_in[l]
        wa_ = k.alloc("wa_", [128, 8, 512], BF16)
        wb_ = k.alloc("wb_", [128, 8, 512], BF16)
        wp_ = k.alloc("wp_", [128, 8, 512], BF16)
        wpg = k.alloc("wpg", [128, 4, 128], BF16)
        ppb = k.alloc("ppb", [128, NPP], F32)
        rcf = k.alloc("rcf", [128, 1280], F32)
        k.load(rcf, cst_d[:, 1024:2304])
        k.load(wa_, wview(W, C_PA, C_PA + 512), queue="pool")
        k.load(wb_, wview(W, C_PB, C_PB + 512), queue="pool")
        k.load(wp_, wview(W, C_PL, C_PL + 512), queue="pool")
        k.load(wpg, w_pool_g[l].rearrange("g c d -> c g d"), queue="pool")
        k.load(ppb, pp_d[l])
        wdw = lambda c, t: ppb.ap[:, 24 + c * 31 + t: 24 + c * 31 + t + 1]
        pcol = lambda base, c: ppb.ap[:, base + c: base + c + 1]
        B_DW, G_LN, B_LN, S_PL = 148, 152, 156, 160
        PLl = k.alloc("PLl", [128, 4, 80 * 64], BF16)
        PLc = k.alloc("PLc", [128, 4, 272], BF16)
        k.memset("pool", PLl, 0.0)
        k.memset("pool", PLc, 0.0)
        k.phase_begin()
        upls = [k.alloc(f"upl{i}", [128, 4, 8 * 94], BF16) for i in range(2)]
        upc = k.alloc("upc", [128, 4, 286], BF16)
        for u_ in upls:
            k.memset("dve", u_, 0.0)
        k.memset("dve", upc, 0.0)
        dg = k.alloc("dg", [128, 4, 31, 128], BF16)
        for c in range(4):
            k.tt("dve" if c % 2 == 0 else "pool", dg.ap[:, c, :, :], ident.unsqueeze(1).broadcast_to([128, 31, 128]),
                 ppb.ap[:, 24 + c * 31: 24 + (c + 1) * 31].unsqueeze(2).broadcast_to([128, 31, 128]), ALU.mult, [cstb, ppb], [dg])
        hts = [k.alloc(f"ht{i}", [128, 8, 512], BF16) for i in range(2)]
        sgm = [k.alloc(f"sgm{i}", [128, 512], F32) for i in range(2)]
        accs = [k.alloc(f"acc{i}", [128, 4, 512], F32) for i in range(2)]
        ybf = k.alloc("ybf", [128, 4, 512], BF16)
        ysq = k.alloc("ysq", [128, 4, 512], BF16)
        mean = k.alloc("mean", [128, 512], F32)
        m2 = k.alloc("m2", [128, 512], F32)
        var = k.alloc("var", [128, 512], F32)
        rs = k.alloc("rs", [128, 512], F32)
        tn = [k.alloc(f"tn{i}", [128, 512], F32) for i in range(2)]
        cvs = [k.alloc(f"cvs{i}", [128, 4, 512], BF16) for i in range(2)]

        def ln_gen(acc, cv_, t0, n):
            for c in range(4):
                k.copy("act", ybf.ap[:, c, 0:n], acc.ap[:, c, 0:n], [acc], [ybf])
                k.act(ysq.ap[:, c, 0:n], acc.ap[:, c, 0:n], AF.Square, [acc], [ysq])
            yield
            pm = k.bank()
            pq = k.bank()
            for c in range(4):
                k.mm(pm, pm.ap[:, 0:n], onesm, ybf.ap[:, c, 0:n], c == 0, c == 3, [cstb, ybf])
            for c in range(4):
                k.mm(pq, pq.ap[:, 0:n], onesm, ysq.ap[:, c, 0:n], c == 0, c == 3, [cstb, ysq])
            k.copy("act", mean.ap[:, 0:n], pm.ap[:, 0:n], [pm], [mean])
            k.act(m2.ap[:, 0:n], pm.ap[:, 0:n], AF.Square, [pm], [m2])
            k.tt("dve", var.ap[:, 0:n], pq.ap[:, 0:n], m2.ap[:, 0:n], ALU.subtract, [pq, m2], [var])
            k.tsmax("dve", var.ap[:, 0:n], var.ap[:, 0:n], 0.0, [var], [var])
            yield
            k.act(m2.ap[:, 0:n], var.ap[:, 0:n], AF.Sqrt, [var], [m2], bias=EPS)
            k.recip("dve", rs.ap[:, 0:n], m2.ap[:, 0:n], [m2], [rs])
            yield
            for c in range(4):
                t_ = tn[c % 2]
                k.tt("dve", t_.ap[:, 0:n], acc.ap[:, c, 0:n], mean.ap[:, 0:n], ALU.subtract, [acc, mean], [t_])
                k.tt("pool", t_.ap[:, 0:n], t_.ap[:, 0:n], rs.ap[:, 0:n], ALU.mult, [t_, rs], [t_])
                k.act(cv_.ap[:, c, 0:n], t_.ap[:, 0:n], AF.Silu, [t_, ppb], [cv_], scale=pcol(G_LN, c), bias=pcol(B_LN, c))
            k.store(cvT[:, :, t0:t0 + n], cv_, src=cv_.ap[:, :, 0:n])

        pending = None
        for ti, (t0, n, kind) in enumerate(TILES512):
            ht = hts[ti % 2]
            acc = accs[ti % 2]
            k.load(ht, hT[:, :, t0:t0 + n], dst=ht.ap[:, :, 0:n])
            lat = kind == "lat"
            up = upls[ti % 2] if lat else upc
            r0 = (t0 - NCTX) // 64
            for c in range(4):
                pa = k.bank()
                pb = k.bank()
                pl = k.bank()
                for (pbk, wsrc) in ((pa, wa_), (pb, wb_), (pl, wp_)):
                    for kc in range(8):
                        k.mm(pbk, pbk.ap[:, 0:n], wsrc.ap[:, kc, c * 128:(c + 1) * 128], ht.ap[:, kc, 0:n],
                             kc == 0, kc == 7, [wsrc, ht])
                sg_ = sgm[c % 2]
                k.act(sg_.ap[:, 0:n], pb.ap[:, 0:n], AF.Sigmoid, [pb], [sg_])
                if lat:
                    uint = up.ap[:, c, :].rearrange("p (r w) -> p r w", w=94)[:, :, 15:79]
                    k.tt("dve", uint, pa.ap.rearrange("p (r w) -> p r w", w=64), sg_.ap.rearrange("p (r w) -> p r w", w=64),
                         ALU.mult, [pa, sg_], [up])
                    k.copy("act", PLl.ap[:, c, (8 + r0) * 64:(8 + r0) * 64 + 512], pl.ap, [pl], [PLl])
                else:
                    k.tt("dve", up.ap[:, c, 15:15 + 256], pa.ap[:, 0:256], sg_.ap[:, 0:256], ALU.mult, [pa, sg_], [up])
                    k.copy("act", PLc.ap[:, c, 8:8 + 256], pl.ap[:, 0:256], [pl], [PLc])
                if pending is not None:
                    next(pending, None)
            if pending is not None:
                for _ in pending:
                    pass
                pending = None
            for c in range(4):
                pc = k.bank()
                for tap in range(31):
                    if lat:
                        src = up.ap[:, c, :].rearrange("p (r w) -> p r w", w=94)[:, :, tap:tap + 64]
                        dst = pc.ap.rearrange("p (r w) -> p r w", w=64)
                    else:
                        src = up.ap[:, c, tap:tap + 256]
                        dst = pc.ap[:, 0:256]
                    k.mm(pc, dst, dg.ap[:, c, tap, :], src, tap == 0, tap == 30, [dg, up])
                k.act(acc.ap[:, c, 0:n], pc.ap[:, 0:n], AF.Identity, [pc, ppb], [acc], bias=pcol(B_DW, c))
            pending = ln_gen(acc, cvs[ti % 2], t0, n)
        for _ in pending:
            pass

        k.phase_end()
        tA = k.alloc("tA", [128, 80 * 64], F32)
        tB = k.alloc("tB", [128, 80 * 64], F32)
        dTb = [k.alloc(f"dTb{i}", [128, 4096], BF16) for i in range(2)]
        pls = [k.alloc(f"pls{i}", [128, 4096], BF16) for i in range(2)]
        for (PL, R, Wd, rc0, tok0) in ((PLc, 256, 1, 256, 0), (PLl, 64, 64, 0, NCTX)):
            ntok = R * Wd
            for g in range(4):
                u = PL.ap[:, g, :]
                sl = lambda a, lo, hi: a[:, lo * Wd: hi * Wd]
                k.tt("dve", sl(tA.ap, 1, R + 16), sl(u, 0, R + 15), sl(u, 1, R + 16), ALU.add, [PL], [tA])
                cur = tA
                if g >= 1:
                    k.tt("dve", sl(tB.ap, 2, R + 15), sl(tA.ap, 1, R + 14), sl(tA.ap, 3, R + 16), ALU.add, [tA], [tB])
                    cur = tB
                if g >= 2:
                    k.tt("dve", sl(tA.ap, 4, R + 13), sl(tB.ap, 2, R + 11), sl(tB.ap, 6, R + 15), ALU.add, [tB], [tA])
                    cur = tA
                if g >= 3:
                    k.tt("dve", sl(tB.ap, 8, R + 9), sl(tA.ap, 4, R + 5), sl(tA.ap, 12, R + 13), ALU.add, [tA], [tB])
                    cur = tB
                oth = tB if cur is tA else tA
                S_ = sl(cur.ap, 8, 8 + R)
                rc = rcf.ap[:, rc0 + g * R: rc0 + (g + 1) * R]
                if Wd > 1:
                    S3 = S_.rearrange("p (r w) -> p r w", w=Wd)
                    O3 = sl(oth.ap, 8, 8 + R).rearrange("p (r w) -> p r w", w=Wd)
                    k.tt("dve", O3, S3, rc.unsqueeze(2).broadcast_to([128, R, Wd]), ALU.mult, [cur, rcf], [oth])
                else:
                    k.tt("dve", sl(oth.ap, 8, 8 + R), S_, rc, ALU.mult, [cur, rcf], [oth])
                db = dTb[g % 2]
                k.tt("pool", db.ap[:, 0:ntok], sl(oth.ap, 8, 8 + R), sl(u, 8, 8 + R), ALU.subtract, [oth, PL], [db])
                po = pls[g % 2]
                for c0 in range(0, ntok, 512):
                    nn = min(512, ntok - c0)
                    pb = k.bank()
                    k.mm(pb, pb.ap[:, 0:nn], wpg.ap[:, g, :], db.ap[:, c0:c0 + nn], True, True, [wpg, db])
                    k.act(po.ap[:, c0:c0 + nn], pb.ap[:, 0:nn], AF.Copy, [pb, ppb], [po], scale=pcol(S_PL, g))
                k.store(plT[:, g, tok0:tok0 + ntok], po, src=po.ap[:, 0:ntok])
        k.phase_end()

    def phase_gla(l, d):
        fwd = d == "f"
        k.phase_begin()
        di = 0 if fwd else 1
        wdec = k.alloc("wdec", [16, 512], BF16)
        bdec = k.alloc("bdec", [1, 512], BF16)
        k.load(wdec, w_decay[l, di], queue="pool")
        k.load(bdec, rows[l, :, 11264 + di * 512: 11264 + (di + 1) * 512], queue="pool")
        Sbs = [k.alloc(f"Sb{i}", [128, 4, 256], BF16) for i in range(2)]
        k.memset("pool", Sbs[0], 0.0)
        k.memset("pool", Sbs[1], 0.0)
        dgam = [k.alloc(f"dgam{i}", [128, 4, 128], BF16) for i in range(3)]
        NS = 2
        qTt = [k.alloc(f"qTt{i}", [128, 4, 512], BF16) for i in range(NS)]
        kTt = [k.alloc(f"kTt{i}", [128, 4, 512], BF16) for i in range(NS)]
        kkt = [k.alloc(f"kkt{i}", [128, 4, 512], BF16) for i in range(NS)]
        vvt = [k.alloc(f"vvt{i}", [128, 4, D], BF16) for i in range(NS)]
        lrt = [k.alloc(f"lrt{i}", [16, 512], BF16) for i in range(NS)]
        oft = [k.alloc(f"oft{i}", [128, 4, D], F32) for i in range(NS)]
        if not fwd:
            sgt = [k.alloc(f"sgt{i}", [128, 4, D], BF16) for i in range(NS)]
            ogs = [k.alloc(f"ogs{i}", [128, 8, 512], BF16) for i in range(NS)]
            otot = k.alloc("otot", [128, D], F32)
            sqj = k.alloc("sqj", [128, 256], BF16)
            ssq = k.alloc("ssq", [128, 4], F32)
            rt4 = k.alloc("rt4", [128, 4], F32)
            rs4 = k.alloc("rs4", [128, 4], F32)
            ogb = [k.alloc(f"og{i}", [128, D], BF16) for i in range(2)]
        et = [k.alloc(f"et{i}", [128, 512], F32) for i in range(2)]
        spb = [k.alloc(f"spb{i}", [128, 512], BF16) for i in range(2)]
        eq = [k.alloc(f"eq{i}", [128, 512], BF16) for i in range(2)]
        ek = [k.alloc(f"ek{i}", [128, 512], BF16) for i in range(2)]
        eke = [k.alloc(f"eke{i}", [128, 512], BF16) for i in range(2)]
        qi = [k.alloc(f"qi{i}", [128, 4, 128], BF16) for i in range(3)]
        ki = [k.alloc(f"ki{i}", [128, 4, 128], BF16) for i in range(3)]
        ke = [k.alloc(f"ke{i}", [128, 512], BF16) for i in range(3)]
        gam = [k.alloc(f"gam{i}", [128, 4], F32) for i in range(3)]
        am = k.alloc("am", [128, 4, 128], BF16)
        zt, bT, bk, at = ps[0], ps[1], ps[2], ps[3]

        tiles = list(TILES512)
        if not fwd:
            tiles = [tiles[0]] + tiles[:0:-1]
        seq = []
        for si, (t0, n, kind) in enumerate(tiles):
            cs_ = list(range(n // 128))
            if not fwd:
                cs_ = cs_[::-1]
            for ci, c in enumerate(cs_):
                seq.append((si, t0, n, c, ci == 0, ci == len(cs_) - 1))
        NCH = len(seq)

        def load_tile(si, t0, n):
            s_ = si % NS
            nsub = n // 128
            k.load(lrt[s_], lrT[di, :, t0:t0 + n], dst=lrt[s_].ap[:, 0:n])
            k.load(qTt[s_], qT[:, :, t0:t0 + n], dst=qTt[s_].ap[:, :, 0:n])
            k.load(kTt[s_], kT[:, :, t0:t0 + n], dst=kTt[s_].ap[:, :, 0:n])
            k.load(kkt[s_], tokmaj(kk, t0, n), dst=kkt[s_].ap[:, 0:nsub, :])
            k.load(vvt[s_], tokmaj(vv, t0, n), dst=vvt[s_].ap[:, 0:nsub, :])
            if not fwd:
                k.load(oft[s_], tokmaj(of_d, t0, n), dst=oft[s_].ap[:, 0:nsub, :])
                k.load(sgt[s_], tokmaj(sg, t0, n), dst=sgt[s_].ap[:, 0:nsub, :])

        def A_pe(i):
            si, t0, n, c, first, last = seq[i]
            s_ = si % NS
            if first:
                load_tile(si, t0, n)
            cs = slice(c * 128, (c + 1) * 128)
            k.mm(zt, zt.ap, lrt[s_].ap[:, cs], wdec.ap, True, False, [lrt[s_], wdec])
            k.mm(zt, zt.ap, onesr.ap, bdec.ap, False, True, [onesr, bdec])

        def A_act(i):
            p_ = i % 2
            k.act(et[p_].ap, zt.ap, AF.Exp, [zt], [et[p_]], scale=-1.0)
            k.act(spb[p_].ap, et[p_].ap, AF.Ln, [et[p_]], [spb[p_]], bias=1.0)

        def B_pe(i):
            p_ = i % 2
            for j in range(4):
                k.mm(bT, bT.ap[:, j * 128:(j + 1) * 128], spb[p_].ap[:, j * 128:(j + 1) * 128], tri[d], True, True, [spb[p_], cstb])
            k.mm(bk, bk.ap, UU[d], spb[p_].ap, True, True, [spb[p_], cstb])

        def B_act(i):
            p_ = i % 2
            k.act(eq[p_].ap, bT.ap, AF.Exp, [bT], [eq[p_]])
            k.act(ek[p_].ap, bT.ap, AF.Exp, [bT], [ek[p_]], scale=-1.0)
            col = 127 if fwd else 0
            k.act(gam[i % 3].ap, bT.ap.rearrange("p (j t) -> p j t", j=4)[:, :, col], AF.Exp, [bT], [gam[i % 3]])
            k.act(eke[p_].ap, bk.ap, AF.Exp, [bk], [eke[p_]])

        def B_vec(i):
            si, t0, n, c, first, last = seq[i]
            s_ = si % NS
            p_ = i % 2
            q_ = i % 3
            cs = slice(c * 128, (c + 1) * 128)
            k.tt("pool", qi[q_].ap, qTt[s_].ap[:, :, cs], eq[p_].ap.rearrange("p (j t) -> p j t", j=4), ALU.mult,
                 [qTt[s_], eq[p_]], [qi[q_]])
            k.tt("pool", ki[q_].ap, kTt[s_].ap[:, :, cs], ek[p_].ap.rearrange("p (j t) -> p j t", j=4), ALU.mult,
                 [kTt[s_], ek[p_]], [ki[q_]])
            k.tt("dve", ke[q_].ap, kkt[s_].ap[:, c, :], eke[p_].ap, ALU.mult, [kkt[s_], eke[p_]], [ke[q_]])
            k.tt("dve", dgam[q_].ap, ident.unsqueeze(1).broadcast_to([128, 4, 128]),
                 gam[q_].ap.unsqueeze(2).broadcast_to([128, 4, 128]), ALU.mult, [cstb, gam[q_]], [dgam[q_]])

        def C_att(i):
            q_ = i % 3
            for j in range(4):
                k.mm(at, at.ap[:, j * 128:(j + 1) * 128], ki[q_].ap[:, j, :], qi[q_].ap[:, j, :], True, True, [ki[q_], qi[q_]])

        def C_mask(i):
            k.tt("dve", am.ap, at.ap.rearrange("p (j t) -> p j t", j=4), msk[d].unsqueeze(1).broadcast_to([128, 4, 128]),
                 ALU.mult, [at, cstb], [am])

        def C_pe2(i):
            si, t0, n, c, first, last = seq[i]
            s_ = si % NS
            p_ = i % 3
            Sb = Sbs[i % 2]
            for j in range(4):
                pd = ps[6 + j // 2]
                dd = pd.ap[:, (j % 2) * 256:(j % 2 + 1) * 256]
                k.mm(pd, dd, dgam[p_].ap[:, j, :], Sb.ap[:, j, :], True, False, [dgam[p_], Sb])
                k.mm(pd, dd, ke[p_].ap[:, j * 128:(j + 1) * 128], vvt[s_].ap[:, c, j * 256:(j + 1) * 256], False, True,
                     [ke[p_], vvt[s_]])
            for j in range(4):
                po = ps[4 + j // 2]
                oo = po.ap[:, (j % 2) * 256:(j % 2 + 1) * 256]
                k.mm(po, oo, am.ap[:, j, :], vvt[s_].ap[:, c, j * 256:(j + 1) * 256], True, False, [am, vvt[s_]])
                k.mm(po, oo, qi[p_].ap[:, j, :], Sb.ap[:, j, :], False, True, [qi[p_], Sb])

        def S_copy(i):
            Sn = Sbs[(i + 1) % 2]
            k.copy("dve", Sn.ap.rearrange("p j v -> p (j v)").rearrange("p (a b) -> p a b", a=2), k.psum_t[:, 6:8, :],
                   [ps[6], ps[7]], [Sn])

        o2 = k.psum_t[:, 4:6, :]

        def D_evac(i):
            si, t0, n, c, first, last = seq[i]
            s_ = si % NS
            nsub = n // 128
            if fwd:
                k.copy("act", oft[s_].ap[:, c, :].rearrange("p (a b) -> p a b", a=2), o2, [ps[4], ps[5]], [oft[s_]])
                if last:
                    k.store(tokmaj(of_d, t0, n), oft[s_], src=oft[s_].ap[:, 0:nsub, :])
            else:
                k.tt("dve", otot.ap.rearrange("p (a b) -> p a b", a=2), o2, oft[s_].ap[:, c, :].rearrange("p (a b) -> p a b", a=2),
                     ALU.add, [ps[4], ps[5], oft[s_]], [otot])

        def D_epi_act(i):
            k.memset("pool", ssq, 0.0)
            for j in range(4):
                k.act(sqj.ap, otot.ap[:, j * 256:(j + 1) * 256], AF.Square, [otot], [sqj, ssq], accum_out=ssq.ap[:, j:j + 1])
            k.act(rt4.ap, ssq.ap, AF.Ln, [ssq], [rt4], scale=1.0 / 256, bias=EPS)
            k.act(rs4.ap, rt4.ap, AF.Exp, [rt4], [rs4], scale=-0.5)

        def D_epi_vec(i):
            si, t0, n, c, first, last = seq[i]
            s_ = si % NS
            for j in range(4):
                js = slice(j * 256, (j + 1) * 256)
                k.stt("dve", ogb[i % 2].ap[:, js], otot.ap[:, js], rs4.ap[:, j:j + 1], sgt[s_].ap[:, c, js],
                      ALU.mult, ALU.mult, [otot, rs4, sgt[s_]], [ogb[i % 2]])

        def E_all(i):
            si, t0, n, c, first, last = seq[i]
            s_ = si % NS
            pbb = at.ap.bitcast(BF16)
            og = ogb[i % 2]
            for j in range(8):
                k.transpose(at, pbb[:, j * 128:(j + 1) * 128], og.ap[:, j * 128:(j + 1) * 128], ident, [og, cstb])
            k.copy("act", ogs[s_].ap[:, :, c * 128:(c + 1) * 128], pbb[:, 0:1024].rearrange("p (j t) -> p j t", j=8),
                   [at], [ogs[s_]])
            if last:
                k.store(ogT[:, :, t0:t0 + n], ogs[s_], src=ogs[s_].ap[:, :, 0:n])

        ok = lambda i: 0 <= i < NCH
        if not fwd:
            k.memset("pool", ssq, 0.0)
        for s_ in range(NCH + 6):
            iA, iB, iC, iD, iE = s_, s_ - 1, s_ - 3, s_ - 4, s_ - 5
            if ok(iD):
                D_evac(iD)
            if ok(iC):
                C_att(iC)
                C_mask(iC)
            if ok(iB):
                B_pe(iB)
            if ok(iA):
                A_pe(iA)
            if ok(iC):
                C_pe2(iC)
                S_copy(iC)
            if ok(iB):
                B_act(iB)
            if ok(iA):
                A_act(iA)
            if ok(iB):
                B_vec(iB)
            if (not fwd) and ok(iD):
                D_epi_act(iD)
                D_epi_vec(iD)
            if (not fwd) and ok(iE):
                E_all(iE)
        k.phase_end()

    def phase_merge(l):
        k.phase_begin()
        TS = 256
        wgo = k.alloc("wgo", [128, 8, D], BF16)
        wco = k.alloc("wco", [128, 4, D], BF16)
        wpo = k.alloc("wpo", [128, 4, D], BF16)
        wgt = k.alloc("wgt", [128, 8, 3 * D], BF16)
        wo = k.alloc("wo", [128, 8, D], BF16)
        ppb = k.alloc("ppb", [128, 24], F32)
        k.load(wgt, wview(w_in[l], C_GT, C_GT + 3 * D), queue="pool")
        k.load(wgo, wview(w_gla_o[l], 0, D), queue="pool")
        k.load(wco, wview(w_conv_o[l], 0, D), queue="pool")
        k.load(wpo, wview(w_pool_o[l], 0, D), queue="pool")
        k.load(wo, wview(w_out[l], 0, D), queue="pool")
        k.load(ppb, pp_d[l, :, 0:24])
        G1 = k.alloc("G1", [128, D], F32)
        A2 = k.alloc("A2", [128, D], F32)
        B2 = k.alloc("B2", [128, D], F32)
        hts = [k.alloc(f"ht{i}", [128, 8, TS], BF16) for i in range(2)]
        ogs_ = [k.alloc(f"og_{i}", [128, 8, TS], BF16) for i in range(2)]
        cvs_ = [k.alloc(f"cv_{i}", [128, 4, TS], BF16) for i in range(2)]
        pls_ = [k.alloc(f"pl_{i}", [128, 4, TS], BF16) for i in range(2)]
        xss = [k.alloc(f"xs{i}", [128, 2, D], F32) for i in range(2)]
        h2ss = [k.alloc(f"h2s{i}", [128, 8, TS], BF16) for i in range(2)]
        gts = [k.alloc(f"gts{i}", [128, 3, TS], BF16) for i in range(2)]
        ysb = [k.alloc(f"ysb{i}", [128, 3, TS], BF16) for i in range(2)]
        mixs = [k.alloc(f"mix{i}", [128, 8, TS], BF16) for i in range(2)]
        tmp = norm_tmp(2)
        ss1 = k.alloc("ss1", [128, 8], F32)
        rt1 = k.alloc("rt1", [128, 4], F32)
        rs1 = k.alloc("rs1", [128, 4], F32)
        junk, t1 = tmp[3], tmp[4]
        xsrc = xin if l == 0 else xres
        state = {"kind": None}
        k.bank_set = [0, 1, 2, 3]

        def epilogue(t0, n, kind, xs, h2s):
            if kind != state["kind"]:
                w = 0 if kind == "lat" else 1
                k.load(G1, modrow[l, w, 2, :].partition_broadcast(128))
                k.load(A2, modrow[l, w, 3, :].partition_broadcast(128))
                k.load(B2, modrow[l, w, 4, :].partition_broadcast(128))
                state["kind"] = kind
            return epi_gen(xs, n // 128, G1, A2, B2, h2s, tmp, ss1, rt1, rs1,
                           lambda: k.store(tokmaj(xres, t0, n), xs),
                           lambda: k.store(h2T[:, :, t0:t0 + n], h2s), True)

        def drain(g):
            if g is not None:
                for _ in g:
                    pass

        pending = None
        for ti, (t0, n, kind) in enumerate(TILES256):
            nsub = n // 128
            ht, og_, cv_, pl_, xs, h2s, mix = (hts[ti % 2], ogs_[ti % 2], cvs_[ti % 2], pls_[ti % 2], xss[ti % 2],
                                               h2ss[ti % 2], mixs[ti % 2])
            k.load(ht, hT[:, :, t0:t0 + n])
            k.load(og_, ogT[:, :, t0:t0 + n])
            k.load(cv_, cvT[:, :, t0:t0 + n])
            k.load(pl_, plT[:, :, t0:t0 + n])
            k.load(xs, tokmaj(xsrc, t0, n))
            for j in range(8):
                gt = gts[j % 2]
                yb = ysb[j % 2]
                for br in range(3):
                    pb = k.bank()
                    for kc in range(8):
                        k.mm(pb, pb.ap[:, 0:n], wgt.ap[:, kc, br * D + j * 128: br * D + (j + 1) * 128], ht.ap[:, kc, :],
                             kc == 0, kc == 7, [wgt, ht])
                    k.act(gt.ap[:, br, :], pb.ap[:, 0:n], AF.Sigmoid, [pb, ppb], [gt], bias=ppb.ap[:, br * 8 + j: br * 8 + j + 1])
                for br, (wsrc, asrc, nk) in enumerate(((wgo, og_, 8), (wco, cv_, 4), (wpo, pl_, 4))):
                    pb = k.bank()
                    for kc in range(nk):
                        k.mm(pb, pb.ap[:, 0:n], wsrc.ap[:, kc, j * 128:(j + 1) * 128], asrc.ap[:, kc, :],
                             kc == 0, kc == nk - 1, [wsrc, asrc])
                    k.copy("dve", yb.ap[:, br, :], pb.ap[:, 0:n], [pb], [yb])
                k.tt("dve", yb.ap, yb.ap, gt.ap, ALU.mult, [yb, gt], [yb])
                k.tt("dve", yb.ap[:, 0, :], yb.ap[:, 0, :], yb.ap[:, 1, :], ALU.add, [yb], [yb])
                k.tt("dve", mix.ap[:, j, :], yb.ap[:, 0, :], yb.ap[:, 2, :], ALU.add, [yb], [mix])
                if pending is not None:
                    next(pending, None)
                    if j in (3, 6):
                        next(pending, None)
            drain(pending)
            for s in range(nsub):
                for hh in range(2):
                    pb = ps[4 + 2 * s + hh]
                    for kc in range(8):
                        k.mm(pb, pb.ap, mix.ap[:, kc, s * 128:(s + 1) * 128], wo.ap[:, kc, hh * 512:(hh + 1) * 512],
                             kc == 0, kc == 7, [mix, wo])
            pending = epilogue(t0, n, kind, xs, h2s)
        drain(pending)
        k.bank_set = list(range(8))
        k.phase_end()

    def phase_mlp(l):
        k.phase_begin()
        last = l == L - 1
        w1q = [k.alloc(f"w1q{i}", [128, 8, D], BF16) for i in range(4)]
        w2 = k.alloc("w2", [128, 32, D], BF16)
        for q4 in range(4):
            k.load(w1q[q4], wview(w_mlp1[l], q4 * D, (q4 + 1) * D), queue="pool")
        for q4 in range(4):
            k.dma("pool", w2.ap[:, q4 * 8:(q4 + 1) * 8, :], w_mlp2[l][q4 * D:(q4 + 1) * D, :].rearrange("(k p) c -> p k c", p=128),
                  [], [w2], w2)
        G2 = k.alloc("G2", [128, D], F32)
        if not last:
            A1 = k.alloc("A1", [128, D], F32)
            B1 = k.alloc("B1", [128, D], BF16)
        h2 = [k.alloc(f"h2_{i}", [128, 8, 256], BF16) for i in range(2)]
        hidA = k.alloc("hidA", [128, 24, 256], BF16)
        hidB = k.alloc("hidB", [128, 8, 256], BF16)
        hidf = lambda f: (hidA, hidA.ap[:, f, :]) if f < 24 else (hidB, hidB.ap[:, f - 24, :])
        xs = k.alloc("xs", [128, 2, D], F32)
        hs = None if last else hidB
        trl = [k.alloc(f"trl{i}", [128, 256], BF16) for i in range(2)]
        hb_t = k.alloc("hb", [128, D], BF16)
        tmp = (k.alloc("ss", [128, 8], F32), k.alloc("rt", [128, 8], F32), k.alloc("rstd", [128, 8], F32),
               hb_t, k.alloc("t1", [128, D], F32), hb_t)
        ss1 = k.alloc("ss1", [128, 8], F32)
        rt1 = k.alloc("rt1", [128, 4], F32)
        rs1 = k.alloc("rs1", [128, 4], F32)
        junk, t1 = tmp[3], tmp[4]
        state = {"kind": None}
        k.bank_set = [0, 1, 2, 3]

        def epilogue(t0, n, kind):
            if kind != state["kind"]:
                w = 0 if kind == "lat" else 1
                k.load(G2, modrow[l, w, 5, :].partition_broadcast(128))
                if not last:
                    k.load(A1, modrow[l + 1, w, 0, :].partition_broadcast(128))
                    k.load(B1, modrow[l + 1, w, 1, :].partition_broadcast(128), queue="pool")
                state["kind"] = kind
            if last:
                return epi_gen(xs, n // 128, G2, None, None, None, tmp, ss1, rt1, rs1,
                               lambda: k.store(tokmaj(out, t0 - NCTX, n), xs), None, False)
            return epi_gen(xs, n // 128, G2, A1, B1, hs, tmp, ss1, rt1, rs1,
                           lambda: k.store(tokmaj(xres, t0, n), xs),
                           lambda: k.store(hT[:, :, t0:t0 + n], hs), True)

        def drain(g):
            if g is not None:
                for _ in g:
                    pass

        pending = None
        for ti, (t0, n, kind) in enumerate(TILES256):
            if last and kind == "ctx":
                continue
            nsub = n // 128
            h_ = h2[ti % 2]
            k.load(h_, h2T[:, :, t0:t0 + n])
            if pending is None:
                k.load(xs, tokmaj(xres, t0, n))
            for f in range(32):
                pb = k.bank()
                for kc in range(8):
                    wq = w1q[f // 8]
                    k.mm(pb, pb.ap[:, 0:n], wq.ap[:, kc, (f % 8) * 128:(f % 8 + 1) * 128], h_.ap[:, kc, :], kc == 0, kc == 7, [wq, h_])
                tr_ = trl[f % 2]
                k.act(tr_.ap[:, 0:n], pb.ap[:, 0:n], AF.Relu, [pb], [tr_])
                hb_, ha_ = hidf(f)
                k.tt("dve" if f % 2 == 0 else "pool", ha_, tr_.ap[:, 0:n], tr_.ap[:, 0:n], ALU.mult, [tr_], [hb_])
                if pending is not None and f % 2 == 1 and f <= 21:
                    if next(pending, "done") == "done":
                        pending = None
                        k.load(xs, tokmaj(xres, t0, n))
            if pending is not None:
                drain(pending)
                pending = None
                k.load(xs, tokmaj(xres, t0, n))
            for s in range(nsub):
                for hh in range(2):
                    pb = ps[4 + 2 * s + hh]
                    for f in range(32):
                        hb_, ha_ = hidf(f)
                        k.mm(pb, pb.ap, ha_[:, s * 128:(s + 1) * 128], w2.ap[:, f, hh * 512:(hh + 1) * 512],
                             f == 0, f == 31, [hb_, w2])
            pending = epilogue(t0, n, kind)
        drain(pending)
        k.bank_set = list(range(8))
        k.phase_end()

    stages = []
    stages.append(("adaln", phase_adaln))
    stages.append(("prenorm0", phase_prenorm0))
    for l in range(L):
        stages.append((f"inproj{l}", lambda l=l: phase_inproj_gla(l)))
        stages.append((f"convpool{l}", lambda l=l: phase_conv_pool(l)))
        stages.append((f"glaf{l}", lambda l=l: phase_gla(l, "f")))
        stages.append((f"glab{l}", lambda l=l: phase_gla(l, "b")))
        stages.append((f"merge{l}", lambda l=l: phase_merge(l)))
        stages.append((f"mlp{l}", lambda l=l: phase_mlp(l)))
    for name, fn in stages:
        fn()
        if stop_after == name:
            break
    k.barrier()
    k.emit()
    nc._k_stats = (k.n_instr, k.nsem)
    return nc


def make_consts():
    cst = np.zeros((128, NCST), np.float32)
    s = np.arange(128)[:, None]
    t = np.arange(128)[None, :]
    cst[:, 0:128] = (s == t)
    cst[:, 128:256] = np.where(s <= t, -1.0 / 16, 0.0)
    cst[:, 256:384] = np.where(s >= t, -1.0 / 16, 0.0)
    cst[:, 384:512] = np.where(s > t, -1.0 / 16, 0.0)
    cst[:, 512:640] = np.where(s < t, -1.0 / 16, 0.0)
    cst[:, 640:768] = (s <= t)
    cst[:, 768:896] = (s >= t)
    cst[:, 896:1024] = 1.0 / 512
    for (Ln, off) in ((64, 1024), (256, 1280)):
        for g, w in enumerate((2, 4, 8, 16)):
            left = w // 2
            right = w - 1 - left
            tt_ = np.arange(Ln)
            lo = np.clip(tt_ - left, 0, Ln)
            hi = np.clip(tt_ + right + 1, 0, Ln)
            cst[:, off + g * Ln: off + (g + 1) * Ln] = (1.0 / (hi - lo).astype(np.float32))[None, :]
    return cst


def pack_inputs(inp):
    f = lambda a: np.ascontiguousarray(np.asarray(a, dtype=np.float32))
    rows = np.zeros((L, 1, NR), np.float32)
    pp = np.zeros((L, 128, NPP), np.float32)
    for l in range(L):
        rows[l, 0, 0:6144] = inp["b_ada"][l]
        rows[l, 0, 6144:7168] = inp["g_pre_mix"][l]
        rows[l, 0, 7168:8192] = inp["g_post_mix"][l]
        rows[l, 0, 8192:9216] = inp["g_pre_mlp"][l]
        rows[l, 0, 9216:10240] = inp["g_post_mlp"][l]
        rows[l, 0, 10240:11264] = inp["g_gla"][l]
        rows[l, 0, 11264:11776] = inp["b_decay"][l, 0]
        rows[l, 0, 11776:12288] = inp["b_decay"][l, 1]
        pp[l, :, 0:24] = np.asarray(inp["b_gate"][l]).reshape(24, 128).T
        wd = np.asarray(inp["w_dw"][l])
        pp[l, :, 24:148] = wd.T.reshape(4, 128, 31).transpose(1, 0, 2).reshape(128, 124)
        pp[l, :, 148:152] = np.asarray(inp["b_dw"][l]).reshape(4, 128).T
        pp[l, :, 152:156] = np.asarray(inp["g_conv_ln"][l]).reshape(4, 128).T
        pp[l, :, 156:160] = np.asarray(inp["b_conv_ln"][l]).reshape(4, 128).T
        pp[l, :, 160:164] = np.asarray(inp["s_pool"][l]).reshape(4, 128).T
    shared = {
        "rows": rows, "pp": pp, "cst": make_consts(),
        "w_ada": f(inp["w_ada"]), "w_in": f(inp["w_in"]), "w_decay": f(inp["w_decay"]),
        "w_gla_o": f(inp["w_gla_o"]), "w_conv_o": f(inp["w_conv_o"]), "w_pool_g": f(inp["w_pool_g"]),
        "w_pool_o": f(inp["w_pool_o"]), "w_out": f(inp["w_out"]), "w_mlp1": f(inp["w_mlp1"]), "w_mlp2": f(inp["w_mlp2"]),
    }
    maps = []
    B = inp["x"].shape[0]
    for b in range(B):
        m = dict(shared)
        m["xin"] = np.ascontiguousarray(np.concatenate([inp["ctx"][b], inp["x"][b]], axis=0).astype(np.float32))
        cv = np.zeros((128, 16), np.float32)
        cv[:, 0:8] = np.asarray(inp["c"][b]).reshape(8, 128).T
        cv[:, 8:16] = np.asarray(inp["c_ctx"]).reshape(8, 128).T
        m["cvec"] = cv
        maps.append(m)
    return maps


_NC_CACHE = {}


def kernel(**inputs):
    inp = {k_: np.asarray(v) for k_, v in inputs.items()}
    maps = pack_inputs(inp)
    if "nc" not in _NC_CACHE:
        _NC_CACHE["nc"] = build_program()
    nc = _NC_CACHE["nc"]
    res = run_bass_kernel_spmd(nc, maps, core_ids=list(range(8)))
    return np.stack([np.asarray(r["out"], dtype=np.float32) for r in res.results], axis=0)
```
